# Optimizing a Trainium2 kernel written in Bass

```python
import math
import jax, jax.numpy as jnp
from jax import lax
import numpy as np

D_MODEL = 1024
BATCH = 8
SEQ = 2048
DEPTH = 2

N_A_LAYERS = DEPTH // 2
N_B_LAYERS = DEPTH - N_A_LAYERS

SSM_EXPAND = 2
D_INNER = SSM_EXPAND * D_MODEL
SSM_HEAD_DIM = 64
SSM_HEADS = D_INNER // SSM_HEAD_DIM
SSM_GROUPS = 4
SSM_STATE = 128
CONV_WIDTH = 4
CHUNK = 128
GN = SSM_GROUPS * SSM_STATE
CONV_DIM = D_INNER + 2 * GN
IN_PROJ_DIM = D_INNER + CONV_DIM + SSM_HEADS

ATT_HEAD_DIM = 64
N_Q_HEADS = D_MODEL // ATT_HEAD_DIM
N_KV_HEADS = 4
WINDOW = 128
ROPE_THETA = 10000.0

D_FF = 2816
FFN_RES_WEIGHT = 0.5
EPS = 1e-5

kernel_name = 'yoco_ssd_swa_sink_macaron'


def rmsnorm(x, w):
    xf = x.astype(jnp.float32)
    xf = xf * lax.rsqrt(jnp.mean(xf * xf, axis=-1, keepdims=True) + EPS)
    return (xf * w.astype(jnp.float32)).astype(x.dtype)


def swiglu(h, w_gate, w_up, w_down):
    return (jax.nn.silu(h @ w_gate) * (h @ w_up)) @ w_down


def rope_tables(seqlen):
    pos = jnp.arange(seqlen, dtype=jnp.float32)
    inv = 1.0 / (ROPE_THETA ** (jnp.arange(0, ATT_HEAD_DIM, 2, dtype=jnp.float32) / ATT_HEAD_DIM))
    ang = pos[:, None] * inv[None, :]
    return jnp.cos(ang), jnp.sin(ang)


def apply_rope(t, cos, sin):
    tf = t.astype(jnp.float32)
    t1, t2 = jnp.split(tf, 2, axis=-1)
    c = cos[:, None, :]
    s = sin[:, None, :]
    return jnp.concatenate([t1 * c - t2 * s, t2 * c + t1 * s], axis=-1).astype(t.dtype)


def causal_depthwise_conv(u, w, b):
    out = lax.conv_general_dilated(
        u, w[:, None, :].astype(u.dtype), window_strides=(1,),
        padding=[(CONV_WIDTH - 1, 0)],
        dimension_numbers=('NWC', 'WIO', 'NWC'),
        feature_group_count=u.shape[-1])
    return out + b


def segsum(a):
    cs = jnp.cumsum(a, axis=-1)
    diff = cs[..., :, None] - cs[..., None, :]
    t = a.shape[-1]
    mask = jnp.tril(jnp.ones((t, t), dtype=bool))
    return jnp.where(mask, diff, -jnp.inf)


def ssd_chunked(xdt, a, b_ssm, c_ssm):
    bsz, seqlen, _, _ = xdt.shape
    nc = seqlen // CHUNK
    r = SSM_HEADS // SSM_GROUPS
    x = xdt.reshape(bsz, nc, CHUNK, SSM_GROUPS, r, SSM_HEAD_DIM)
    a = a.reshape(bsz, nc, CHUNK, SSM_GROUPS, r).transpose(0, 3, 4, 1, 2)
    bc = b_ssm.reshape(bsz, nc, CHUNK, SSM_GROUPS, SSM_STATE)
    cc = c_ssm.reshape(bsz, nc, CHUNK, SSM_GROUPS, SSM_STATE)
    a_cs = jnp.cumsum(a, axis=-1)
    decay_in = jnp.exp(segsum(a))
    cb = jnp.einsum('bclgn,bcsgn->bcgls', cc, bc)
    y_diag = jnp.einsum('bcgls,bgrcls,bcsgrp->bclgrp', cb, decay_in, x)
    decay_states = jnp.exp(a_cs[..., -1:] - a_cs)
    states = jnp.einsum('bclgn,bgrcl,bclgrp->bcgrpn', bc, decay_states, x)
    chunk_decay = jnp.exp(a_cs[..., -1])
    states_c = jnp.moveaxis(states, 1, 0)
    decay_c = jnp.moveaxis(chunk_decay, 3, 0)

    def step(carry, inp):
        s, d = inp
        return carry * d[..., None, None] + s, carry

    _, prev = lax.scan(step, jnp.zeros_like(states_c[0]), (states_c, decay_c))
    prev = jnp.moveaxis(prev, 0, 1)
    decay_out = jnp.exp(a_cs)
    y_off = jnp.einsum('bclgn,bcgrpn,bgrcl->bclgrp', cc, prev, decay_out)
    return (y_diag + y_off).reshape(bsz, seqlen, SSM_HEADS, SSM_HEAD_DIM)


def mamba2_mixer(h, w_in, conv_w, conv_b, dt_bias, a_log, d_skip, norm_w, w_out):
    bsz, seqlen, _ = h.shape
    zxbcdt = h @ w_in
    z = zxbcdt[..., :D_INNER]
    xbc = zxbcdt[..., D_INNER:D_INNER + CONV_DIM]
    dt_raw = zxbcdt[..., D_INNER + CONV_DIM:]
    xbc = jax.nn.silu(causal_depthwise_conv(xbc, conv_w, conv_b))
    xs = xbc[..., :D_INNER].reshape(bsz, seqlen, SSM_HEADS, SSM_HEAD_DIM).astype(jnp.float32)
    b_ssm = xbc[..., D_INNER:D_INNER + GN].reshape(bsz, seqlen, SSM_GROUPS, SSM_STATE).astype(jnp.float32)
    c_ssm = xbc[..., D_INNER + GN:].reshape(bsz, seqlen, SSM_GROUPS, SSM_STATE).astype(jnp.float32)
    dt = jax.nn.softplus(dt_raw.astype(jnp.float32) + dt_bias.astype(jnp.float32))
    a = -jnp.exp(a_log.astype(jnp.float32))
    y = ssd_chunked(xs * dt[..., None], dt * a, b_ssm, c_ssm)
    y = y + xs * d_skip.astype(jnp.float32)[:, None]
    y = y.reshape(bsz, seqlen, D_INNER) * jax.nn.silu(z.astype(jnp.float32))
    yg = y.reshape(bsz, seqlen, SSM_GROUPS, D_INNER // SSM_GROUPS)
    yg = yg * lax.rsqrt(jnp.mean(yg * yg, axis=-1, keepdims=True) + EPS)
    y = (yg.reshape(bsz, seqlen, D_INNER) * norm_w.astype(jnp.float32)).astype(h.dtype)
    return y @ w_out


def shared_kv(x, kv_norm_w, w_k, b_k, w_v, b_v, cos, sin):
    bsz, seqlen, _ = x.shape
    hkv = rmsnorm(x, kv_norm_w)
    k = (hkv @ w_k + b_k).reshape(bsz, seqlen, N_KV_HEADS, ATT_HEAD_DIM)
    v = (hkv @ w_v + b_v).reshape(bsz, seqlen, N_KV_HEADS, ATT_HEAD_DIM)
    return apply_rope(k, cos, sin), v


def band_blocks(t):
    prev = jnp.pad(t[:, :-1], ((0, 0), (1, 0), (0, 0), (0, 0), (0, 0)))
    return jnp.concatenate([prev, t], axis=2)


def swa_sink_attention(h, k_rot, v, w_q, b_q, sinks, w_o, b_o, cos, sin):
    bsz, seqlen, _ = h.shape
    nb = seqlen // WINDOW
    grp = N_Q_HEADS // N_KV_HEADS
    q = (h @ w_q + b_q).reshape(bsz, seqlen, N_Q_HEADS, ATT_HEAD_DIM)
    q = apply_rope(q, cos, sin).reshape(bsz, nb, WINDOW, N_KV_HEADS, grp, ATT_HEAD_DIM)
    k_band = band_blocks(k_rot.reshape(bsz, nb, WINDOW, N_KV_HEADS, ATT_HEAD_DIM))
    v_band = band_blocks(v.reshape(bsz, nb, WINDOW, N_KV_HEADS, ATT_HEAD_DIM))
    scale = 1.0 / math.sqrt(ATT_HEAD_DIM)
    scores = jnp.einsum('bnqkgd,bnskd->bnkgqs', q, k_band,
                        preferred_element_type=jnp.float32) * scale
    qpos = jnp.arange(WINDOW)[:, None] + WINDOW
    kpos = jnp.arange(2 * WINDOW)[None, :]
    in_window = (kpos <= qpos) & (kpos > qpos - WINDOW)
    has_prev = (jnp.arange(nb) > 0)[:, None, None] | (kpos >= WINDOW)[None]
    mask = in_window[None] & has_prev
    scores = jnp.where(mask[None, :, None, None], scores, -jnp.inf)
    sink = sinks.astype(jnp.float32).reshape(N_KV_HEADS, grp)[None, None, :, :, None]
    m = jnp.maximum(scores.max(axis=-1), sink)
    p = jnp.exp(scores - m[..., None])
    probs = p / (p.sum(axis=-1) + jnp.exp(sink - m))[..., None]
    out = jnp.einsum('bnkgqs,bnskd->bnqkgd', probs.astype(v.dtype), v_band)
    out = out.reshape(bsz, seqlen, N_Q_HEADS * ATT_HEAD_DIM)
    return out @ w_o + b_o


def setup_inputs(seed: int = 0) -> dict:
    key = jax.random.key(seed)
    ks = jax.random.split(key, 24)
    f32 = jnp.float32

    def nrm(k, shape, scale):
        return jax.random.normal(k, shape, f32) * scale

    dt0 = jnp.exp(jax.random.uniform(ks[8], (N_A_LAYERS, SSM_HEADS), f32,
                                     math.log(1e-3), math.log(1e-1)))
    return {
        'x': nrm(ks[0], (BATCH, SEQ, D_MODEL), 1.0),
        'norm_w': 1.0 + nrm(ks[1], (DEPTH, 3, D_MODEL), 0.01),
        'ffn_w_gate': nrm(ks[2], (DEPTH, 2, D_MODEL, D_FF), D_MODEL ** -0.5),
        'ffn_w_up': nrm(ks[3], (DEPTH, 2, D_MODEL, D_FF), D_MODEL ** -0.5),
        'ffn_w_down': nrm(ks[4], (DEPTH, 2, D_FF, D_MODEL), D_FF ** -0.5),
        'ssm_w_in': nrm(ks[5], (N_A_LAYERS, D_MODEL, IN_PROJ_DIM), D_MODEL ** -0.5),
        'ssm_conv_w': nrm(ks[6], (N_A_LAYERS, CONV_WIDTH, CONV_DIM), CONV_WIDTH ** -0.5),
        'ssm_conv_b': nrm(ks[7], (N_A_LAYERS, CONV_DIM), 0.01),
        'ssm_dt_bias': dt0 + jnp.log(-jnp.expm1(-dt0)),
        'ssm_a_log': jnp.log(jax.random.uniform(ks[9], (N_A_LAYERS, SSM_HEADS), f32, 1.0, 16.0)),
        'ssm_d': 1.0 + nrm(ks[10], (N_A_LAYERS, SSM_HEADS), 0.01),
        'ssm_norm_w': 1.0 + nrm(ks[11], (N_A_LAYERS, D_INNER), 0.01),
        'ssm_w_out': nrm(ks[12], (N_A_LAYERS, D_INNER, D_MODEL), D_INNER ** -0.5),
        'kv_norm_w': 1.0 + nrm(ks[13], (D_MODEL,), 0.01),
        'w_k': nrm(ks[14], (D_MODEL, N_KV_HEADS * ATT_HEAD_DIM), D_MODEL ** -0.5),
        'b_k': nrm(ks[15], (N_KV_HEADS * ATT_HEAD_DIM,), 0.01),
        'w_v': nrm(ks[16], (D_MODEL, N_KV_HEADS * ATT_HEAD_DIM), D_MODEL ** -0.5),
        'b_v': nrm(ks[17], (N_KV_HEADS * ATT_HEAD_DIM,), 0.01),
        'attn_w_q': nrm(ks[18], (N_B_LAYERS, D_MODEL, N_Q_HEADS * ATT_HEAD_DIM), D_MODEL ** -0.5),
        'attn_b_q': nrm(ks[19], (N_B_LAYERS, N_Q_HEADS * ATT_HEAD_DIM), 0.01),
        'attn_sinks': nrm(ks[20], (N_B_LAYERS, N_Q_HEADS), 0.5),
        'attn_w_o': nrm(ks[21], (N_B_LAYERS, N_Q_HEADS * ATT_HEAD_DIM, D_MODEL),
                        (N_Q_HEADS * ATT_HEAD_DIM) ** -0.5),
        'attn_b_o': nrm(ks[22], (N_B_LAYERS, D_MODEL), 0.01),
        'final_norm_w': 1.0 + nrm(ks[23], (D_MODEL,), 0.01),
    }


def reference(x, norm_w, ffn_w_gate, ffn_w_up, ffn_w_down,
              ssm_w_in, ssm_conv_w, ssm_conv_b, ssm_dt_bias, ssm_a_log, ssm_d, ssm_norm_w, ssm_w_out,
              kv_norm_w, w_k, b_k, w_v, b_v,
              attn_w_q, attn_b_q, attn_sinks, attn_w_o, attn_b_o,
              final_norm_w):
    cos, sin = rope_tables(x.shape[1])
    k_shared = None
    v_shared = None
    for layer in range(DEPTH):
        if layer == N_A_LAYERS:
            k_shared, v_shared = shared_kv(x, kv_norm_w, w_k, b_k, w_v, b_v, cos, sin)
        x = x + FFN_RES_WEIGHT * swiglu(rmsnorm(x, norm_w[layer, 0]), ffn_w_gate[layer, 0],
                                        ffn_w_up[layer, 0], ffn_w_down[layer, 0])
        h = rmsnorm(x, norm_w[layer, 1])
        if layer < N_A_LAYERS:
            i = layer
            x = x + mamba2_mixer(h, ssm_w_in[i], ssm_conv_w[i], ssm_conv_b[i], ssm_dt_bias[i],
                                 ssm_a_log[i], ssm_d[i], ssm_norm_w[i], ssm_w_out[i])
        else:
            j = layer - N_A_LAYERS
            x = x + swa_sink_attention(h, k_shared, v_shared, attn_w_q[j], attn_b_q[j],
                                       attn_sinks[j], attn_w_o[j], attn_b_o[j], cos, sin)
        x = x + FFN_RES_WEIGHT * swiglu(rmsnorm(x, norm_w[layer, 2]), ffn_w_gate[layer, 1],
                                        ffn_w_up[layer, 1], ffn_w_down[layer, 1])
    return rmsnorm(x, final_norm_w)
```

```python
import numpy as np
from contextlib import ExitStack
import concourse.bass as bass
import concourse.mybir as mybir
from concourse.bass_utils import run_bass_kernel_spmd

F32 = mybir.dt.float32
BF16 = mybir.dt.bfloat16
AF = mybir.ActivationFunctionType
ALU = mybir.AluOpType

ENGS = ["pe", "act", "dve", "pool", "sp"]
BLOCK_ATTR = {"pe": "tensor", "act": "scalar", "dve": "vector", "pool": "gpsimd", "sp": "sync"}
COMPUTE = ("pe", "act", "dve")


class Res:
    __slots__ = ("name", "w", "rs", "dsem", "dcount")

    def __init__(self, name):
        self.name = name
        self.w = None
        self.rs = []
        self.dsem = None
        self.dcount = 0


class Op:
    __slots__ = ("eng", "fn", "deps", "idx", "is_dma", "sig", "dres", "sigval", "sem", "final")


class Prog:
    def __init__(self, nc):
        self.nc = nc
        self.ops = {e: [] for e in ENGS}
        self.dma_res = []

    def fresh(self, name):
        r = Res(name)
        r.rs = [self.ops[e][-1] for e in COMPUTE if self.ops[e]]
        return r

    def add(self, eng, fn, reads=(), writes=(), dma=False):
        op = Op()
        op.eng = eng
        op.fn = fn
        op.is_dma = dma
        op.idx = len(self.ops[eng])
        op.sig = False
        op.sigval = 0
        op.sem = None
        op.dres = None
        op.final = False
        deps = []
        for r in reads:
            if r.w is not None:
                deps.append(r.w)
        for w in writes:
            if w.w is not None:
                deps.append(w.w)
            for x in w.rs:
                deps.append(x)
        best = {}
        out = []
        for d in deps:
            if d is op:
                continue
            if d.is_dma:
                out.append(d)
                continue
            if d.eng == "pe" and eng == "pe" and not dma:
                continue
            if d.eng not in best or best[d.eng].idx < d.idx:
                best[d.eng] = d
        op.deps = out + list(best.values())
        for r in reads:
            if dma:
                r.rs.append(op)
            else:
                r.rs = [x for x in r.rs if x.is_dma or x.eng != eng] + [op]
        for w in writes:
            w.w = op
            w.rs = []
        if dma:
            assert len(writes) == 1
            op.dres = writes[0]
            if op.dres.dsem is None:
                op.dres.dsem = True
                self.dma_res.append(op.dres)
        self.ops[eng].append(op)
        return op

    def emit(self, stack):
        nc = self.nc
        for e in ENGS:
            for op in self.ops[e]:
                for d in op.deps:
                    d.sig = True
        esem = {e: stack.enter_context(nc.semaphore("s_" + e)) for e in ENGS}
        for i, r in enumerate(self.dma_res):
            r.dsem = stack.enter_context(nc.semaphore("d%d" % i))
            r.dcount = 0
        for e in ENGS:
            cnt = 0
            for op in self.ops[e]:
                if op.is_dma:
                    r = op.dres
                    r.dcount += 16
                    op.sem = r.dsem
                    op.sigval = r.dcount
                elif op.sig:
                    cnt += 1
                    op.sem = esem[e]
                    op.sigval = cnt
        block = stack.enter_context(nc.Block())
        for e in ENGS:
            if self.ops[e]:
                self._emit_engine(block, e)

    def _emit_engine(self, block, e):
        ops = self.ops[e]
        deco = getattr(block, BLOCK_ATTR[e])

        @deco
        def _(eng):
            waited = {}
            for op in ops:
                need = {}
                for d in op.deps:
                    key = id(d.sem)
                    if need.get(key, (None, 0))[1] < d.sigval:
                        need[key] = (d.sem, d.sigval)
                for key, (sem, val) in need.items():
                    if waited.get(key, 0) >= val:
                        continue
                    eng.wait_ge(sem, val)
                    waited[key] = val
                ins = op.fn(eng)
                if op.is_dma:
                    ins.then_inc(op.sem, 16)
                elif op.sig:
                    ins.then_inc(op.sem, 1)
            for op in ops:
                if op.is_dma and op.final:
                    eng.wait_ge(op.sem, op.sigval)


_SBN = [0]


def SB(nc, name, shape, dt):
    _SBN[0] += 1
    return nc.sbuf_tensor("%s_%d" % (name, _SBN[0]), shape, dt)


D = 1024
S = 2048
NDC = 8
NTG = 4
TGW = 512
DFF = 2816
NFC = 22
PASSES = [6, 6, 5, 5]
EPS = 1e-5
NH_SSM = 32
RING = 6
LOOK = 4

C_NW = 0
C_CW = 64
C_CB = C_CW + 96
C_BQ = C_CB + 24
C_BK = C_BQ + 16
C_BO = C_BK + 8
NCC = C_BO + 8
B_DTB = 0
B_ALOG = 32
B_DSK = 64
B_BV = 96
B_SINK = 352
B_SMALL = 368
B_SNW = 368
NCB = B_SNW + 2048
NCM = 512

PHASES = ["ffn0a", "mamba", "ffn0b", "kv", "ffn1a", "attn", "ffn1b", "final"]


def n_blocks(phases):
    n = 0
    for p in phases:
        if p.startswith("ffn"):
            n += NFC + 4 * len(PASSES)
        elif p == "mamba":
            n += 1 + 4 * 7
        elif p == "kv":
            n += 5
        elif p == "attn":
            n += 12
    return n


class KB:
    def __init__(self, phases, debug_out=None):
        self.phases = phases
        self.nblk = max(1, n_blocks(phases))
        nc = bass.Bass("TRN2", target_bir_lowering=False)
        self.nc = nc
        self.P = Prog(nc)
        self.st = ExitStack()
        st = self.st
        self.xT_d = nc.dram_tensor("xT", [D, S], F32, kind="ExternalInput").ap()
        self.wb_d = nc.dram_tensor("wblk", [self.nblk, 128, 2048], F32, kind="ExternalInput").ap()
        self.cc_d = nc.dram_tensor("ccol", [128, NCC], F32, kind="ExternalInput").ap()
        self.cb_d = nc.dram_tensor("cbc", [128, NCB], F32, kind="ExternalInput").ap()
        self.cm_d = nc.dram_tensor("cmat", [128, NCM], F32, kind="ExternalInput").ap()
        self.mk_d = nc.dram_tensor("maskb", [128, 256], F32, kind="ExternalInput").ap()
        self.rope_d = nc.dram_tensor("rope", [128, 2, S], F32, kind="ExternalInput").ap()
        self.out_d = nc.dram_tensor("outT", [D, S], F32, kind="ExternalOutput").ap()
        P = self.P
        sb = lambda name, shape, dt: st.enter_context(SB(nc, name, shape, dt))
        self.xT = sb("xT_sb", [128, NDC, S], F32)
        self.rx = [[Res("x%d_%d" % (dc, tg)) for tg in range(NTG)] for dc in range(NDC)]
        self.ring = sb("ring", [128, RING, 2048], BF16)
        self.rring = [Res("ring%d" % i) for i in range(RING)]
        self.ccol = sb("ccol_sb", [128, NCC], F32)
        self.cbc = sb("cbc_sb", [128, B_SMALL], F32)
        self.cmb = sb("cmat_bf", [128, NCM], BF16)
        self.cmf = sb("cmat_f", [128, NCM], F32)
        self.maskb = sb("maskb_sb", [128, 256], F32)
        self.rconst = Res("const")
        self.ps = [st.enter_context(nc.psum_tensor("ps%d" % i, [128, 512], F32)) for i in range(8)]
        self.rps = [Res("ps%d" % i) for i in range(8)]
        self.wi = 0
        self.wissued = 0
        rc = [Res("c%d" % i) for i in range(6)]
        P.add("sp", lambda e: e.dma_start(out=self.ccol[:], in_=self.cc_d), writes=[rc[0]], dma=True)
        P.add("sp", lambda e: e.dma_start(out=self.cbc[:], in_=self.cb_d[:, 0:B_SMALL]), writes=[rc[1]], dma=True)
        P.add("sp", lambda e: e.dma_start(out=self.cmf[:], in_=self.cm_d), writes=[rc[2]], dma=True)
        P.add("sp", lambda e: e.dma_start(out=self.maskb[:], in_=self.mk_d), writes=[rc[3]], dma=True)
        P.add("pool", lambda e: e.dma_start(out=self.cmb[:], in_=self.cm_d), writes=[rc[4]], dma=True)
        self.rcs = rc[:5]
        xv = self.xT_d.rearrange("(c p) t -> p c t", p=128)
        for tg in range(NTG):
            for dc in range(NDC):
                sl = slice(tg * TGW, (tg + 1) * TGW)
                P.add("sp", lambda e, dc=dc, sl=sl: e.dma_start(out=self.xT[:, dc, sl], in_=xv[:, dc, sl]),
                      writes=[self.rx[dc][tg]], dma=True)
        self.ident = self.cmb[:, 0:128]
        self.tri_b = self.cmb[:, 128:256]
        self.U_b = self.cmb[:, 256:384]
        self.ones_b = self.cmb[:, 384:512]
        self.tri_f = self.cmf[:, 128:256]
        self.ones_f = self.cmf[:, 384:512]
        self.kT = None

    def wget(self, keep=0):
        P = self.P
        i = self.wi
        self.wi += 1
        assert i < self.nblk, "weight stream overrun"
        lim = min(self.nblk, i + 1 + LOOK, i - keep + RING)
        while self.wissued < lim:
            k = self.wissued
            s = k % RING
            P.add("pool", lambda e, k=k, s=s: e.dma_start(out=self.ring[:, s, :], in_=self.wb_d[k]),
                  writes=[self.rring[s]], dma=True)
            self.wissued += 1
        s = i % RING
        return self.ring[:, s, :], self.rring[s]

    def creads(self):
        return list(self.rcs)

    def emit_norm(self, ph, widx, out_t, rout, final=False):
        P, nc = self.P, self.nc
        sq = [ph.enter_context(SB(nc, "nsq%d" % k, [128, TGW], BF16)) for k in range(2)]
        rsq = [P.fresh("nsq%d" % k) for k in range(2)]
        sd = [ph.enter_context(SB(nc, "nsd%d" % k, [128, TGW], F32)) for k in range(2)]
        rsd = [P.fresh("nsd%d" % k) for k in range(2)]
        rstd = [ph.enter_context(SB(nc, "nrstd%d" % k, [128, TGW], F32)) for k in range(2)]
        rrstd = [P.fresh("nrstd%d" % k) for k in range(2)]
        psn, rpsn = self.ps[6], self.rps[6]
        cr = self.creads()
        for tg in range(NTG):
            sl = slice(tg * TGW, (tg + 1) * TGW)
            k2 = tg % 2
            for dc in range(NDC):
                k = dc % 2
                P.add("act", lambda e, k=k, dc=dc, sl=sl: e.activation(out=sq[k][:], in_=self.xT[:, dc, sl], func=AF.Square),
                      reads=[self.rx[dc][tg]], writes=[rsq[k]])
                P.add("pe", lambda e, k=k, dc=dc: e.matmul(psn[:], lhsT=self.ones_b, rhs=sq[k][:], start=(dc == 0), stop=(dc == NDC - 1)),
                      reads=[rsq[k]] + cr, writes=[rpsn])
            P.add("act", lambda e, k2=k2: e.activation(out=sd[k2][:], in_=psn[:], func=AF.Ln, bias=self.epsc[:, 0:1], scale=1.0 / D),
                  reads=[rpsn] + cr, writes=[rsd[k2]])
            P.add("act", lambda e, k2=k2: e.activation(out=rstd[k2][:], in_=sd[k2][:], func=AF.Exp, scale=-0.5),
                  reads=[rsd[k2]], writes=[rrstd[k2]])
            for dc in range(NDC):
                col = C_NW + widx * 8 + dc
                P.add("dve", lambda e, dc=dc, sl=sl, col=col, k2=k2: e.scalar_tensor_tensor(
                    out=out_t[:, dc, sl], in0=self.xT[:, dc, sl], scalar=self.ccol[:, col:col + 1], in1=rstd[k2][:],
                    op0=ALU.mult, op1=ALU.mult),
                    reads=[self.rx[dc][tg], rrstd[k2]] + cr, writes=[rout[dc][tg]])

    def emit_ffn(self, widx):
        P, nc = self.P, self.nc
        with ExitStack() as ph:
            hT = ph.enter_context(SB(nc, "hT", [128, NDC, S], BF16))
            rh = [[P.fresh("h") for _ in range(NTG)] for _ in range(NDC)]
            with ExitStack() as phn:
                self.emit_norm(phn, widx, hT, rh)
            npmax = max(PASSES)
            aT = ph.enter_context(SB(nc, "aT", [128, npmax, S], BF16))
            ra = [[P.fresh("a") for _ in range(NTG)] for _ in range(npmax)]
            sg = [ph.enter_context(SB(nc, "sg%d" % k, [128, TGW], F32)) for k in range(2)]
            rsg = [P.fresh("sg") for _ in range(2)]
            cnt = 0
            for npass_i, npass in enumerate(PASSES):
                for jj in range(npass):
                    slot, rslot = self.wget()
                    sv = slot.rearrange("p (a b c) -> p a b c", a=2, b=NDC)
                    for tg in range(NTG):
                        sl = slice(tg * TGW, (tg + 1) * TGW)
                        k = cnt % 2
                        cnt += 1
                        pg, rpg = self.ps[k], self.rps[k]
                        pu, rpu = self.ps[2 + k], self.rps[2 + k]
                        for which, (pp, rpp) in enumerate(((pg, rpg), (pu, rpu))):
                            for dc in range(NDC):
                                P.add("pe", lambda e, pp=pp, which=which, dc=dc, sl=sl, sv=sv: e.matmul(
                                    pp[:], lhsT=sv[:, which, dc, :], rhs=hT[:, dc, sl], start=(dc == 0), stop=(dc == NDC - 1)),
                                    reads=[rslot, rh[dc][tg]], writes=[rpp])
                        P.add("act", lambda e, k=k, pg=pg: e.activation(out=sg[k][:], in_=pg[:], func=AF.Silu),
                              reads=[rpg], writes=[rsg[k]])
                        P.add("dve", lambda e, k=k, pu=pu, jj=jj, sl=sl: e.tensor_tensor(
                            out=aT[:, jj, sl], in0=sg[k][:], in1=pu[:], op=ALU.mult),
                            reads=[rsg[k], rpu], writes=[ra[jj][tg]])
                last = (npass_i == len(PASSES) - 1)
                if last:
                    blks = []
                    for b in range(4):
                        slot, rslot = self.wget(keep=b)
                        blks.append((slot[:, 0:npass * 256].rearrange("p (j c) -> p j c", j=npass), rslot))
                    for tg in range(NTG):
                        sl = slice(tg * TGW, (tg + 1) * TGW)
                        for dc in range(NDC):
                            sv, rslot = blks[dc // 2]
                            d2 = dc % 2
                            k = cnt % 2
                            cnt += 1
                            pd, rpd = self.ps[4 + k], self.rps[4 + k]
                            for jj in range(npass):
                                P.add("pe", lambda e, pd=pd, sv=sv, jj=jj, d2=d2, sl=sl, npass=npass: e.matmul(
                                    pd[:], lhsT=sv[:, jj, d2 * 128:(d2 + 1) * 128], rhs=aT[:, jj, sl],
                                    start=(jj == 0), stop=(jj == npass - 1)),
                                    reads=[rslot, ra[jj][tg]], writes=[rpd])
                            P.add("dve", lambda e, pd=pd, dc=dc, sl=sl: e.scalar_tensor_tensor(
                                out=self.xT[:, dc, sl], in0=pd[:], scalar=0.5, in1=self.xT[:, dc, sl],
                                op0=ALU.mult, op1=ALU.add),
                                reads=[rpd, self.rx[dc][tg]], writes=[self.rx[dc][tg]])
                for b in range(0 if last else 4):
                    slot, rslot = self.wget()
                    sv = slot[:, 0:npass * 256].rearrange("p (j c) -> p j c", j=npass)
                    for d2 in range(2):
                        dc = b * 2 + d2
                        for tg in range(NTG):
                            sl = slice(tg * TGW, (tg + 1) * TGW)
                            k = cnt % 2
                            cnt += 1
                            pd, rpd = self.ps[4 + k], self.rps[4 + k]
                            for jj in range(npass):
                                P.add("pe", lambda e, pd=pd, sv=sv, jj=jj, d2=d2, sl=sl, npass=npass: e.matmul(
                                    pd[:], lhsT=sv[:, jj, d2 * 128:(d2 + 1) * 128], rhs=aT[:, jj, sl],
                                    start=(jj == 0), stop=(jj == npass - 1)),
                                    reads=[rslot, ra[jj][tg]], writes=[rpd])
                            P.add("dve", lambda e, pd=pd, dc=dc, sl=sl: e.scalar_tensor_tensor(
                                out=self.xT[:, dc, sl], in0=pd[:], scalar=0.5, in1=self.xT[:, dc, sl],
                                op0=ALU.mult, op1=ALU.add),
                                reads=[rpd, self.rx[dc][tg]], writes=[self.rx[dc][tg]])

    def emit_final(self, do_norm=True):
        P, nc = self.P, self.nc
        ov = self.out_d.rearrange("(c p) t -> p c t", p=128)
        with ExitStack() as ph:
            if do_norm:
                oT = ph.enter_context(SB(nc, "oT", [128, NDC, S], F32))
                ro = [[P.fresh("o") for _ in range(NTG)] for _ in range(NDC)]
                self.emit_norm(ph, 7, oT, ro)
            else:
                oT, ro = self.xT, self.rx
            for tg in range(NTG):
                for dc in range(NDC):
                    sl = slice(tg * TGW, (tg + 1) * TGW)
                    o = P.add("sp", lambda e, dc=dc, sl=sl: e.dma_start(out=ov[:, dc, sl], in_=oT[:, dc, sl]),
                              reads=[ro[dc][tg]], writes=[Res("out")], dma=True)
                    o.final = True

    def emit_rope_proj(self, ph, hT, rh, nchunks, bias_col0, out_t, rout, ropeb, rrope, tmp, rtmp, cnt0=0):
        P = self.P
        cr = self.creads()
        cnt = cnt0
        for c in range(nchunks):
            slot, rslot = self.wget()
            sv = slot.rearrange("p (a b c) -> p a b c", a=2, b=NDC)
            for tg in range(NTG):
                sl = slice(tg * TGW, (tg + 1) * TGW)
                k = cnt % 2
                cnt += 1
                pq, rpq = self.ps[k], self.rps[k]
                pqs, rpqs = self.ps[2 + k], self.rps[2 + k]
                P.add("sp", lambda e, k=k, sl=sl: e.dma_start(out=ropeb[k][:], in_=self.rope_d[:, :, sl]),
                      writes=[rrope[k]], dma=True)
                for which, (pp, rpp) in enumerate(((pq, rpq), (pqs, rpqs))):
                    for dc in range(NDC):
                        P.add("pe", lambda e, pp=pp, which=which, dc=dc, sl=sl, sv=sv: e.matmul(
                            pp[:], lhsT=sv[:, which, dc, :], rhs=hT[:, dc, sl], start=(dc == 0), stop=(dc == NDC - 1)),
                            reads=[rslot, rh[dc][tg]], writes=[rpp])
                b0 = bias_col0 + c * 2
                P.add("dve", lambda e, k=k, pq=pq, b0=b0: e.scalar_tensor_tensor(
                    out=tmp[2 * k][:], in0=pq[:], scalar=self.ccol[:, b0:b0 + 1], in1=ropeb[k][:, 0, :],
                    op0=ALU.add, op1=ALU.mult), reads=[rpq, rrope[k]] + cr, writes=[rtmp[2 * k]])
                P.add("dve", lambda e, k=k, pqs=pqs, b0=b0: e.scalar_tensor_tensor(
                    out=tmp[2 * k + 1][:], in0=pqs[:], scalar=self.ccol[:, b0 + 1:b0 + 2], in1=ropeb[k][:, 1, :],
                    op0=ALU.add, op1=ALU.mult), reads=[rpqs, rrope[k]] + cr, writes=[rtmp[2 * k + 1]])
                P.add("dve", lambda e, k=k, c=c, sl=sl: e.tensor_tensor(
                    out=out_t[:, c, sl], in0=tmp[2 * k][:], in1=tmp[2 * k + 1][:], op=ALU.add),
                    reads=[rtmp[2 * k], rtmp[2 * k + 1]], writes=[rout[c][tg]])
        return cnt

    def _rope_bufs(self, ph):
        P, nc = self.P, self.nc
        ropeb = [ph.enter_context(SB(nc, "ropeb%d" % k, [128, 2, TGW], F32)) for k in range(2)]
        rrope = [P.fresh("ropeb") for _ in range(2)]
        tmp = [ph.enter_context(SB(nc, "rtmp%d" % k, [128, TGW], F32)) for k in range(4)]
        rtmp = [P.fresh("rtmp") for _ in range(4)]
        return ropeb, rrope, tmp, rtmp

    def emit_kv(self):
        P, nc = self.P, self.nc
        st = self.st
        self.kT = st.enter_context(SB(nc, "kT", [128, 4, S], BF16))
        self.rk = [[P.fresh("k") for _ in range(NTG)] for _ in range(4)]
        self.vtok = st.enter_context(SB(nc, "vtok", [128, 16, 256], BF16))
        self.rv = [P.fresh("v") for _ in range(16)]
        cr = self.creads()
        with ExitStack() as ph:
            hT = ph.enter_context(SB(nc, "hT", [128, NDC, S], BF16))
            rh = [[P.fresh("h") for _ in range(NTG)] for _ in range(NDC)]
            with ExitStack() as phn:
                self.emit_norm(phn, 6, hT, rh)
            ropeb, rrope, tmp, rtmp = self._rope_bufs(ph)
            self.emit_rope_proj(ph, hT, rh, 4, C_BK, self.kT, self.rk, ropeb, rrope, tmp, rtmp)
            slot, rslot = self.wget()
            sv = slot.rearrange("p (b c) -> p b c", b=NDC)
            for n in range(16):
                k = n % 2
                pv, rpv = self.ps[4 + k], self.rps[4 + k]
                for dc in range(NDC):
                    P.add("pe", lambda e, pv=pv, dc=dc, n=n: e.matmul(
                        pv[:, 0:256], lhsT=hT[:, dc, n * 128:(n + 1) * 128], rhs=sv[:, dc, :],
                        start=(dc == 0), stop=(dc == NDC - 1)), reads=[rslot, rh[dc][n // 4]], writes=[rpv])
                P.add("dve", lambda e, pv=pv, n=n: e.tensor_tensor(
                    out=self.vtok[:, n, :], in0=pv[:, 0:256], in1=self.cbc[:, B_BV:B_BV + 256], op=ALU.add),
                    reads=[rpv] + cr, writes=[self.rv[n]])

    def emit_attn(self):
        P, nc = self.P, self.nc
        cr = self.creads()
        with ExitStack() as ph:
            qT = ph.enter_context(SB(nc, "qT", [128, NDC, S], BF16))
            rq = [[P.fresh("q") for _ in range(NTG)] for _ in range(NDC)]
            with ExitStack() as ph2:
                hT = ph2.enter_context(SB(nc, "hT", [128, NDC, S], BF16))
                rh = [[P.fresh("h") for _ in range(NTG)] for _ in range(NDC)]
                with ExitStack() as phn:
                    self.emit_norm(phn, 4, hT, rh)
                ropeb, rrope, tmp, rtmp = self._rope_bufs(ph2)
                self.emit_rope_proj(ph2, hT, rh, 8, C_BQ, qT, rq, ropeb, rrope, tmp, rtmp)
            aTt = ph.enter_context(SB(nc, "attnT", [128, NDC, S], BF16))
            raT = [[P.fresh("aT") for _ in range(NTG)] for _ in range(NDC)]
            sm = [ph.enter_context(SB(nc, "sm%d" % k, [128, 4, 256], F32)) for k in range(2)]
            rsm = [P.fresh("sm") for _ in range(2)]
            pb = [ph.enter_context(SB(nc, "pb%d" % k, [128, 4, 256], BF16)) for k in range(2)]
            rpb = [P.fresh("pb") for _ in range(2)]
            ptb = [ph.enter_context(SB(nc, "ptb%d" % k, [128, 8, 128], BF16)) for k in range(2)]
            rptb = [P.fresh("ptb") for _ in range(2)]
            stat2 = [ph.enter_context(SB(nc, "stat%d" % i, [128, 6, 16], F32)) for i in range(2)]
            rstat2 = [[P.fresh("stat%d" % i) for i in range(6)] for _ in range(2)]
            atok = ph.enter_context(SB(nc, "atok", [128, 1024], BF16))
            ratok = P.fresh("atok")
            po = [self.ps[4], self.ps[5]]
            rpo = [self.rps[4], self.rps[5]]
            X = mybir.AxisListType.X

            sinkmax = ph.enter_context(SB(nc, "sinkmax", [128, 4], F32))
            rsinkmax = P.fresh("sinkmax")
            P.add("dve", lambda e: e.tensor_reduce(
                out=sinkmax[:], in_=self.cbc[:, B_SINK:B_SINK + 16].rearrange("p (j h) -> p j h", j=4), axis=X, op=ALU.max),
                reads=cr, writes=[rsinkmax])

            def geom(n):
                nk = 128 if n == 0 else 256
                return nk, nk // 128, (0 if n == 0 else (n - 1) * 128), (128 if n == 0 else 0)

            def stA1(g):
                n, j = g // 4, g % 4
                k = g % 2
                nk, nhalf, k0, mcol = geom(n)
                stat, rstat = stat2[n % 2], rstat2[n % 2]
                kh = j
                kreads = [self.rk[kh][n // 4]] + ([self.rk[kh][(n - 1) // 4]] if n > 0 else [])
                for hh in range(4):
                    h = 4 * j + hh
                    c, base = h // 2, 64 * (h % 2)
                    pS, rpS = self.ps[2 * k + hh % 2], self.rps[2 * k + hh % 2]
                    P.add("pe", lambda e, pS=pS, base=base, c=c, n=n, kh=kh, k0=k0, nk=nk, hh=hh: e.matmul(
                        pS[:, (hh // 2) * 256:(hh // 2) * 256 + nk], lhsT=qT[base:base + 64, c, n * 128:(n + 1) * 128],
                        rhs=self.kT[base:base + 64, kh, k0:k0 + nk], start=True, stop=True),
                        reads=[rq[c][n // 4]] + kreads, writes=[rpS])
                for b2 in range(2):
                    pS, rpS = self.ps[2 * k + b2], self.rps[2 * k + b2]
                    P.add("dve", lambda e, k=k, pS=pS, nk=nk, mcol=mcol, b2=b2: e.scalar_tensor_tensor(
                        out=sm[k][:, 2 * b2:2 * b2 + 2, 0:nk], in0=pS[:].rearrange("p (h s) -> p h s", h=2)[:, :, 0:nk],
                        scalar=0.125, in1=self.maskb[:, mcol:mcol + nk].unsqueeze(1).broadcast_to([128, 2, nk]),
                        op0=ALU.mult, op1=ALU.add), reads=[rpS] + cr, writes=[rsm[k]])
                P.add("dve", lambda e, k=k, nk=nk, j=j, stat=stat: e.tensor_reduce(
                    out=stat[:, 0, j:j + 1], in_=sm[k][:, :, 0:nk], axis=mybir.AxisListType.XY, op=ALU.max),
                    reads=[rsm[k]], writes=[rstat[0]])
                P.add("dve", lambda e, j=j, stat=stat: e.tensor_tensor(
                    out=stat[:, 0, j:j + 1], in0=stat[:, 0, j:j + 1], in1=sinkmax[:, j:j + 1], op=ALU.max),
                    reads=[rstat[0], rsinkmax], writes=[rstat[0]])
                P.add("dve", lambda e, j=j, stat=stat: e.tensor_scalar(
                    out=stat[:, 1, j:j + 1], in0=stat[:, 0, j:j + 1], scalar1=-1.0, scalar2=None, op0=ALU.mult),
                    reads=[rstat[0]], writes=[rstat[1]])

            def stA2(g):
                n, j = g // 4, g % 4
                k = g % 2
                nk, nhalf, k0, mcol = geom(n)
                stat, rstat = stat2[n % 2], rstat2[n % 2]
                P.add("act", lambda e, k=k, nk=nk, j=j, stat=stat: e.activation(
                    out=pb[k][:, :, 0:nk], in_=sm[k][:, :, 0:nk], func=AF.Exp, bias=stat[:, 1, j:j + 1], scale=1.0),
                    reads=[rsm[k], rstat[1]], writes=[rpb[k]])

            def stB1(g):
                n, j = g // 4, g % 4
                k = g % 2
                nk, nhalf, k0, mcol = geom(n)
                stat, rstat = stat2[n % 2], rstat2[n % 2]
                P.add("dve", lambda e, k=k, nk=nk, j=j, stat=stat: e.tensor_reduce(
                    out=stat[:, 2, 4 * j:4 * j + 4].rearrange("p (a b) -> p b a", a=2),
                    in_=pb[k][:, :, 0:nk].rearrange("p (b a) s -> p b a s", b=2), axis=X, op=ALU.add),
                    reads=[rpb[k]], writes=[rstat[2]])
                if j == 3:
                    P.add("dve", lambda e, stat=stat: e.tensor_tensor(
                        out=stat[:, 3, :].rearrange("p (j h) -> p j h", j=4),
                        in0=self.cbc[:, B_SINK:B_SINK + 16].rearrange("p (j h) -> p j h", j=4),
                        in1=stat[:, 0, 0:4].to_broadcast([128, 4, 4]), op=ALU.subtract),
                        reads=[rstat[0]] + cr, writes=[rstat[3]])
                    P.add("act", lambda e, stat=stat: e.activation(out=stat[:, 3, :], in_=stat[:, 3, :], func=AF.Exp), reads=[rstat[3]], writes=[rstat[3]])
                    P.add("dve", lambda e, stat=stat: e.tensor_tensor(out=stat[:, 4, :], in0=stat[:, 3, :], in1=stat[:, 2, :], op=ALU.add),
                          reads=[rstat[3], rstat[2]], writes=[rstat[4]])
                    P.add("dve", lambda e, stat=stat: e.reciprocal(out=stat[:, 5, :], in_=stat[:, 4, :]), reads=[rstat[4]], writes=[rstat[5]])

                for hh in range(4):
                    pt, rpt = self.ps[6 + hh // 2], self.rps[6 + hh // 2]
                    for hf in range(nhalf):
                        slot_i = (hh % 2) * 2 + hf
                        P.add("pe", lambda e, pt=pt, k=k, hh=hh, hf=hf, slot_i=slot_i: e.matmul(
                            pt[:, slot_i * 128:(slot_i + 1) * 128], lhsT=pb[k][:, (hh % 2) * 2 + hh // 2, hf * 128:(hf + 1) * 128], rhs=self.ident,
                            start=True, stop=True), reads=[rpb[k]] + cr, writes=[rpt])
                for b2 in range(2):
                    pt, rpt = self.ps[6 + b2], self.rps[6 + b2]
                    for a2 in range(2):
                        P.add("act", lambda e, k=k, pt=pt, b2=b2, nhalf=nhalf, a2=a2: e.copy(
                            out=ptb[k][:, 4 * b2 + 2 * a2:4 * b2 + 2 * a2 + nhalf, :],
                            in_=pt[:, a2 * 256:a2 * 256 + nhalf * 128].rearrange("p (f t) -> p f t", f=nhalf)),
                            reads=[rpt], writes=[rptb[k]])

            def stB2(g):
                n, j = g // 4, g % 4
                k = g % 2
                nk, nhalf, k0, mcol = geom(n)
                stat, rstat = stat2[n % 2], rstat2[n % 2]
                kh = j
                for hh in range(4):
                    h = 4 * j + hh
                    ob = h // 8
                    oc = (h % 8) * 64
                    for hf in range(nhalf):
                        nb = n if nhalf == 1 else (n - 1 + hf)
                        si = hh * 2 + hf
                        P.add("pe", lambda e, ob=ob, oc=oc, k=k, hf=hf, nb=nb, kh=kh, nhalf=nhalf, si=si: e.matmul(
                            po[ob][:, oc:oc + 64], lhsT=ptb[k][:, si, :],
                            rhs=self.vtok[:, nb, kh * 64:(kh + 1) * 64], start=(hf == 0), stop=(hf == nhalf - 1)),
                            reads=[rptb[k], self.rv[nb]], writes=[rpo[ob]])
                if j == 3:
                    for ob in range(2):
                        P.add("dve", lambda e, ob=ob, stat=stat: e.tensor_tensor(
                            out=atok[:, ob * 512:(ob + 1) * 512].rearrange("p (h d) -> p h d", h=8),
                            in0=po[ob][:].rearrange("p (h d) -> p h d", h=8),
                            in1=stat[:, 5, ob * 8:(ob + 1) * 8].to_broadcast([128, 8, 64]), op=ALU.mult),
                            reads=[rpo[ob], rstat[5]], writes=[ratok])
                    for half in range(2):
                        pt, rpt = self.ps[6 + half], self.rps[6 + half]
                        for c4 in range(4):
                            c = half * 4 + c4
                            P.add("pe", lambda e, pt=pt, c4=c4, c=c: e.matmul(
                                pt[:, c4 * 128:(c4 + 1) * 128], lhsT=atok[:, c * 128:(c + 1) * 128], rhs=self.ident,
                                start=True, stop=True), reads=[ratok] + cr, writes=[rpt])
                        P.add("act", lambda e, pt=pt, half=half, n=n: e.copy(
                            out=aTt[:, half * 4:(half + 1) * 4, n * 128:(n + 1) * 128],
                            in_=pt[:].rearrange("p (c t) -> p c t", c=4)),
                            reads=[rpt], writes=[raT[half * 4 + c4][n // 4] for c4 in range(4)])

            stages = [stA1, stA2, stB1, stB2]
            NG = 64
            for t in range(NG + len(stages) - 1):
                for si_ in reversed(range(len(stages))):
                    g = t - si_
                    if 0 <= g < NG:
                        stages[si_](g)
            cnt = 0
            blks = []
            for b in range(4):
                slot, rslot = self.wget(keep=b)
                blks.append((slot.rearrange("p (c k) -> p c k", c=NDC), rslot))
            for tg in range(NTG):
                sl = slice(tg * TGW, (tg + 1) * TGW)
                for dc in range(NDC):
                    sv, rslot = blks[dc // 2]
                    d2 = dc % 2
                    k = cnt % 2
                    cnt += 1
                    pd, rpd = self.ps[2 + k], self.rps[2 + k]
                    for c in range(NDC):
                        P.add("pe", lambda e, pd=pd, sv=sv, c=c, d2=d2, sl=sl: e.matmul(
                            pd[:], lhsT=sv[:, c, d2 * 128:(d2 + 1) * 128], rhs=aTt[:, c, sl],
                            start=(c == 0), stop=(c == NDC - 1)), reads=[rslot, raT[c][tg]], writes=[rpd])
                    P.add("dve", lambda e, pd=pd, dc=dc, sl=sl: e.scalar_tensor_tensor(
                        out=self.xT[:, dc, sl], in0=pd[:], scalar=self.ccol[:, C_BO + dc:C_BO + dc + 1], in1=self.xT[:, dc, sl],
                        op0=ALU.add, op1=ALU.add), reads=[rpd, self.rx[dc][tg]] + cr, writes=[self.rx[dc][tg]])

    def emit_mamba(self):
        P, nc = self.P, self.nc
        cr = self.creads()
        X = mybir.AxisListType.X
        with ExitStack() as ph:
            hT = ph.enter_context(SB(nc, "hT", [128, NDC, S], BF16))
            rh = [[P.fresh("h") for _ in range(NTG)] for _ in range(NDC)]
            with ExitStack() as phn:
                self.emit_norm(phn, 1, hT, rh)
            tab = ph.enter_context(SB(nc, "tab", [128, 5, 512], F32))
            rtab = [P.fresh("tab%d" % i) for i in range(5)]
            with ExitStack() as p0:
                tA = [p0.enter_context(SB(nc, "tA%d" % k, [128, 512], F32)) for k in range(2)]
                rtA = [P.fresh("tA") for _ in range(2)]
                eA = p0.enter_context(SB(nc, "eA", [128, 32], F32))
                reA = P.fresh("eA")
                slot, rslot = self.wget()
                sv = slot[:, 0:256].rearrange("p (b h) -> p b h", b=NDC)
                pdt, rpdt = self.ps[0], self.rps[0]
                for c in range(16):
                    for dc in range(NDC):
                        P.add("pe", lambda e, c=c, dc=dc, sv=sv: e.matmul(
                            pdt[:, c * 32:(c + 1) * 32], lhsT=hT[:, dc, c * 128:(c + 1) * 128], rhs=sv[:, dc, :],
                            start=(dc == 0), stop=(dc == NDC - 1)), reads=[rslot, rh[dc][c // 4]], writes=[rpdt])
                v3 = lambda ap: ap.rearrange("p (c h) -> p c h", c=16)
                P.add("dve", lambda e: e.tensor_tensor(
                    out=v3(tA[0][:]), in0=v3(pdt[:]), in1=self.cbc[:, B_DTB:B_DTB + 32].unsqueeze(1).broadcast_to([128, 16, 32]),
                    op=ALU.add), reads=[rpdt] + cr, writes=[rtA[0]])
                P.add("act", lambda e: e.activation(out=tA[0][:], in_=tA[0][:], func=AF.Exp), reads=[rtA[0]], writes=[rtA[0]])
                P.add("act", lambda e: e.activation(out=tab[:, 0, :], in_=tA[0][:], func=AF.Ln, bias=self.kc[:, 1:2], scale=1.0),
                      reads=[rtA[0]] + cr, writes=[rtab[0]])
                P.add("act", lambda e: e.activation(out=eA[:], in_=self.cbc[:, B_ALOG:B_ALOG + 32], func=AF.Exp),
                      reads=cr, writes=[reA])
                P.add("dve", lambda e: e.scalar_tensor_tensor(
                    out=v3(tab[:, 1, :]), in0=v3(tab[:, 0, :]), scalar=-1.0, in1=eA[:].unsqueeze(1).broadcast_to([128, 16, 32]),
                    op0=ALU.mult, op1=ALU.mult), reads=[rtab[0], reA], writes=[rtab[1]])
                pacs, rpacs = self.ps[1], self.rps[1]
                pal, rpal = self.ps[2], self.rps[2]
                ahl = [p0.enter_context(SB(nc, "ahl%d" % k, [128, 512], BF16)) for k in range(2)]
                rahl = [P.fresh("ahl") for _ in range(2)]
                P.add("dve", lambda e: e.tensor_copy(out=ahl[0][:], in_=tab[:, 1, :]), reads=[rtab[1]], writes=[rahl[0]])
                P.add("dve", lambda e: e.tensor_tensor(out=ahl[1][:], in0=tab[:, 1, :], in1=ahl[0][:], op=ALU.subtract),
                      reads=[rtab[1], rahl[0]], writes=[rahl[1]])
                for k in range(2):
                    P.add("pe", lambda e, k=k: e.matmul(pacs[:], lhsT=self.tri_b, rhs=ahl[k][:], start=(k == 0), stop=(k == 1)),
                          reads=[rahl[k]] + cr, writes=[rpacs])
                for k in range(2):
                    P.add("pe", lambda e, k=k: e.matmul(pal[:], lhsT=self.ones_b, rhs=ahl[k][:], start=(k == 0), stop=(k == 1)),
                          reads=[rahl[k]] + cr, writes=[rpal])
                P.add("act", lambda e: e.activation(out=tab[:, 2, :], in_=pacs[:], func=AF.Exp), reads=[rpacs], writes=[rtab[2]])
                P.add("act", lambda e: e.activation(out=tab[:, 4, :], in_=pal[:], func=AF.Exp), reads=[rpal], writes=[rtab[4]])
                P.add("act", lambda e: e.copy(out=tA[1][:], in_=pacs[:]), reads=[rpacs], writes=[rtA[1]])
                P.add("dve", lambda e: e.tensor_tensor(out=tA[0][:], in0=pal[:], in1=tA[1][:], op=ALU.subtract),
                      reads=[rpal, rtA[1], rtA[0]], writes=[rtA[0]])
                P.add("act", lambda e: e.activation(out=tA[0][:], in_=tA[0][:], func=AF.Exp), reads=[rtA[0]], writes=[rtA[0]])
                P.add("dve", lambda e: e.tensor_tensor(out=tab[:, 3, :], in0=tab[:, 0, :], in1=tA[0][:], op=ALU.mult),
                      reads=[rtab[0], rtA[0]], writes=[rtab[3]])
            self.dbg = 0
            if self.dbg == 1:
                return
            for g in range(4):
                self._mamba_group(g, hT, rh, tab, rtab, cr)
                if self.dbg >= 2:
                    return

    def _mamba_group(self, g, hT, rh, tab, rtab, cr):
        P, nc = self.P, self.nc
        with ExitStack() as pg:
            BT = pg.enter_context(SB(nc, "BT", [128, S], BF16))
            CT = pg.enter_context(SB(nc, "CT", [128, S], BF16))
            Btok = pg.enter_context(SB(nc, "Btok", [128, 16, 128], BF16))
            xstok = pg.enter_context(SB(nc, "xstok", [128, 16, 512], BF16))
            snw = pg.enter_context(SB(nc, "snw", [128, 512], F32))
            Sst = pg.enter_context(SB(nc, "Sst", [128, 512], F32))
            Sbf = pg.enter_context(SB(nc, "Sbf", [128, 512], BF16))
            rBT, rCT, rS, rSbf, rsnw = P.fresh("BT"), P.fresh("CT"), P.fresh("S"), P.fresh("Sbf"), P.fresh("snw")
            rBtok = [P.fresh("Btok") for _ in range(4)]
            rxstok = [P.fresh("xstok") for _ in range(4)]
            P.add("sp", lambda e, g=g: e.dma_start(out=snw[:], in_=self.cb_d[:, B_SNW + g * 512:B_SNW + (g + 1) * 512]),
                  writes=[rsnw], dma=True)
            tcnt = 0
            with ExitStack() as s1:
                ubuf = s1.enter_context(SB(nc, "ubuf", [128, S + 4], F32))
                acc = s1.enter_context(SB(nc, "acc", [128, S], F32))
                xsT = s1.enter_context(SB(nc, "xsTc", [128, S], BF16))
                rub = [P.fresh("ubuf") for _ in range(NTG + 1)]
                racc = [P.fresh("acc") for _ in range(NTG)]
                P.add("dve", lambda e: e.memset(ubuf[:, 0:3], 0.0), writes=[rub[0]])
                xsT2 = [xsT, s1.enter_context(SB(nc, "xsTd", [128, S], BF16))]
                units = []
                state = {"ccnt": 0, "tcnt": 0}
                rBTl, rCTl = [None], [None]

                def stA(u):
                    ci, tg = u // 4, u % 4
                    bi, k2 = ci // 2, ci % 2
                    if tg == 0:
                        if k2 == 0:
                            slot, rslot = self.wget()
                            state["sv"] = slot.rearrange("p (k b c) -> p k b c", k=2, b=NDC)
                            state["rslot"] = rslot
                        if ci < 4:
                            state["dst"] = xsT2[state["ccnt"] % 2]
                            state["ccnt"] += 1
                        else:
                            state["dst"] = BT if ci == 4 else CT
                        state["rdst"] = [P.fresh("dst") for _ in range(NTG)]
                        if ci == 4:
                            rBTl[0] = state["rdst"]
                        elif ci == 5:
                            rCTl[0] = state["rdst"]
                    sv, rslot, dst, rdst = state["sv"], state["rslot"], state["dst"], state["rdst"]
                    cc = g * 4 + ci if ci < 4 else (16 + g if ci == 4 else 20 + g)
                    units.append((ci, tg, dst, rdst))
                    sl = slice(tg * TGW, (tg + 1) * TGW)
                    pu, rpu = self.ps[tg % 2], self.rps[tg % 2]
                    for dc in range(NDC):
                        P.add("pe", lambda e, pu=pu, sv=sv, k2=k2, dc=dc, sl=sl: e.matmul(
                            pu[:], lhsT=sv[:, k2, dc, :], rhs=hT[:, dc, sl], start=(dc == 0), stop=(dc == NDC - 1)),
                            reads=[rslot, rh[dc][tg]], writes=[rpu])
                    P.add("act", lambda e, pu=pu, tg=tg: e.copy(out=ubuf[:, 3 + tg * TGW:3 + (tg + 1) * TGW], in_=pu[:]),
                          reads=[rpu], writes=[rub[tg + 1]])
                    c3 = C_CW + cc * 4 + 3
                    P.add("act", lambda e, pu=pu, sl=sl, c3=c3, cc=cc: e.activation(
                        out=acc[:, sl], in_=pu[:], func=AF.Identity, bias=self.ccol[:, C_CB + cc:C_CB + cc + 1],
                        scale=self.ccol[:, c3:c3 + 1]), reads=[rpu] + cr, writes=[racc[tg]])
                    for kk in (2, 1, 0):
                        ck = C_CW + cc * 4 + kk
                        P.add("dve", lambda e, kk=kk, ck=ck, tg=tg, sl=sl: e.scalar_tensor_tensor(
                            out=acc[:, sl], in0=ubuf[:, kk + tg * TGW:kk + (tg + 1) * TGW], scalar=self.ccol[:, ck:ck + 1],
                            in1=acc[:, sl], op0=ALU.mult, op1=ALU.add),
                            reads=[rub[tg], rub[tg + 1], racc[tg]] + cr, writes=[racc[tg]])

                def stA2(u):
                    ci, tg, dst, rdst = units[u]
                    sl = slice(tg * TGW, (tg + 1) * TGW)
                    P.add("act", lambda e, dst=dst, sl=sl: e.activation(out=dst[:, sl], in_=acc[:, sl], func=AF.Silu),
                          reads=[racc[tg]], writes=[rdst[tg]])

                def stB(u):
                    ci, tg, dst, rdst = units[u]
                    if ci > 4:
                        return
                    k = state["tcnt"] % 2
                    state["tcnt"] += 1
                    pt, rpt = self.ps[2 + k], self.rps[2 + k]
                    for cq in range(4):
                        c = tg * 4 + cq
                        P.add("pe", lambda e, pt=pt, cq=cq, c=c, dst=dst: e.matmul(
                            pt[:, cq * 128:(cq + 1) * 128], lhsT=dst[:, c * 128:(c + 1) * 128], rhs=self.ident,
                            start=True, stop=True), reads=[rdst[tg]] + cr, writes=[rpt])
                    if ci < 4:
                        P.add("act", lambda e, pt=pt, tg=tg, ci=ci: e.copy(
                            out=xstok[:, tg * 4:(tg + 1) * 4, ci * 128:(ci + 1) * 128],
                            in_=pt[:].rearrange("p (c t) -> p c t", c=4)), reads=[rpt], writes=[rxstok[tg]])
                    else:
                        P.add("act", lambda e, pt=pt, tg=tg: e.copy(
                            out=Btok[:, tg * 4:(tg + 1) * 4, :],
                            in_=pt[:].rearrange("p (c t) -> p c t", c=4)), reads=[rpt], writes=[rBtok[tg]])

                NU, LAG = 24, 4
                for t in range(NU + LAG):
                    if t < NU:
                        stA(t)
                    if 0 <= t - 1 < NU:
                        stA2(t - 1)
                    if t - LAG >= 0:
                        stB(t - LAG)
                rBT, rCT = rBTl[0], rCTl[0]
            if self.dbg == 2:
                return
            with ExitStack() as s2:
                A = lambda name, shape, dt: s2.enter_context(SB(nc, name, shape, dt))
                yT = A("yT", [128, 4, S], BF16)
                ryT = [P.fresh("yT") for _ in range(NTG)]
                cbm = A("cbm", [128, 128], F32)
                Rb = A("Rb", [128, 1024], BF16)
                Eb = [A("Eb%d" % k, [128, 512], BF16) for k in range(2)]
                MT = [A("MT%d" % k, [128, 512], BF16) for k in range(2)]
                xdt = A("xdt", [128, 512], BF16)
                xdts = A("xdts", [128, 512], BF16)
                xsD = A("xsD", [128, 512], BF16)
                t1 = A("t1", [128, 512], F32)
                yb = A("yb", [128, 512], F32)
                sz = A("sz", [128, 512], F32)
                yg2 = [A("yg%d" % i, [128, 512], F32) for i in range(2)]
                yn2 = [A("yn%d" % i, [128, 512], BF16) for i in range(2)]
                ss2 = [A("ss%d" % i, [128, 4], F32) for i in range(2)]
                rRb2 = [P.fresh("Rb") for _ in range(2)]
                ryg2 = [P.fresh("yg") for _ in range(2)]
                ryn2 = [P.fresh("yn") for _ in range(2)]
                rss2 = [P.fresh("ss") for _ in range(2)]
                rcbm, rRb, rxdt, rxdts, rxsD, rt1, ryb, rsz, ryg, rjunk, ryn, rss = [P.fresh("s2") for _ in range(12)]
                rEb = [P.fresh("Eb") for _ in range(2)]
                rMT = [P.fresh("MT") for _ in range(2)]
                z0, rz0 = self.wget()
                z1, rz1 = self.wget()
                zv = [z0.rearrange("p (b c) -> p b c", b=NDC), z1.rearrange("p (b c) -> p b c", b=NDC)]
                rz = [rz0, rz1]
                b3 = lambda ap, n=8: ap.rearrange("p (h d) -> p h d", h=n)
                nhalf_c = P.fresh("nh")
                nh = A("nh", [128, 1], F32)
                P.add("dve", lambda e: e.memset(nh[:], -0.5), writes=[nhalf_c])
                pcb, rpcb = self.ps[2], self.rps[2]
                pyd, rpyd = self.ps[5], self.rps[5]
                pyo, rpyo = self.ps[6], self.rps[6]
                pst_, rpst_ = self.ps[0], self.rps[0]
                pz, rpz = self.ps[1], self.rps[1]
                ptr, rptr = self.ps[7], self.rps[7]

                def geo(c):
                    return slice(c * 128, (c + 1) * 128), c // 4, slice(c * 32 + g * 8, c * 32 + g * 8 + 8)

                def S0(c):
                    tsl, tg, hsl = geo(c)
                    P.add("pe", lambda e, tsl=tsl: e.matmul(pcb[:, 0:128], lhsT=BT[:, tsl], rhs=CT[:, tsl], start=True, stop=True),
                          reads=[rBT[tg], rCT[tg]], writes=[rpcb])
                    P.add("dve", lambda e: e.tensor_tensor(out=cbm[:], in0=pcb[:, 0:128], in1=self.tri_f, op=ALU.mult),
                          reads=[rpcb] + cr, writes=[rcbm])
                    for j in range(2):
                        pD, rpD = self.ps[3 + j], self.rps[3 + j]
                        P.add("pe", lambda e, pD=pD, j=j: e.matmul(pD[:], lhsT=self.U_b, rhs=Rb[:, j * 512:(j + 1) * 512], start=True, stop=True),
                              reads=[rRb2[j]] + cr, writes=[rpD])
                    for j in range(2):
                        pD, rpD = self.ps[3 + j], self.rps[3 + j]
                        P.add("act", lambda e, pD=pD, j=j: e.activation(out=Eb[j][:], in_=pD[:], func=AF.Exp),
                              reads=[rpD], writes=[rEb[j]])

                def SR(c):
                    tsl, tg, hsl = geo(c)
                    h0 = c * 32 + g * 8
                    P.add("dve", lambda e, h0=h0: e.tensor_tensor(
                        out=b3(Rb[:, 0:512], 4), in0=tab[:, 1, h0:h0 + 4].to_broadcast([128, 4, 128]),
                        in1=self.tri_f.unsqueeze(1).broadcast_to([128, 4, 128]), op=ALU.mult),
                        reads=[rtab[1]] + cr, writes=[rRb2[0]])
                    for hh in range(4, 8):
                        P.add("act", lambda e, h0=h0, hh=hh: e.activation(
                            out=Rb[:, hh * 128:(hh + 1) * 128], in_=self.tri_f, func=AF.Identity,
                            scale=tab[:, 1, h0 + hh:h0 + hh + 1]), reads=[rtab[1]] + cr, writes=[rRb2[1]])

                def S1(c):
                    tsl, tg, hsl = geo(c)
                    for j in range(2):
                        P.add("dve", lambda e, j=j: e.tensor_tensor(
                            out=b3(MT[j][:], 4), in0=b3(Eb[j][:], 4), in1=cbm[:].unsqueeze(1).broadcast_to([128, 4, 128]), op=ALU.mult),
                            reads=[rEb[j], rcbm], writes=[rMT[j]])
                    P.add("dve", lambda e, c=c, hsl=hsl: e.tensor_tensor(
                        out=b3(xdt[:]), in0=b3(xstok[:, c, :]), in1=tab[:, 0, hsl].to_broadcast([128, 8, 64]), op=ALU.mult),
                        reads=[rxstok[tg], rtab[0]], writes=[rxdt])
                    P.add("dve", lambda e, c=c, hsl=hsl: e.tensor_tensor(
                        out=b3(xdts[:]), in0=b3(xstok[:, c, :]), in1=tab[:, 3, hsl].to_broadcast([128, 8, 64]), op=ALU.mult),
                        reads=[rxstok[tg], rtab[3]], writes=[rxdts])
                    P.add("dve", lambda e, c=c: e.tensor_tensor(
                        out=b3(xsD[:]), in0=b3(xstok[:, c, :]),
                        in1=self.cbc[:, B_DSK + g * 8:B_DSK + g * 8 + 8].to_broadcast([128, 8, 64]), op=ALU.mult),
                        reads=[rxstok[tg]] + cr, writes=[rxsD])

                def S2(c):
                    tsl, tg, hsl = geo(c)
                    P.add("pe", lambda e: e.matmul(pyd[:], lhsT=self.ident, rhs=xsD[:], start=True, stop=False),
                          reads=[rxsD] + cr, writes=[rpyd])
                    for h in range(8):
                        P.add("pe", lambda e, h=h: e.matmul(
                            pyd[:, h * 64:(h + 1) * 64], lhsT=MT[h // 4][:, (h % 4) * 128:(h % 4 + 1) * 128],
                            rhs=xdt[:, h * 64:(h + 1) * 64], start=False, stop=(h == 7)),
                            reads=[rMT[h // 4], rxdt], writes=[rpyd])
                    if c > 0:
                        P.add("pe", lambda e, tsl=tsl: e.matmul(pyo[:], lhsT=CT[:, tsl], rhs=Sbf[:], start=True, stop=True),
                              reads=[rCT[tg], rSbf], writes=[rpyo])
                    P.add("pe", lambda e, c=c: e.matmul(pst_[:], lhsT=Btok[:, c, :], rhs=xdts[:], start=True, stop=True),
                          reads=[rBtok[tg], rxdts], writes=[rpst_])
                    if c == 0:
                        P.add("act", lambda e: e.copy(out=Sst[:], in_=pst_[:]), reads=[rpst_], writes=[rS])
                    else:
                        P.add("dve", lambda e, hsl=hsl: e.tensor_tensor(
                            out=b3(Sst[:]), in0=b3(Sst[:]), in1=tab[:, 4, hsl].to_broadcast([128, 8, 64]), op=ALU.mult),
                            reads=[rS, rtab[4]], writes=[rS])
                        P.add("dve", lambda e: e.tensor_tensor(out=Sst[:], in0=Sst[:], in1=pst_[:], op=ALU.add),
                              reads=[rS, rpst_], writes=[rS])
                    if c < 15:
                        P.add("act", lambda e: e.copy(out=Sbf[:], in_=Sst[:]), reads=[rS], writes=[rSbf])
                    for hf in range(2):
                        for dc in range(NDC):
                            P.add("pe", lambda e, hf=hf, dc=dc, tsl=tsl: e.matmul(
                                pz[:, hf * 256:(hf + 1) * 256], lhsT=hT[:, dc, tsl], rhs=zv[hf][:, dc, :],
                                start=(dc == 0), stop=(dc == NDC - 1)), reads=[rz[hf], rh[dc][tg]], writes=[rpz])
                    P.add("act", lambda e: e.activation(out=sz[:], in_=pz[:], func=AF.Silu), reads=[rpz], writes=[rsz])

                def S3(c):
                    tsl, tg, hsl = geo(c)
                    yg, ryg, ss, rss = yg2[c % 2], ryg2[c % 2], ss2[c % 2], rss2[c % 2]
                    if c > 0:
                        P.add("dve", lambda e, hsl=hsl: e.tensor_tensor(
                            out=b3(t1[:]), in0=b3(pyo[:]), in1=tab[:, 2, hsl].to_broadcast([128, 8, 64]), op=ALU.mult),
                            reads=[rpyo, rtab[2]], writes=[rt1])
                        P.add("dve", lambda e: e.tensor_tensor(out=yb[:], in0=t1[:], in1=pyd[:], op=ALU.add),
                              reads=[rt1, rpyd], writes=[ryb])
                        P.add("dve", lambda e, yg=yg: e.tensor_tensor(out=yg[:], in0=yb[:], in1=sz[:], op=ALU.mult),
                              reads=[ryb, rsz], writes=[ryg])
                    else:
                        P.add("dve", lambda e, yg=yg: e.tensor_tensor(out=yg[:], in0=sz[:], in1=pyd[:], op=ALU.mult),
                              reads=[rpyd, rsz], writes=[ryg])
                    P.add("act", lambda e, yg=yg, ss=ss: e.activation(out=yb[:], in_=yg[:], func=AF.Square, accum_out=ss[:, 0:1]),
                          reads=[ryg], writes=[ryb, rss])
                    P.add("pool", lambda e, ss=ss: e.tensor_scalar(out=ss[:, 1:2], in0=ss[:, 0:1], scalar1=1.0 / 512, scalar2=EPS,
                                                            op0=ALU.mult, op1=ALU.add), reads=[rss], writes=[rss])
                    P.add("pool", lambda e, ss=ss: e.tensor_tensor(out=ss[:, 2:3], in0=ss[:, 1:2], in1=nh[:], op=ALU.pow),
                          reads=[rss, nhalf_c], writes=[rss])

                def S3b(c):
                    yg, ryg, ss, rss = yg2[c % 2], ryg2[c % 2], ss2[c % 2], rss2[c % 2]
                    yn, ryn = yn2[c % 2], ryn2[c % 2]
                    P.add("dve", lambda e, yg=yg, ss=ss, yn=yn: e.scalar_tensor_tensor(
                        out=yn[:], in0=yg[:], scalar=ss[:, 2:3], in1=snw[:], op0=ALU.mult, op1=ALU.mult),
                        reads=[ryg, rss, rsnw], writes=[ryn])

                def S4(c):
                    tsl, tg, hsl = geo(c)
                    yn, ryn = yn2[c % 2], ryn2[c % 2]
                    for q in range(4):
                        P.add("pe", lambda e, q=q, yn=yn: e.matmul(
                            ptr[:, q * 128:(q + 1) * 128], lhsT=yn[:, q * 128:(q + 1) * 128], rhs=self.ident,
                            start=True, stop=True), reads=[ryn] + cr, writes=[rptr])
                    P.add("act", lambda e, tsl=tsl: e.copy(out=yT[:, :, tsl], in_=ptr[:].rearrange("p (c t) -> p c t", c=4)),
                          reads=[rptr], writes=[ryT[tg]])

                order = [(SR, 0), (S3, 3), (S3b, 4), (S4, 5), (S2, 2), (S1, 1), (S0, 0)]
                for t in range(16 + 5):
                    for fn, lag in order:
                        c = t - lag
                        if 0 <= c < 16:
                            fn(c)
                cnt = 0
                oblk = []
                for hf in range(2):
                    slot, rslot = self.wget(keep=hf)
                    oblk.append((slot.rearrange("p (k c) -> p k c", k=4), rslot))
                for tg in range(NTG):
                    sl = slice(tg * TGW, (tg + 1) * TGW)
                    for dc in range(NDC):
                        ov, rslot = oblk[dc // 4]
                        d4 = dc % 4
                        pd, rpd = self.ps[3 + cnt % 2], self.rps[3 + cnt % 2]
                        cnt += 1
                        for kq in range(4):
                            P.add("pe", lambda e, pd=pd, ov=ov, kq=kq, d4=d4, sl=sl: e.matmul(
                                pd[:], lhsT=ov[:, kq, d4 * 128:(d4 + 1) * 128], rhs=yT[:, kq, sl],
                                start=(kq == 0), stop=(kq == 3)), reads=[rslot, ryT[tg]], writes=[rpd])
                        P.add("dve", lambda e, pd=pd, dc=dc, sl=sl: e.tensor_tensor(
                            out=self.xT[:, dc, sl], in0=pd[:], in1=self.xT[:, dc, sl], op=ALU.add),
                            reads=[rpd, self.rx[dc][tg]], writes=[self.rx[dc][tg]])

    def build(self, final_norm=True):
        P, nc = self.P, self.nc
        self.kc = self.st.enter_context(SB(nc, "kc", [128, 2], F32))
        self.epsc = self.kc
        reps = Res("eps")
        P.add("dve", lambda e: e.memset(self.kc[:, 0:1], EPS), writes=[reps])
        P.add("dve", lambda e: e.memset(self.kc[:, 1:2], 1.0), writes=[reps])
        self.rcs.append(reps)
        for p in self.phases:
            if p == "ffn0a":
                self.emit_ffn(0)
            elif p == "ffn0b":
                self.emit_ffn(2)
            elif p == "ffn1a":
                self.emit_ffn(3)
            elif p == "ffn1b":
                self.emit_ffn(5)
            elif p == "mamba":
                self.emit_mamba()
            elif p == "kv":
                self.emit_kv()
            elif p == "attn":
                self.emit_attn()
        self.emit_final(do_norm=("final" in self.phases))
        assert getattr(self, "dbg", 0) or self.wi == n_blocks(self.phases), (self.wi, n_blocks(self.phases))
        P.emit(self.st)
        self.st.close()
        return nc


def _pad_block(a):
    out = np.zeros((128, 2048), np.float32)
    a = np.ascontiguousarray(a).reshape(128, -1)
    out[:, :a.shape[1]] = a
    return out


def _colchunks(W, starts):
    outs = []
    for s in starts:
        idx = np.arange(s, s + 128) if np.isscalar(s) else s
        outs.append(W[:, idx].reshape(8, 128, 128).transpose(1, 0, 2))
    return np.stack(outs, axis=1)


def _rowblock(W, row0, nrow_chunks, col0, ncols):
    return W[row0 * 128:(row0 + nrow_chunks) * 128, col0:col0 + ncols].reshape(nrow_chunks, 128, ncols).transpose(1, 0, 2)


PERM64 = np.concatenate([np.arange(32, 64), np.arange(0, 32)])
PERM128 = np.concatenate([PERM64, 64 + PERM64])


def pack_ffn(Wg, Wu, Wd):
    blocks = []
    f0 = 0
    for npass in PASSES:
        for jj in range(npass):
            j = f0 + jj
            g = _colchunks(Wg, [j * 128])[:, 0]
            u = _colchunks(Wu, [j * 128])[:, 0]
            blocks.append(_pad_block(np.stack([g, u], axis=1)))
        for b in range(4):
            blocks.append(_pad_block(_rowblock(Wd, f0, npass, b * 256, 256)))
        f0 += npass
    return blocks


def pack_weights(inp, phases):
    blocks = []
    for p in phases:
        if p.startswith("ffn"):
            l, i = {"ffn0a": (0, 0), "ffn0b": (0, 1), "ffn1a": (1, 0), "ffn1b": (1, 1)}[p]
            blocks += pack_ffn(inp["ffn_w_gate"][l, i], inp["ffn_w_up"][l, i], inp["ffn_w_down"][l, i])
        elif p == "mamba":
            Win = inp["ssm_w_in"][0]
            Wout = inp["ssm_w_out"][0]
            blocks.append(_pad_block(Win[:, 5120:5152].reshape(8, 128, 32).transpose(1, 0, 2)))
            for g in range(4):
                xs0 = 2048 + g * 512
                blocks.append(_pad_block(_colchunks(Win, [xs0, xs0 + 128])))
                blocks.append(_pad_block(_colchunks(Win, [xs0 + 256, xs0 + 384])))
                blocks.append(_pad_block(_colchunks(Win, [2048 + 2048 + g * 128, 2048 + 2560 + g * 128])))
                for hf in range(2):
                    z0 = g * 512 + hf * 256
                    blocks.append(_pad_block(Win[:, z0:z0 + 256].reshape(8, 128, 256).transpose(1, 0, 2)))
                for hf in range(2):
                    blocks.append(_pad_block(_rowblock(Wout, g * 4, 4, hf * 512, 512)))
        elif p == "kv":
            Wk, Wv = inp["w_k"], inp["w_v"]
            for kh in range(4):
                pl = kh * 64 + np.concatenate([np.arange(64), np.arange(64)])
                sw = kh * 64 + np.concatenate([PERM64, PERM64])
                blocks.append(_pad_block(_colchunks(Wk, [pl, sw])))
            blocks.append(_pad_block(Wv.reshape(8, 128, 256).transpose(1, 0, 2)))
        elif p == "attn":
            Wq, Wo = inp["attn_w_q"][0], inp["attn_w_o"][0]
            for c in range(8):
                blocks.append(_pad_block(_colchunks(Wq, [c * 128 + np.arange(128), c * 128 + PERM128])))
            for b in range(4):
                blocks.append(_pad_block(_rowblock(Wo, 0, 8, b * 256, 256)))
    if not blocks:
        blocks.append(np.zeros((128, 2048), np.float32))
    return np.stack(blocks, axis=0)


def pack_consts(inp):
    cc = np.zeros((128, NCC), np.float32)
    vecs = [inp["norm_w"][l, i] for l in range(2) for i in range(3)] + [inp["kv_norm_w"], inp["final_norm_w"]]
    for v, w in enumerate(vecs):
        cc[:, C_NW + v * 8:C_NW + v * 8 + 8] = w.reshape(8, 128).T
    cw = inp["ssm_conv_w"][0]
    cc[:, C_CW:C_CW + 96] = cw.reshape(4, 24, 128).transpose(2, 1, 0).reshape(128, 96)
    cc[:, C_CB:C_CB + 24] = inp["ssm_conv_b"][0].reshape(24, 128).T
    bq = inp["attn_b_q"][0]
    for c in range(8):
        cc[:, C_BQ + c * 2] = bq[c * 128 + np.arange(128)]
        cc[:, C_BQ + c * 2 + 1] = bq[c * 128 + PERM128]
    bk = inp["b_k"]
    for kh in range(4):
        cc[:, C_BK + kh * 2] = bk[kh * 64 + np.concatenate([np.arange(64), np.arange(64)])]
        cc[:, C_BK + kh * 2 + 1] = bk[kh * 64 + np.concatenate([PERM64, PERM64])]
    cc[:, C_BO:C_BO + 8] = inp["attn_b_o"][0].reshape(8, 128).T
    cb = np.zeros((128, NCB), np.float32)
    cb[:, B_DTB:B_DTB + 32] = inp["ssm_dt_bias"][0][None, :]
    cb[:, B_ALOG:B_ALOG + 32] = inp["ssm_a_log"][0][None, :]
    cb[:, B_DSK:B_DSK + 32] = inp["ssm_d"][0][None, :]
    cb[:, B_BV:B_BV + 256] = inp["b_v"][None, :]
    cb[:, B_SINK:B_SINK + 16] = inp["attn_sinks"][0][None, :]
    cb[:, B_SNW:B_SNW + 2048] = inp["ssm_norm_w"][0][None, :]
    return cc, cb


def const_tables():
    i = np.arange(128)
    ident = np.eye(128, dtype=np.float32)
    tri = (i[:, None] <= i[None, :]).astype(np.float32)
    U = (i[:, None] > i[None, :]).astype(np.float32)
    ones = np.ones((128, 128), np.float32)
    cmat = np.concatenate([ident, tri, U, ones], axis=1)
    q = i[:, None]
    j = np.arange(256)[None, :]
    valid = (j <= q + 128) & (j > q)
    maskb = np.where(valid, 0.0, -30000.0).astype(np.float32)
    pos = np.arange(S, dtype=np.float64)
    inv = 1.0 / (10000.0 ** (np.arange(0, 64, 2, dtype=np.float64) / 64.0))
    ang = pos[:, None] * inv[None, :]
    cos, sin = np.cos(ang).astype(np.float32), np.sin(ang).astype(np.float32)
    fi = (i % 64) % 32
    sign = np.where((i % 64) < 32, -1.0, 1.0).astype(np.float32)
    rope = np.stack([cos[:, fi].T, sin[:, fi].T * sign[:, None]], axis=1).astype(np.float32)
    return cmat, maskb, np.ascontiguousarray(rope)


_CACHE = {}


def run(inputs, phases, cores):
    key = tuple(phases)
    kb = KB(list(phases))
    nc = kb.build()
    wblk = pack_weights(inputs, phases)
    cc, cb = pack_consts(inputs)
    cmat, maskb, rope = const_tables()
    x = np.asarray(inputs["x"], np.float32)
    in_maps = []
    for b in cores:
        in_maps.append({"xT": np.ascontiguousarray(x[b].T), "wblk": wblk, "ccol": cc, "cbc": cb, "cmat": cmat,
                        "maskb": maskb, "rope": rope})
    res = run_bass_kernel_spmd(nc, in_maps, core_ids=list(range(len(cores))))
    outs = [np.ascontiguousarray(np.asarray(r["outT"]).T) for r in res.results]
    return np.stack(outs, axis=0)


def kernel(**inputs):
    inputs = {k: np.asarray(v) for k, v in inputs.items()}
    return run(inputs, PHASES, list(range(8))).astype(np.float32)
```

```python
import numpy as np
from contextlib import ExitStack
import concourse.bass as bass
import concourse.mybir as mybir
from concourse.bass_utils import run_bass_kernel_spmd

F32 = mybir.dt.float32
BF16 = mybir.dt.bfloat16
AF = mybir.ActivationFunctionType
ALU = mybir.AluOpType

ENGS = ["pe", "act", "dve", "pool", "sp"]
BLOCK_ATTR = {"pe": "tensor", "act": "scalar", "dve": "vector", "pool": "gpsimd", "sp": "sync"}
COMPUTE = ("pe", "act", "dve")


class Res:
    __slots__ = ("name", "w", "rs", "dsem", "dcount")

    def __init__(self, name):
        self.name = name
        self.w = None
        self.rs = []
        self.dsem = None
        self.dcount = 0


class Op:
    __slots__ = ("eng", "fn", "deps", "idx", "is_dma", "sig", "dres", "sigval", "sem", "final")


class Prog:
    def __init__(self, nc):
        self.nc = nc
        self.ops = {e: [] for e in ENGS}
        self.dma_res = []

    def fresh(self, name):
        r = Res(name)
        r.rs = [self.ops[e][-1] for e in COMPUTE if self.ops[e]]
        return r

    def add(self, eng, fn, reads=(), writes=(), dma=False):
        op = Op()
        op.eng = eng
        op.fn = fn
        op.is_dma = dma
        op.idx = len(self.ops[eng])
        op.sig = False
        op.sigval = 0
        op.sem = None
        op.dres = None
        op.final = False
        deps = []
        for r in reads:
            if r.w is not None:
                deps.append(r.w)
        for w in writes:
            if w.w is not None:
                deps.append(w.w)
            for x in w.rs:
                deps.append(x)
        best = {}
        out = []
        for d in deps:
            if d is op:
                continue
            if d.is_dma:
                out.append(d)
                continue
            if d.eng == "pe" and eng == "pe" and not dma:
                continue
            if d.eng not in best or best[d.eng].idx < d.idx:
                best[d.eng] = d
        op.deps = out + list(best.values())
        for r in reads:
            if dma:
                r.rs.append(op)
            else:
                r.rs = [x for x in r.rs if x.is_dma or x.eng != eng] + [op]
        for w in writes:
            w.w = op
            w.rs = []
        if dma:
            assert len(writes) == 1
            op.dres = writes[0]
            if op.dres.dsem is None:
                op.dres.dsem = True
                self.dma_res.append(op.dres)
        self.ops[eng].append(op)
        return op

    def emit(self, stack):
        nc = self.nc
        for e in ENGS:
            for op in self.ops[e]:
                for d in op.deps:
                    d.sig = True
        esem = {e: stack.enter_context(nc.semaphore("s_" + e)) for e in ENGS}
        for i, r in enumerate(self.dma_res):
            r.dsem = stack.enter_context(nc.semaphore("d%d" % i))
            r.dcount = 0
        for e in ENGS:
            cnt = 0
            for op in self.ops[e]:
                if op.is_dma:
                    r = op.dres
                    r.dcount += 16
                    op.sem = r.dsem
                    op.sigval = r.dcount
                elif op.sig:
                    cnt += 1
                    op.sem = esem[e]
                    op.sigval = cnt
        block = stack.enter_context(nc.Block())
        for e in ENGS:
            if self.ops[e]:
                self._emit_engine(block, e)

    def _emit_engine(self, block, e):
        ops = self.ops[e]
        deco = getattr(block, BLOCK_ATTR[e])

        @deco
        def _(eng):
            waited = {}
            for op in ops:
                need = {}
                for d in op.deps:
                    key = id(d.sem)
                    if need.get(key, (None, 0))[1] < d.sigval:
                        need[key] = (d.sem, d.sigval)
                for key, (sem, val) in need.items():
                    if waited.get(key, 0) >= val:
                        continue
                    eng.wait_ge(sem, val)
                    waited[key] = val
                ins = op.fn(eng)
                if op.is_dma:
                    ins.then_inc(op.sem, 16)
                elif op.sig:
                    ins.then_inc(op.sem, 1)
            for op in ops:
                if op.is_dma and op.final:
                    eng.wait_ge(op.sem, op.sigval)


_SBN = [0]


def SB(nc, name, shape, dt):
    _SBN[0] += 1
    return nc.sbuf_tensor("%s_%d" % (name, _SBN[0]), shape, dt)


D = 1024
S = 2048
NDC = 8
NTG = 4
TGW = 512
DFF = 2816
NFC = 22
PASSES = [6, 6, 5, 5]
EPS = 1e-5
NH_SSM = 32
RING = 6
LOOK = 4

C_NW = 0
C_CW = 64
C_CB = C_CW + 96
C_BQ = C_CB + 24
C_BK = C_BQ + 16
C_BO = C_BK + 8
NCC = C_BO + 8
B_DTB = 0
B_ALOG = 32
B_DSK = 64
B_BV = 96
B_SINK = 352
B_SMALL = 368
B_SNW = 368
NCB = B_SNW + 2048
NCM = 512

PHASES = ["ffn0a", "mamba", "ffn0b", "kv", "ffn1a", "attn", "ffn1b", "final"]


def n_blocks(phases):
    n = 0
    for p in phases:
        if p.startswith("ffn"):
            n += NFC + 4 * len(PASSES)
        elif p == "mamba":
            n += 1 + 4 * 7
        elif p == "kv":
            n += 5
        elif p == "attn":
            n += 12
    return n


class KB:
    def __init__(self, phases, debug_out=None):
        self.phases = phases
        self.nblk = max(1, n_blocks(phases))
        nc = bass.Bass("TRN2", target_bir_lowering=False)
        self.nc = nc
        self.P = Prog(nc)
        self.st = ExitStack()
        st = self.st
        self.xT_d = nc.dram_tensor("xT", [D, S], F32, kind="ExternalInput").ap()
        self.wb_d = nc.dram_tensor("wblk", [self.nblk, 128, 2048], F32, kind="ExternalInput").ap()
        self.cc_d = nc.dram_tensor("ccol", [128, NCC], F32, kind="ExternalInput").ap()
        self.cb_d = nc.dram_tensor("cbc", [128, NCB], F32, kind="ExternalInput").ap()
        self.cm_d = nc.dram_tensor("cmat", [128, NCM], F32, kind="ExternalInput").ap()
        self.mk_d = nc.dram_tensor("maskb", [128, 256], F32, kind="ExternalInput").ap()
        self.rope_d = nc.dram_tensor("rope", [128, 2, S], F32, kind="ExternalInput").ap()
        self.out_d = nc.dram_tensor("outT", [D, S], F32, kind="ExternalOutput").ap()
        P = self.P
        sb = lambda name, shape, dt: st.enter_context(SB(nc, name, shape, dt))
        self.xT = sb("xT_sb", [128, NDC, S], F32)
        self.rx = [[Res("x%d_%d" % (dc, tg)) for tg in range(NTG)] for dc in range(NDC)]
        self.ring = sb("ring", [128, RING, 2048], BF16)
        self.rring = [Res("ring%d" % i) for i in range(RING)]
        self.ccol = sb("ccol_sb", [128, NCC], F32)
        self.cbc = sb("cbc_sb", [128, B_SMALL], F32)
        self.cmb = sb("cmat_bf", [128, NCM], BF16)
        self.cmf = sb("cmat_f", [128, NCM], F32)
        self.maskb = sb("maskb_sb", [128, 256], F32)
        self.rconst = Res("const")
        self.ps = [st.enter_context(nc.psum_tensor("ps%d" % i, [128, 512], F32)) for i in range(8)]
        self.rps = [Res("ps%d" % i) for i in range(8)]
        self.wi = 0
        self.wissued = 0
        rc = [Res("c%d" % i) for i in range(6)]
        P.add("sp", lambda e: e.dma_start(out=self.ccol[:], in_=self.cc_d), writes=[rc[0]], dma=True)
        P.add("sp", lambda e: e.dma_start(out=self.cbc[:], in_=self.cb_d[:, 0:B_SMALL]), writes=[rc[1]], dma=True)
        P.add("sp", lambda e: e.dma_start(out=self.cmf[:], in_=self.cm_d), writes=[rc[2]], dma=True)
        P.add("sp", lambda e: e.dma_start(out=self.maskb[:], in_=self.mk_d), writes=[rc[3]], dma=True)
        P.add("pool", lambda e: e.dma_start(out=self.cmb[:], in_=self.cm_d), writes=[rc[4]], dma=True)
        self.rcs = rc[:5]
        xv = self.xT_d.rearrange("(c p) t -> p c t", p=128)
        for tg in range(NTG):
            for dc in range(NDC):
                sl = slice(tg * TGW, (tg + 1) * TGW)
                P.add("sp", lambda e, dc=dc, sl=sl: e.dma_start(out=self.xT[:, dc, sl], in_=xv[:, dc, sl]),
                      writes=[self.rx[dc][tg]], dma=True)
        self.ident = self.cmb[:, 0:128]
        self.tri_b = self.cmb[:, 128:256]
        self.U_b = self.cmb[:, 256:384]
        self.ones_b = self.cmb[:, 384:512]
        self.tri_f = self.cmf[:, 128:256]
        self.ones_f = self.cmf[:, 384:512]
        self.kT = None

    def wget(self, keep=0):
        P = self.P
        i = self.wi
        self.wi += 1
        assert i < self.nblk, "weight stream overrun"
        lim = min(self.nblk, i + 1 + LOOK, i - keep + RING)
        while self.wissued < lim:
            k = self.wissued
            s = k % RING
            P.add("pool", lambda e, k=k, s=s: e.dma_start(out=self.ring[:, s, :], in_=self.wb_d[k]),
                  writes=[self.rring[s]], dma=True)
            self.wissued += 1
        s = i % RING
        return self.ring[:, s, :], self.rring[s]

    def creads(self):
        return list(self.rcs)

    def emit_norm(self, ph, widx, out_t, rout, final=False):
        P, nc = self.P, self.nc
        sq = [ph.enter_context(SB(nc, "nsq%d" % k, [128, TGW], BF16)) for k in range(2)]
        rsq = [P.fresh("nsq%d" % k) for k in range(2)]
        sd = [ph.enter_context(SB(nc, "nsd%d" % k, [128, TGW], F32)) for k in range(2)]
        rsd = [P.fresh("nsd%d" % k) for k in range(2)]
        rstd = [ph.enter_context(SB(nc, "nrstd%d" % k, [128, TGW], F32)) for k in range(2)]
        rrstd = [P.fresh("nrstd%d" % k) for k in range(2)]
        psn, rpsn = self.ps[6], self.rps[6]
        cr = self.creads()
        for tg in range(NTG):
            sl = slice(tg * TGW, (tg + 1) * TGW)
            k2 = tg % 2
            for dc in range(NDC):
                k = dc % 2
                P.add("act", lambda e, k=k, dc=dc, sl=sl: e.activation(out=sq[k][:], in_=self.xT[:, dc, sl], func=AF.Square),
                      reads=[self.rx[dc][tg]], writes=[rsq[k]])
                P.add("pe", lambda e, k=k, dc=dc: e.matmul(psn[:], lhsT=self.ones_b, rhs=sq[k][:], start=(dc == 0), stop=(dc == NDC - 1)),
                      reads=[rsq[k]] + cr, writes=[rpsn])
            P.add("act", lambda e, k2=k2: e.activation(out=sd[k2][:], in_=psn[:], func=AF.Ln, bias=self.epsc[:, 0:1], scale=1.0 / D),
                  reads=[rpsn] + cr, writes=[rsd[k2]])
            P.add("act", lambda e, k2=k2: e.activation(out=rstd[k2][:], in_=sd[k2][:], func=AF.Exp, scale=-0.5),
                  reads=[rsd[k2]], writes=[rrstd[k2]])
            for dc in range(NDC):
                col = C_NW + widx * 8 + dc
                P.add("dve", lambda e, dc=dc, sl=sl, col=col, k2=k2: e.scalar_tensor_tensor(
                    out=out_t[:, dc, sl], in0=self.xT[:, dc, sl], scalar=self.ccol[:, col:col + 1], in1=rstd[k2][:],
                    op0=ALU.mult, op1=ALU.mult),
                    reads=[self.rx[dc][tg], rrstd[k2]] + cr, writes=[rout[dc][tg]])

    def emit_ffn(self, widx):
        P, nc = self.P, self.nc
        with ExitStack() as ph:
            hT = ph.enter_context(SB(nc, "hT", [128, NDC, S], BF16))
            rh = [[P.fresh("h") for _ in range(NTG)] for _ in range(NDC)]
            with ExitStack() as phn:
                self.emit_norm(phn, widx, hT, rh)
            npmax = max(PASSES)
            aT = ph.enter_context(SB(nc, "aT", [128, npmax, S], BF16))
            ra = [[P.fresh("a") for _ in range(NTG)] for _ in range(npmax)]
            sg = [ph.enter_context(SB(nc, "sg%d" % k, [128, TGW], F32)) for k in range(2)]
            rsg = [P.fresh("sg") for _ in range(2)]
            cnt = 0
            for npass_i, npass in enumerate(PASSES):
                for jj in range(npass):
                    slot, rslot = self.wget()
                    sv = slot.rearrange("p (a b c) -> p a b c", a=2, b=NDC)
                    for tg in range(NTG):
                        sl = slice(tg * TGW, (tg + 1) * TGW)
                        k = cnt % 2
                        cnt += 1
                        pg, rpg = self.ps[k], self.rps[k]
                        pu, rpu = self.ps[2 + k], self.rps[2 + k]
                        for which, (pp, rpp) in enumerate(((pg, rpg), (pu, rpu))):
                            for dc in range(NDC):
                                P.add("pe", lambda e, pp=pp, which=which, dc=dc, sl=sl, sv=sv: e.matmul(
                                    pp[:], lhsT=sv[:, which, dc, :], rhs=hT[:, dc, sl], start=(dc == 0), stop=(dc == NDC - 1)),
                                    reads=[rslot, rh[dc][tg]], writes=[rpp])
                        P.add("act", lambda e, k=k, pg=pg: e.activation(out=sg[k][:], in_=pg[:], func=AF.Silu),
                              reads=[rpg], writes=[rsg[k]])
                        P.add("dve", lambda e, k=k, pu=pu, jj=jj, sl=sl: e.tensor_tensor(
                            out=aT[:, jj, sl], in0=sg[k][:], in1=pu[:], op=ALU.mult),
                            reads=[rsg[k], rpu], writes=[ra[jj][tg]])
                last = (npass_i == len(PASSES) - 1)
                if last:
                    blks = []
                    for b in range(4):
                        slot, rslot = self.wget(keep=b)
                        blks.append((slot[:, 0:npass * 256].rearrange("p (j c) -> p j c", j=npass), rslot))
                    for tg in range(NTG):
                        sl = slice(tg * TGW, (tg + 1) * TGW)
                        for dc in range(NDC):
                            sv, rslot = blks[dc // 2]
                            d2 = dc % 2
                            k = cnt % 2
                            cnt += 1
                            pd, rpd = self.ps[4 + k], self.rps[4 + k]
                            for jj in range(npass):
                                P.add("pe", lambda e, pd=pd, sv=sv, jj=jj, d2=d2, sl=sl, npass=npass: e.matmul(
                                    pd[:], lhsT=sv[:, jj, d2 * 128:(d2 + 1) * 128], rhs=aT[:, jj, sl],
                                    start=(jj == 0), stop=(jj == npass - 1)),
                                    reads=[rslot, ra[jj][tg]], writes=[rpd])
                            P.add("dve", lambda e, pd=pd, dc=dc, sl=sl: e.scalar_tensor_tensor(
                                out=self.xT[:, dc, sl], in0=pd[:], scalar=0.5, in1=self.xT[:, dc, sl],
                                op0=ALU.mult, op1=ALU.add),
                                reads=[rpd, self.rx[dc][tg]], writes=[self.rx[dc][tg]])
                for b in range(0 if last else 4):
                    slot, rslot = self.wget()
                    sv = slot[:, 0:npass * 256].rearrange("p (j c) -> p j c", j=npass)
                    for d2 in range(2):
                        dc = b * 2 + d2
                        for tg in range(NTG):
                            sl = slice(tg * TGW, (tg + 1) * TGW)
                            k = cnt % 2
                            cnt += 1
                            pd, rpd = self.ps[4 + k], self.rps[4 + k]
                            for jj in range(npass):
                                P.add("pe", lambda e, pd=pd, sv=sv, jj=jj, d2=d2, sl=sl, npass=npass: e.matmul(
                                    pd[:], lhsT=sv[:, jj, d2 * 128:(d2 + 1) * 128], rhs=aT[:, jj, sl],
                                    start=(jj == 0), stop=(jj == npass - 1)),
                                    reads=[rslot, ra[jj][tg]], writes=[rpd])
                            P.add("dve", lambda e, pd=pd, dc=dc, sl=sl: e.scalar_tensor_tensor(
                                out=self.xT[:, dc, sl], in0=pd[:], scalar=0.5, in1=self.xT[:, dc, sl],
                                op0=ALU.mult, op1=ALU.add),
                                reads=[rpd, self.rx[dc][tg]], writes=[self.rx[dc][tg]])

    def emit_final(self, do_norm=True):
        P, nc = self.P, self.nc
        ov = self.out_d.rearrange("(c p) t -> p c t", p=128)
        with ExitStack() as ph:
            if do_norm:
                oT = ph.enter_context(SB(nc, "oT", [128, NDC, S], F32))
                ro = [[P.fresh("o") for _ in range(NTG)] for _ in range(NDC)]
                self.emit_norm(ph, 7, oT, ro)
            else:
                oT, ro = self.xT, self.rx
            for tg in range(NTG):
                for dc in range(NDC):
                    sl = slice(tg * TGW, (tg + 1) * TGW)
                    o = P.add("sp", lambda e, dc=dc, sl=sl: e.dma_start(out=ov[:, dc, sl], in_=oT[:, dc, sl]),
                              reads=[ro[dc][tg]], writes=[Res("out")], dma=True)
                    o.final = True

    def emit_rope_proj(self, ph, hT, rh, nchunks, bias_col0, out_t, rout, ropeb, rrope, tmp, rtmp, cnt0=0):
        P = self.P
        cr = self.creads()
        cnt = cnt0
        for c in range(nchunks):
            slot, rslot = self.wget()
            sv = slot.rearrange("p (a b c) -> p a b c", a=2, b=NDC)
            for tg in range(NTG):
                sl = slice(tg * TGW, (tg + 1) * TGW)
                k = cnt % 2
                cnt += 1
                pq, rpq = self.ps[k], self.rps[k]
                pqs, rpqs = self.ps[2 + k], self.rps[2 + k]
                P.add("sp", lambda e, k=k, sl=sl: e.dma_start(out=ropeb[k][:], in_=self.rope_d[:, :, sl]),
                      writes=[rrope[k]], dma=True)
                for which, (pp, rpp) in enumerate(((pq, rpq), (pqs, rpqs))):
                    for dc in range(NDC):
                        P.add("pe", lambda e, pp=pp, which=which, dc=dc, sl=sl, sv=sv: e.matmul(
                            pp[:], lhsT=sv[:, which, dc, :], rhs=hT[:, dc, sl], start=(dc == 0), stop=(dc == NDC - 1)),
                            reads=[rslot, rh[dc][tg]], writes=[rpp])
                b0 = bias_col0 + c * 2
                P.add("dve", lambda e, k=k, pq=pq, b0=b0: e.scalar_tensor_tensor(
                    out=tmp[2 * k][:], in0=pq[:], scalar=self.ccol[:, b0:b0 + 1], in1=ropeb[k][:, 0, :],
                    op0=ALU.add, op1=ALU.mult), reads=[rpq, rrope[k]] + cr, writes=[rtmp[2 * k]])
                P.add("dve", lambda e, k=k, pqs=pqs, b0=b0: e.scalar_tensor_tensor(
                    out=tmp[2 * k + 1][:], in0=pqs[:], scalar=self.ccol[:, b0 + 1:b0 + 2], in1=ropeb[k][:, 1, :],
                    op0=ALU.add, op1=ALU.mult), reads=[rpqs, rrope[k]] + cr, writes=[rtmp[2 * k + 1]])
                P.add("dve", lambda e, k=k, c=c, sl=sl: e.tensor_tensor(
                    out=out_t[:, c, sl], in0=tmp[2 * k][:], in1=tmp[2 * k + 1][:], op=ALU.add),
                    reads=[rtmp[2 * k], rtmp[2 * k + 1]], writes=[rout[c][tg]])
        return cnt

    def _rope_bufs(self, ph):
        P, nc = self.P, self.nc
        ropeb = [ph.enter_context(SB(nc, "ropeb%d" % k, [128, 2, TGW], F32)) for k in range(2)]
        rrope = [P.fresh("ropeb") for _ in range(2)]
        tmp = [ph.enter_context(SB(nc, "rtmp%d" % k, [128, TGW], F32)) for k in range(4)]
        rtmp = [P.fresh("rtmp") for _ in range(4)]
        return ropeb, rrope, tmp, rtmp

    def emit_kv(self):
        P, nc = self.P, self.nc
        st = self.st
        self.kT = st.enter_context(SB(nc, "kT", [128, 4, S], BF16))
        self.rk = [[P.fresh("k") for _ in range(NTG)] for _ in range(4)]
        self.vtok = st.enter_context(SB(nc, "vtok", [128, 16, 256], BF16))
        self.rv = [P.fresh("v") for _ in range(16)]
        cr = self.creads()
        with ExitStack() as ph:
            hT = ph.enter_context(SB(nc, "hT", [128, NDC, S], BF16))
            rh = [[P.fresh("h") for _ in range(NTG)] for _ in range(NDC)]
            with ExitStack() as phn:
                self.emit_norm(phn, 6, hT, rh)
            ropeb, rrope, tmp, rtmp = self._rope_bufs(ph)
            self.emit_rope_proj(ph, hT, rh, 4, C_BK, self.kT, self.rk, ropeb, rrope, tmp, rtmp)
            slot, rslot = self.wget()
            sv = slot.rearrange("p (b c) -> p b c", b=NDC)
            for n in range(16):
                k = n % 2
                pv, rpv = self.ps[4 + k], self.rps[4 + k]
                for dc in range(NDC):
                    P.add("pe", lambda e, pv=pv, dc=dc, n=n: e.matmul(
                        pv[:, 0:256], lhsT=hT[:, dc, n * 128:(n + 1) * 128], rhs=sv[:, dc, :],
                        start=(dc == 0), stop=(dc == NDC - 1)), reads=[rslot, rh[dc][n // 4]], writes=[rpv])
                P.add("dve", lambda e, pv=pv, n=n: e.tensor_tensor(
                    out=self.vtok[:, n, :], in0=pv[:, 0:256], in1=self.cbc[:, B_BV:B_BV + 256], op=ALU.add),
                    reads=[rpv] + cr, writes=[self.rv[n]])

    def emit_attn(self):
        P, nc = self.P, self.nc
        cr = self.creads()
        with ExitStack() as ph:
            qT = ph.enter_context(SB(nc, "qT", [128, NDC, S], BF16))
            rq = [[P.fresh("q") for _ in range(NTG)] for _ in range(NDC)]
            with ExitStack() as ph2:
                hT = ph2.enter_context(SB(nc, "hT", [128, NDC, S], BF16))
                rh = [[P.fresh("h") for _ in range(NTG)] for _ in range(NDC)]
                with ExitStack() as phn:
                    self.emit_norm(phn, 4, hT, rh)
                ropeb, rrope, tmp, rtmp = self._rope_bufs(ph2)
                self.emit_rope_proj(ph2, hT, rh, 8, C_BQ, qT, rq, ropeb, rrope, tmp, rtmp)
            aTt = ph.enter_context(SB(nc, "attnT", [128, NDC, S], BF16))
            raT = [[P.fresh("aT") for _ in range(NTG)] for _ in range(NDC)]
            sm = [ph.enter_context(SB(nc, "sm%d" % k, [128, 4, 256], F32)) for k in range(2)]
            rsm = [P.fresh("sm") for _ in range(2)]
            pb = [ph.enter_context(SB(nc, "pb%d" % k, [128, 4, 256], BF16)) for k in range(2)]
            rpb = [P.fresh("pb") for _ in range(2)]
            ptb = [ph.enter_context(SB(nc, "ptb%d" % k, [128, 8, 128], BF16)) for k in range(2)]
            rptb = [P.fresh("ptb") for _ in range(2)]
            stat2 = [ph.enter_context(SB(nc, "stat%d" % i, [128, 6, 16], F32)) for i in range(2)]
            rstat2 = [[P.fresh("stat%d" % i) for i in range(6)] for _ in range(2)]
            atok = ph.enter_context(SB(nc, "atok", [128, 1024], BF16))
            ratok = P.fresh("atok")
            po = [self.ps[4], self.ps[5]]
            rpo = [self.rps[4], self.rps[5]]
            X = mybir.AxisListType.X

            sinkmax = ph.enter_context(SB(nc, "sinkmax", [128, 4], F32))
            rsinkmax = P.fresh("sinkmax")
            P.add("dve", lambda e: e.tensor_reduce(
                out=sinkmax[:], in_=self.cbc[:, B_SINK:B_SINK + 16].rearrange("p (j h) -> p j h", j=4), axis=X, op=ALU.max),
                reads=cr, writes=[rsinkmax])

            def geom(n):
                nk = 128 if n == 0 else 256
                return nk, nk // 128, (0 if n == 0 else (n - 1) * 128), (128 if n == 0 else 0)

            def stA1(g):
                n, j = g // 4, g % 4
                k = g % 2
                nk, nhalf, k0, mcol = geom(n)
                stat, rstat = stat2[n % 2], rstat2[n % 2]
                kh = j
                kreads = [self.rk[kh][n // 4]] + ([self.rk[kh][(n - 1) // 4]] if n > 0 else [])
                for hh in range(4):
                    h = 4 * j + hh
                    c, base = h // 2, 64 * (h % 2)
                    pS, rpS = self.ps[2 * k + hh % 2], self.rps[2 * k + hh % 2]
                    P.add("pe", lambda e, pS=pS, base=base, c=c, n=n, kh=kh, k0=k0, nk=nk, hh=hh: e.matmul(
                        pS[:, (hh // 2) * 256:(hh // 2) * 256 + nk], lhsT=qT[base:base + 64, c, n * 128:(n + 1) * 128],
                        rhs=self.kT[base:base + 64, kh, k0:k0 + nk], start=True, stop=True),
                        reads=[rq[c][n // 4]] + kreads, writes=[rpS])
                for b2 in range(2):
                    pS, rpS = self.ps[2 * k + b2], self.rps[2 * k + b2]
                    P.add("dve", lambda e, k=k, pS=pS, nk=nk, mcol=mcol, b2=b2: e.scalar_tensor_tensor(
                        out=sm[k][:, 2 * b2:2 * b2 + 2, 0:nk], in0=pS[:].rearrange("p (h s) -> p h s", h=2)[:, :, 0:nk],
                        scalar=0.125, in1=self.maskb[:, mcol:mcol + nk].unsqueeze(1).broadcast_to([128, 2, nk]),
                        op0=ALU.mult, op1=ALU.add), reads=[rpS] + cr, writes=[rsm[k]])
                P.add("dve", lambda e, k=k, nk=nk, j=j, stat=stat: e.tensor_reduce(
                    out=stat[:, 0, j:j + 1], in_=sm[k][:, :, 0:nk], axis=mybir.AxisListType.XY, op=ALU.max),
                    reads=[rsm[k]], writes=[rstat[0]])
                P.add("dve", lambda e, j=j, stat=stat: e.tensor_tensor(
                    out=stat[:, 0, j:j + 1], in0=stat[:, 0, j:j + 1], in1=sinkmax[:, j:j + 1], op=ALU.max),
                    reads=[rstat[0], rsinkmax], writes=[rstat[0]])
                P.add("dve", lambda e, j=j, stat=stat: e.tensor_scalar(
                    out=stat[:, 1, j:j + 1], in0=stat[:, 0, j:j + 1], scalar1=-1.0, scalar2=None, op0=ALU.mult),
                    reads=[rstat[0]], writes=[rstat[1]])

            def stA2(g):
                n, j = g // 4, g % 4
                k = g % 2
                nk, nhalf, k0, mcol = geom(n)
                stat, rstat = stat2[n % 2], rstat2[n % 2]
                P.add("act", lambda e, k=k, nk=nk, j=j, stat=stat: e.activation(
                    out=pb[k][:, :, 0:nk], in_=sm[k][:, :, 0:nk], func=AF.Exp, bias=stat[:, 1, j:j + 1], scale=1.0),
                    reads=[rsm[k], rstat[1]], writes=[rpb[k]])

            def stB1(g):
                n, j = g // 4, g % 4
                k = g % 2
                nk, nhalf, k0, mcol = geom(n)
                stat, rstat = stat2[n % 2], rstat2[n % 2]
                P.add("dve", lambda e, k=k, nk=nk, j=j, stat=stat: e.tensor_reduce(
                    out=stat[:, 2, 4 * j:4 * j + 4].rearrange("p (a b) -> p b a", a=2),
                    in_=pb[k][:, :, 0:nk].rearrange("p (b a) s -> p b a s", b=2), axis=X, op=ALU.add),
                    reads=[rpb[k]], writes=[rstat[2]])
                if j == 3:
                    P.add("dve", lambda e, stat=stat: e.tensor_tensor(
                        out=stat[:, 3, :].rearrange("p (j h) -> p j h", j=4),
                        in0=self.cbc[:, B_SINK:B_SINK + 16].rearrange("p (j h) -> p j h", j=4),
                        in1=stat[:, 0, 0:4].to_broadcast([128, 4, 4]), op=ALU.subtract),
                        reads=[rstat[0]] + cr, writes=[rstat[3]])
                    P.add("act", lambda e, stat=stat: e.activation(out=stat[:, 3, :], in_=stat[:, 3, :], func=AF.Exp), reads=[rstat[3]], writes=[rstat[3]])
                    P.add("dve", lambda e, stat=stat: e.tensor_tensor(out=stat[:, 4, :], in0=stat[:, 3, :], in1=stat[:, 2, :], op=ALU.add),
                          reads=[rstat[3], rstat[2]], writes=[rstat[4]])
                    P.add("dve", lambda e, stat=stat: e.reciprocal(out=stat[:, 5, :], in_=stat[:, 4, :]), reads=[rstat[4]], writes=[rstat[5]])

                for hh in range(4):
                    pt, rpt = self.ps[6 + hh // 2], self.rps[6 + hh // 2]
                    for hf in range(nhalf):
                        slot_i = (hh % 2) * 2 + hf
                        P.add("pe", lambda e, pt=pt, k=k, hh=hh, hf=hf, slot_i=slot_i: e.matmul(
                            pt[:, slot_i * 128:(slot_i + 1) * 128], lhsT=pb[k][:, (hh % 2) * 2 + hh // 2, hf * 128:(hf + 1) * 128], rhs=self.ident,
                            start=True, stop=True), reads=[rpb[k]] + cr, writes=[rpt])
                for b2 in range(2):
                    pt, rpt = self.ps[6 + b2], self.rps[6 + b2]
                    for a2 in range(2):
                        P.add("act", lambda e, k=k, pt=pt, b2=b2, nhalf=nhalf, a2=a2: e.copy(
                            out=ptb[k][:, 4 * b2 + 2 * a2:4 * b2 + 2 * a2 + nhalf, :],
                            in_=pt[:, a2 * 256:a2 * 256 + nhalf * 128].rearrange("p (f t) -> p f t", f=nhalf)),
                            reads=[rpt], writes=[rptb[k]])

            def stB2(g):
                n, j = g // 4, g % 4
                k = g % 2
                nk, nhalf, k0, mcol = geom(n)
                stat, rstat = stat2[n % 2], rstat2[n % 2]
                kh = j
                for hh in range(4):
                    h = 4 * j + hh
                    ob = h // 8
                    oc = (h % 8) * 64
                    for hf in range(nhalf):
                        nb = n if nhalf == 1 else (n - 1 + hf)
                        si = hh * 2 + hf
                        P.add("pe", lambda e, ob=ob, oc=oc, k=k, hf=hf, nb=nb, kh=kh, nhalf=nhalf, si=si: e.matmul(
                            po[ob][:, oc:oc + 64], lhsT=ptb[k][:, si, :],
                            rhs=self.vtok[:, nb, kh * 64:(kh + 1) * 64], start=(hf == 0), stop=(hf == nhalf - 1)),
                            reads=[rptb[k], self.rv[nb]], writes=[rpo[ob]])
                if j == 3:
                    for ob in range(2):
                        P.add("dve", lambda e, ob=ob, stat=stat: e.tensor_tensor(
                            out=atok[:, ob * 512:(ob + 1) * 512].rearrange("p (h d) -> p h d", h=8),
                            in0=po[ob][:].rearrange("p (h d) -> p h d", h=8),
                            in1=stat[:, 5, ob * 8:(ob + 1) * 8].to_broadcast([128, 8, 64]), op=ALU.mult),
                            reads=[rpo[ob], rstat[5]], writes=[ratok])
                    for half in range(2):
                        pt, rpt = self.ps[6 + half], self.rps[6 + half]
                        for c4 in range(4):
                            c = half * 4 + c4
                            P.add("pe", lambda e, pt=pt, c4=c4, c=c: e.matmul(
                                pt[:, c4 * 128:(c4 + 1) * 128], lhsT=atok[:, c * 128:(c + 1) * 128], rhs=self.ident,
                                start=True, stop=True), reads=[ratok] + cr, writes=[rpt])
                        P.add("act", lambda e, pt=pt, half=half, n=n: e.copy(
                            out=aTt[:, half * 4:(half + 1) * 4, n * 128:(n + 1) * 128],
                            in_=pt[:].rearrange("p (c t) -> p c t", c=4)),
                            reads=[rpt], writes=[raT[half * 4 + c4][n // 4] for c4 in range(4)])

            stages = [stA1, stA2, stB1, stB2]
            NG = 64
            for t in range(NG + len(stages) - 1):
                for si_ in reversed(range(len(stages))):
                    g = t - si_
                    if 0 <= g < NG:
                        stages[si_](g)
            cnt = 0
            blks = []
            for b in range(4):
                slot, rslot = self.wget(keep=b)
                blks.append((slot.rearrange("p (c k) -> p c k", c=NDC), rslot))
            for tg in range(NTG):
                sl = slice(tg * TGW, (tg + 1) * TGW)
                for dc in range(NDC):
                    sv, rslot = blks[dc // 2]
                    d2 = dc % 2
                    k = cnt % 2
                    cnt += 1
                    pd, rpd = self.ps[2 + k], self.rps[2 + k]
                    for c in range(NDC):
                        P.add("pe", lambda e, pd=pd, sv=sv, c=c, d2=d2, sl=sl: e.matmul(
                            pd[:], lhsT=sv[:, c, d2 * 128:(d2 + 1) * 128], rhs=aTt[:, c, sl],
                            start=(c == 0), stop=(c == NDC - 1)), reads=[rslot, raT[c][tg]], writes=[rpd])
                    P.add("dve", lambda e, pd=pd, dc=dc, sl=sl: e.scalar_tensor_tensor(
                        out=self.xT[:, dc, sl], in0=pd[:], scalar=self.ccol[:, C_BO + dc:C_BO + dc + 1], in1=self.xT[:, dc, sl],
                        op0=ALU.add, op1=ALU.add), reads=[rpd, self.rx[dc][tg]] + cr, writes=[self.rx[dc][tg]])

    def emit_mamba(self):
        P, nc = self.P, self.nc
        cr = self.creads()
        X = mybir.AxisListType.X
        with ExitStack() as ph:
            hT = ph.enter_context(SB(nc, "hT", [128, NDC, S], BF16))
            rh = [[P.fresh("h") for _ in range(NTG)] for _ in range(NDC)]
            with ExitStack() as phn:
                self.emit_norm(phn, 1, hT, rh)
            tab = ph.enter_context(SB(nc, "tab", [128, 5, 512], F32))
            rtab = [P.fresh("tab%d" % i) for i in range(5)]
            with ExitStack() as p0:
                tA = [p0.enter_context(SB(nc, "tA%d" % k, [128, 512], F32)) for k in range(2)]
                rtA = [P.fresh("tA") for _ in range(2)]
                eA = p0.enter_context(SB(nc, "eA", [128, 32], F32))
                reA = P.fresh("eA")
                slot, rslot = self.wget()
                sv = slot[:, 0:256].rearrange("p (b h) -> p b h", b=NDC)
                pdt, rpdt = self.ps[0], self.rps[0]
                for c in range(16):
                    for dc in range(NDC):
                        P.add("pe", lambda e, c=c, dc=dc, sv=sv: e.matmul(
                            pdt[:, c * 32:(c + 1) * 32], lhsT=hT[:, dc, c * 128:(c + 1) * 128], rhs=sv[:, dc, :],
                            start=(dc == 0), stop=(dc == NDC - 1)), reads=[rslot, rh[dc][c // 4]], writes=[rpdt])
                v3 = lambda ap: ap.rearrange("p (c h) -> p c h", c=16)
                P.add("dve", lambda e: e.tensor_tensor(
                    out=v3(tA[0][:]), in0=v3(pdt[:]), in1=self.cbc[:, B_DTB:B_DTB + 32].unsqueeze(1).broadcast_to([128, 16, 32]),
                    op=ALU.add), reads=[rpdt] + cr, writes=[rtA[0]])
                P.add("act", lambda e: e.activation(out=tA[0][:], in_=tA[0][:], func=AF.Exp), reads=[rtA[0]], writes=[rtA[0]])
                P.add("act", lambda e: e.activation(out=tab[:, 0, :], in_=tA[0][:], func=AF.Ln, bias=self.kc[:, 1:2], scale=1.0),
                      reads=[rtA[0]] + cr, writes=[rtab[0]])
                P.add("act", lambda e: e.activation(out=eA[:], in_=self.cbc[:, B_ALOG:B_ALOG + 32], func=AF.Exp),
                      reads=cr, writes=[reA])
                P.add("dve", lambda e: e.scalar_tensor_tensor(
                    out=v3(tab[:, 1, :]), in0=v3(tab[:, 0, :]), scalar=-1.0, in1=eA[:].unsqueeze(1).broadcast_to([128, 16, 32]),
                    op0=ALU.mult, op1=ALU.mult), reads=[rtab[0], reA], writes=[rtab[1]])
                pacs, rpacs = self.ps[1], self.rps[1]
                pal, rpal = self.ps[2], self.rps[2]
                ahl = [p0.enter_context(SB(nc, "ahl%d" % k, [128, 512], BF16)) for k in range(2)]
                rahl = [P.fresh("ahl") for _ in range(2)]
                P.add("dve", lambda e: e.tensor_copy(out=ahl[0][:], in_=tab[:, 1, :]), reads=[rtab[1]], writes=[rahl[0]])
                P.add("dve", lambda e: e.tensor_tensor(out=ahl[1][:], in0=tab[:, 1, :], in1=ahl[0][:], op=ALU.subtract),
                      reads=[rtab[1], rahl[0]], writes=[rahl[1]])
                for k in range(2):
                    P.add("pe", lambda e, k=k: e.matmul(pacs[:], lhsT=self.tri_b, rhs=ahl[k][:], start=(k == 0), stop=(k == 1)),
                          reads=[rahl[k]] + cr, writes=[rpacs])
                for k in range(2):
                    P.add("pe", lambda e, k=k: e.matmul(pal[:], lhsT=self.ones_b, rhs=ahl[k][:], start=(k == 0), stop=(k == 1)),
                          reads=[rahl[k]] + cr, writes=[rpal])
                P.add("act", lambda e: e.activation(out=tab[:, 2, :], in_=pacs[:], func=AF.Exp), reads=[rpacs], writes=[rtab[2]])
                P.add("act", lambda e: e.activation(out=tab[:, 4, :], in_=pal[:], func=AF.Exp), reads=[rpal], writes=[rtab[4]])
                P.add("act", lambda e: e.copy(out=tA[1][:], in_=pacs[:]), reads=[rpacs], writes=[rtA[1]])
                P.add("dve", lambda e: e.tensor_tensor(out=tA[0][:], in0=pal[:], in1=tA[1][:], op=ALU.subtract),
                      reads=[rpal, rtA[1], rtA[0]], writes=[rtA[0]])
                P.add("act", lambda e: e.activation(out=tA[0][:], in_=tA[0][:], func=AF.Exp), reads=[rtA[0]], writes=[rtA[0]])
                P.add("dve", lambda e: e.tensor_tensor(out=tab[:, 3, :], in0=tab[:, 0, :], in1=tA[0][:], op=ALU.mult),
                      reads=[rtab[0], rtA[0]], writes=[rtab[3]])
            self.dbg = 0
            if self.dbg == 1:
                return
            for g in range(4):
                self._mamba_group(g, hT, rh, tab, rtab, cr)
                if self.dbg >= 2:
                    return

    def _mamba_group(self, g, hT, rh, tab, rtab, cr):
        P, nc = self.P, self.nc
        with ExitStack() as pg:
            BT = pg.enter_context(SB(nc, "BT", [128, S], BF16))
            CT = pg.enter_context(SB(nc, "CT", [128, S], BF16))
            Btok = pg.enter_context(SB(nc, "Btok", [128, 16, 128], BF16))
            xstok = pg.enter_context(SB(nc, "xstok", [128, 16, 512], BF16))
            snw = pg.enter_context(SB(nc, "snw", [128, 512], F32))
            Sst = pg.enter_context(SB(nc, "Sst", [128, 512], F32))
            Sbf = pg.enter_context(SB(nc, "Sbf", [128, 512], BF16))
            rBT, rCT, rS, rSbf, rsnw = P.fresh("BT"), P.fresh("CT"), P.fresh("S"), P.fresh("Sbf"), P.fresh("snw")
            rBtok = [P.fresh("Btok") for _ in range(4)]
            rxstok = [P.fresh("xstok") for _ in range(4)]
            P.add("sp", lambda e, g=g: e.dma_start(out=snw[:], in_=self.cb_d[:, B_SNW + g * 512:B_SNW + (g + 1) * 512]),
                  writes=[rsnw], dma=True)
            tcnt = 0
            with ExitStack() as s1:
                ubuf = s1.enter_context(SB(nc, "ubuf", [128, S + 4], F32))
                acc = s1.enter_context(SB(nc, "acc", [128, S], F32))
                xsT = s1.enter_context(SB(nc, "xsTc", [128, S], BF16))
                rub = [P.fresh("ubuf") for _ in range(NTG + 1)]
                racc = [P.fresh("acc") for _ in range(NTG)]
                P.add("dve", lambda e: e.memset(ubuf[:, 0:3], 0.0), writes=[rub[0]])
                xsT2 = [xsT, s1.enter_context(SB(nc, "xsTd", [128, S], BF16))]
                units = []
                state = {"ccnt": 0, "tcnt": 0}
                rBTl, rCTl = [None], [None]

                def stA(u):
                    ci, tg = u // 4, u % 4
                    bi, k2 = ci // 2, ci % 2
                    if tg == 0:
                        if k2 == 0:
                            slot, rslot = self.wget()
                            state["sv"] = slot.rearrange("p (k b c) -> p k b c", k=2, b=NDC)
                            state["rslot"] = rslot
                        if ci < 4:
                            state["dst"] = xsT2[state["ccnt"] % 2]
                            state["ccnt"] += 1
                        else:
                            state["dst"] = BT if ci == 4 else CT
                        state["rdst"] = [P.fresh("dst") for _ in range(NTG)]
                        if ci == 4:
                            rBTl[0] = state["rdst"]
                        elif ci == 5:
                            rCTl[0] = state["rdst"]
                    sv, rslot, dst, rdst = state["sv"], state["rslot"], state["dst"], state["rdst"]
                    cc = g * 4 + ci if ci < 4 else (16 + g if ci == 4 else 20 + g)
                    units.append((ci, tg, dst, rdst))
                    sl = slice(tg * TGW, (tg + 1) * TGW)
                    pu, rpu = self.ps[tg % 2], self.rps[tg % 2]
                    for dc in range(NDC):
                        P.add("pe", lambda e, pu=pu, sv=sv, k2=k2, dc=dc, sl=sl: e.matmul(
                            pu[:], lhsT=sv[:, k2, dc, :], rhs=hT[:, dc, sl], start=(dc == 0), stop=(dc == NDC - 1)),
                            reads=[rslot, rh[dc][tg]], writes=[rpu])
                    P.add("act", lambda e, pu=pu, tg=tg: e.copy(out=ubuf[:, 3 + tg * TGW:3 + (tg + 1) * TGW], in_=pu[:]),
                          reads=[rpu], writes=[rub[tg + 1]])
                    c3 = C_CW + cc * 4 + 3
                    P.add("act", lambda e, pu=pu, sl=sl, c3=c3, cc=cc: e.activation(
                        out=acc[:, sl], in_=pu[:], func=AF.Identity, bias=self.ccol[:, C_CB + cc:C_CB + cc + 1],
                        scale=self.ccol[:, c3:c3 + 1]), reads=[rpu] + cr, writes=[racc[tg]])
                    for kk in (2, 1, 0):
                        ck = C_CW + cc * 4 + kk
                        P.add("dve", lambda e, kk=kk, ck=ck, tg=tg, sl=sl: e.scalar_tensor_tensor(
                            out=acc[:, sl], in0=ubuf[:, kk + tg * TGW:kk + (tg + 1) * TGW], scalar=self.ccol[:, ck:ck + 1],
                            in1=acc[:, sl], op0=ALU.mult, op1=ALU.add),
                            reads=[rub[tg], rub[tg + 1], racc[tg]] + cr, writes=[racc[tg]])

                def stA2(u):
                    ci, tg, dst, rdst = units[u]
                    sl = slice(tg * TGW, (tg + 1) * TGW)
                    P.add("act", lambda e, dst=dst, sl=sl: e.activation(out=dst[:, sl], in_=acc[:, sl], func=AF.Silu),
                          reads=[racc[tg]], writes=[rdst[tg]])

                def stB(u):
                    ci, tg, dst, rdst = units[u]
                    if ci > 4:
                        return
                    k = state["tcnt"] % 2
                    state["tcnt"] += 1
                    pt, rpt = self.ps[2 + k], self.rps[2 + k]
                    for cq in range(4):
                        c = tg * 4 + cq
                        P.add("pe", lambda e, pt=pt, cq=cq, c=c, dst=dst: e.matmul(
                            pt[:, cq * 128:(cq + 1) * 128], lhsT=dst[:, c * 128:(c + 1) * 128], rhs=self.ident,
                            start=True, stop=True), reads=[rdst[tg]] + cr, writes=[rpt])
                    if ci < 4:
                        P.add("act", lambda e, pt=pt, tg=tg, ci=ci: e.copy(
                            out=xstok[:, tg * 4:(tg + 1) * 4, ci * 128:(ci + 1) * 128],
                            in_=pt[:].rearrange("p (c t) -> p c t", c=4)), reads=[rpt], writes=[rxstok[tg]])
                    else:
                        P.add("act", lambda e, pt=pt, tg=tg: e.copy(
                            out=Btok[:, tg * 4:(tg + 1) * 4, :],
                            in_=pt[:].rearrange("p (c t) -> p c t", c=4)), reads=[rpt], writes=[rBtok[tg]])

                NU, LAG = 24, 4
                for t in range(NU + LAG):
                    if t < NU:
                        stA(t)
                    if 0 <= t - 1 < NU:
                        stA2(t - 1)
                    if t - LAG >= 0:
                        stB(t - LAG)
                rBT, rCT = rBTl[0], rCTl[0]
            if self.dbg == 2:
                return
            with ExitStack() as s2:
                A = lambda name, shape, dt: s2.enter_context(SB(nc, name, shape, dt))
                yT = A("yT", [128, 4, S], BF16)
                ryT = [P.fresh("yT") for _ in range(NTG)]
                cbm = A("cbm", [128, 128], F32)
                Rb = A("Rb", [128, 1024], BF16)
                Eb = [A("Eb%d" % k, [128, 512], BF16) for k in range(2)]
                MT = [A("MT%d" % k, [128, 512], BF16) for k in range(2)]
                xdt = A("xdt", [128, 512], BF16)
                xdts = A("xdts", [128, 512], BF16)
                identD = A("identD", [128, 8, 128], BF16)
                ridentD = P.fresh("identD")
                for h in range(8):
                    P.add("dve", lambda e, h=h: e.tensor_scalar(
                        out=identD[:, h, :], in0=self.ident, scalar1=self.cbc[:, B_DSK + g * 8 + h:B_DSK + g * 8 + h + 1],
                        scalar2=None, op0=ALU.mult), reads=cr, writes=[ridentD])
                yb = A("yb", [128, 512], F32)
                sz = A("sz", [128, 512], F32)
                yg2 = [A("yg%d" % i, [128, 512], F32) for i in range(2)]
                yn2 = [A("yn%d" % i, [128, 512], BF16) for i in range(2)]
                ss2 = [A("ss%d" % i, [128, 4], F32) for i in range(2)]
                rRb2 = [P.fresh("Rb") for _ in range(2)]
                ryg2 = [P.fresh("yg") for _ in range(2)]
                ryn2 = [P.fresh("yn") for _ in range(2)]
                rss2 = [P.fresh("ss") for _ in range(2)]
                rcbm, rRb, rxdt, rxdts, rxsD, rt1, ryb, rsz, ryg, rjunk, ryn, rss = [P.fresh("s2") for _ in range(12)]
                rEb = [P.fresh("Eb") for _ in range(2)]
                rMT = [P.fresh("MT") for _ in range(2)]
                z0, rz0 = self.wget()
                z1, rz1 = self.wget()
                zv = [z0.rearrange("p (b c) -> p b c", b=NDC), z1.rearrange("p (b c) -> p b c", b=NDC)]
                rz = [rz0, rz1]
                b3 = lambda ap, n=8: ap.rearrange("p (h d) -> p h d", h=n)
                nhalf_c = P.fresh("nh")
                nh = A("nh", [128, 1], F32)
                P.add("dve", lambda e: e.memset(nh[:], -0.5), writes=[nhalf_c])
                pcb, rpcb = self.ps[2], self.rps[2]
                pyd, rpyd = self.ps[5], self.rps[5]
                pyo, rpyo = self.ps[6], self.rps[6]
                pst_, rpst_ = self.ps[0], self.rps[0]
                pz, rpz = self.ps[1], self.rps[1]
                ptr, rptr = self.ps[7], self.rps[7]

                def geo(c):
                    return slice(c * 128, (c + 1) * 128), c // 4, slice(c * 32 + g * 8, c * 32 + g * 8 + 8)

                def S0(c):
                    tsl, tg, hsl = geo(c)
                    P.add("pe", lambda e, tsl=tsl: e.matmul(pcb[:, 0:128], lhsT=BT[:, tsl], rhs=CT[:, tsl], start=True, stop=True),
                          reads=[rBT[tg], rCT[tg]], writes=[rpcb])
                    P.add("dve", lambda e: e.tensor_tensor(out=cbm[:], in0=pcb[:, 0:128], in1=self.tri_f, op=ALU.mult),
                          reads=[rpcb] + cr, writes=[rcbm])
                    for j in range(2):
                        pD, rpD = self.ps[3 + j], self.rps[3 + j]
                        P.add("pe", lambda e, pD=pD, j=j: e.matmul(pD[:], lhsT=self.U_b, rhs=Rb[:, j * 512:(j + 1) * 512], start=True, stop=True),
                              reads=[rRb2[j]] + cr, writes=[rpD])
                    for j in range(2):
                        pD, rpD = self.ps[3 + j], self.rps[3 + j]
                        P.add("act", lambda e, pD=pD, j=j: e.activation(out=Eb[j][:], in_=pD[:], func=AF.Exp),
                              reads=[rpD], writes=[rEb[j]])

                def SR(c):
                    tsl, tg, hsl = geo(c)
                    h0 = c * 32 + g * 8
                    P.add("dve", lambda e, h0=h0: e.tensor_tensor(
                        out=b3(Rb[:, 0:512], 4), in0=tab[:, 1, h0:h0 + 4].to_broadcast([128, 4, 128]),
                        in1=self.tri_f.unsqueeze(1).broadcast_to([128, 4, 128]), op=ALU.mult),
                        reads=[rtab[1]] + cr, writes=[rRb2[0]])
                    for hh in range(4, 8):
                        P.add("act", lambda e, h0=h0, hh=hh: e.activation(
                            out=Rb[:, hh * 128:(hh + 1) * 128], in_=self.tri_f, func=AF.Identity,
                            scale=tab[:, 1, h0 + hh:h0 + hh + 1]), reads=[rtab[1]] + cr, writes=[rRb2[1]])

                def S1(c):
                    tsl, tg, hsl = geo(c)
                    for j in range(2):
                        P.add("dve", lambda e, j=j: e.tensor_tensor(
                            out=b3(MT[j][:], 4), in0=b3(Eb[j][:], 4), in1=cbm[:].unsqueeze(1).broadcast_to([128, 4, 128]), op=ALU.mult),
                            reads=[rEb[j], rcbm], writes=[rMT[j]])
                    P.add("dve", lambda e, c=c, hsl=hsl: e.tensor_tensor(
                        out=b3(xdt[:]), in0=b3(xstok[:, c, :]), in1=tab[:, 0, hsl].to_broadcast([128, 8, 64]), op=ALU.mult),
                        reads=[rxstok[tg], rtab[0]], writes=[rxdt])
                    P.add("dve", lambda e, c=c, hsl=hsl: e.tensor_tensor(
                        out=b3(xdts[:]), in0=b3(xstok[:, c, :]), in1=tab[:, 3, hsl].to_broadcast([128, 8, 64]), op=ALU.mult),
                        reads=[rxstok[tg], rtab[3]], writes=[rxdts])

                def S2(c):
                    tsl, tg, hsl = geo(c)
                    for h in range(8):
                        P.add("pe", lambda e, h=h, c=c: e.matmul(
                            pyd[:, h * 64:(h + 1) * 64], lhsT=identD[:, h, :],
                            rhs=xstok[:, c, h * 64:(h + 1) * 64], start=True, stop=False),
                            reads=[ridentD, rxstok[tg]], writes=[rpyd])
                        P.add("pe", lambda e, h=h: e.matmul(
                            pyd[:, h * 64:(h + 1) * 64], lhsT=MT[h // 4][:, (h % 4) * 128:(h % 4 + 1) * 128],
                            rhs=xdt[:, h * 64:(h + 1) * 64], start=False, stop=True),
                            reads=[rMT[h // 4], rxdt], writes=[rpyd])
                    if c > 0:
                        P.add("pe", lambda e, tsl=tsl: e.matmul(pyo[:], lhsT=CT[:, tsl], rhs=Sbf[:], start=True, stop=True),
                              reads=[rCT[tg], rSbf], writes=[rpyo])
                    P.add("pe", lambda e, c=c: e.matmul(pst_[:], lhsT=Btok[:, c, :], rhs=xdts[:], start=True, stop=True),
                          reads=[rBtok[tg], rxdts], writes=[rpst_])
                    if c == 0:
                        P.add("act", lambda e: e.copy(out=Sst[:], in_=pst_[:]), reads=[rpst_], writes=[rS])
                    else:
                        P.add("dve", lambda e, hsl=hsl: e.tensor_tensor(
                            out=b3(Sst[:]), in0=b3(Sst[:]), in1=tab[:, 4, hsl].to_broadcast([128, 8, 64]), op=ALU.mult),
                            reads=[rS, rtab[4]], writes=[rS])
                        P.add("dve", lambda e: e.tensor_tensor(out=Sst[:], in0=Sst[:], in1=pst_[:], op=ALU.add),
                              reads=[rS, rpst_], writes=[rS])
                    if c < 15:
                        P.add("act", lambda e: e.copy(out=Sbf[:], in_=Sst[:]), reads=[rS], writes=[rSbf])
                    for hf in range(2):
                        for dc in range(NDC):
                            P.add("pe", lambda e, hf=hf, dc=dc, tsl=tsl: e.matmul(
                                pz[:, hf * 256:(hf + 1) * 256], lhsT=hT[:, dc, tsl], rhs=zv[hf][:, dc, :],
                                start=(dc == 0), stop=(dc == NDC - 1)), reads=[rz[hf], rh[dc][tg]], writes=[rpz])
                    P.add("act", lambda e: e.activation(out=sz[:], in_=pz[:], func=AF.Silu), reads=[rpz], writes=[rsz])

                def S3(c):
                    tsl, tg, hsl = geo(c)
                    yg, ryg, ss, rss = yg2[c % 2], ryg2[c % 2], ss2[c % 2], rss2[c % 2]
                    if c > 0:
                        P.add("dve", lambda e, hsl=hsl: e.tensor_tensor(
                            out=b3(yb[:]), in0=b3(pyo[:]), in1=tab[:, 2, hsl].to_broadcast([128, 8, 64]), op=ALU.mult),
                            reads=[rpyo, rtab[2]], writes=[ryb])
                        P.add("dve", lambda e: e.tensor_tensor(out=yb[:], in0=yb[:], in1=pyd[:], op=ALU.add),
                              reads=[ryb, rpyd], writes=[ryb])
                        P.add("dve", lambda e, yg=yg: e.tensor_tensor(out=yg[:], in0=yb[:], in1=sz[:], op=ALU.mult),
                              reads=[ryb, rsz], writes=[ryg])
                    else:
                        P.add("dve", lambda e, yg=yg: e.tensor_tensor(out=yg[:], in0=sz[:], in1=pyd[:], op=ALU.mult),
                              reads=[rpyd, rsz], writes=[ryg])
                    P.add("act", lambda e, yg=yg, ss=ss: e.activation(out=yb[:], in_=yg[:], func=AF.Square, accum_out=ss[:, 0:1]),
                          reads=[ryg], writes=[ryb, rss])
                    P.add("pool", lambda e, ss=ss: e.tensor_scalar(out=ss[:, 1:2], in0=ss[:, 0:1], scalar1=1.0 / 512, scalar2=EPS,
                                                            op0=ALU.mult, op1=ALU.add), reads=[rss], writes=[rss])
                    P.add("pool", lambda e, ss=ss: e.tensor_tensor(out=ss[:, 2:3], in0=ss[:, 1:2], in1=nh[:], op=ALU.pow),
                          reads=[rss, nhalf_c], writes=[rss])

                def S3b(c):
                    yg, ryg, ss, rss = yg2[c % 2], ryg2[c % 2], ss2[c % 2], rss2[c % 2]
                    yn, ryn = yn2[c % 2], ryn2[c % 2]
                    P.add("dve", lambda e, yg=yg, ss=ss, yn=yn: e.scalar_tensor_tensor(
                        out=yn[:], in0=yg[:], scalar=ss[:, 2:3], in1=snw[:], op0=ALU.mult, op1=ALU.mult),
                        reads=[ryg, rss, rsnw], writes=[ryn])

                def S4(c):
                    tsl, tg, hsl = geo(c)
                    yn, ryn = yn2[c % 2], ryn2[c % 2]
                    for q in range(4):
                        P.add("pe", lambda e, q=q, yn=yn: e.matmul(
                            ptr[:, q * 128:(q + 1) * 128], lhsT=yn[:, q * 128:(q + 1) * 128], rhs=self.ident,
                            start=True, stop=True), reads=[ryn] + cr, writes=[rptr])
                    P.add("act", lambda e, tsl=tsl: e.copy(out=yT[:, :, tsl], in_=ptr[:].rearrange("p (c t) -> p c t", c=4)),
                          reads=[rptr], writes=[ryT[tg]])

                order = [(SR, 0), (S3, 3), (S3b, 4), (S4, 5), (S2, 2), (S1, 1), (S0, 0)]
                for t in range(16 + 5):
                    for fn, lag in order:
                        c = t - lag
                        if 0 <= c < 16:
                            fn(c)
                cnt = 0
                oblk = []
                for hf in range(2):
                    slot, rslot = self.wget(keep=hf)
                    oblk.append((slot.rearrange("p (k c) -> p k c", k=4), rslot))
                for tg in range(NTG):
                    sl = slice(tg * TGW, (tg + 1) * TGW)
                    for dc in range(NDC):
                        ov, rslot = oblk[dc // 4]
                        d4 = dc % 4
                        pd, rpd = self.ps[3 + cnt % 2], self.rps[3 + cnt % 2]
                        cnt += 1
                        for kq in range(4):
                            P.add("pe", lambda e, pd=pd, ov=ov, kq=kq, d4=d4, sl=sl: e.matmul(
                                pd[:], lhsT=ov[:, kq, d4 * 128:(d4 + 1) * 128], rhs=yT[:, kq, sl],
                                start=(kq == 0), stop=(kq == 3)), reads=[rslot, ryT[tg]], writes=[rpd])
                        P.add("dve", lambda e, pd=pd, dc=dc, sl=sl: e.tensor_tensor(
                            out=self.xT[:, dc, sl], in0=pd[:], in1=self.xT[:, dc, sl], op=ALU.add),
                            reads=[rpd, self.rx[dc][tg]], writes=[self.rx[dc][tg]])

    def build(self, final_norm=True):
        P, nc = self.P, self.nc
        self.kc = self.st.enter_context(SB(nc, "kc", [128, 2], F32))
        self.epsc = self.kc
        reps = Res("eps")
        P.add("dve", lambda e: e.memset(self.kc[:, 0:1], EPS), writes=[reps])
        P.add("dve", lambda e: e.memset(self.kc[:, 1:2], 1.0), writes=[reps])
        self.rcs.append(reps)
        for p in self.phases:
            if p == "ffn0a":
                self.emit_ffn(0)
            elif p == "ffn0b":
                self.emit_ffn(2)
            elif p == "ffn1a":
                self.emit_ffn(3)
            elif p == "ffn1b":
                self.emit_ffn(5)
            elif p == "mamba":
                self.emit_mamba()
            elif p == "kv":
                self.emit_kv()
            elif p == "attn":
                self.emit_attn()
        self.emit_final(do_norm=("final" in self.phases))
        assert getattr(self, "dbg", 0) or self.wi == n_blocks(self.phases), (self.wi, n_blocks(self.phases))
        P.emit(self.st)
        self.st.close()
        return nc


def _pad_block(a):
    out = np.zeros((128, 2048), np.float32)
    a = np.ascontiguousarray(a).reshape(128, -1)
    out[:, :a.shape[1]] = a
    return out


def _colchunks(W, starts):
    outs = []
    for s in starts:
        idx = np.arange(s, s + 128) if np.isscalar(s) else s
        outs.append(W[:, idx].reshape(8, 128, 128).transpose(1, 0, 2))
    return np.stack(outs, axis=1)


def _rowblock(W, row0, nrow_chunks, col0, ncols):
    return W[row0 * 128:(row0 + nrow_chunks) * 128, col0:col0 + ncols].reshape(nrow_chunks, 128, ncols).transpose(1, 0, 2)


PERM64 = np.concatenate([np.arange(32, 64), np.arange(0, 32)])
PERM128 = np.concatenate([PERM64, 64 + PERM64])


def pack_ffn(Wg, Wu, Wd):
    blocks = []
    f0 = 0
    for npass in PASSES:
        for jj in range(npass):
            j = f0 + jj
            g = _colchunks(Wg, [j * 128])[:, 0]
            u = _colchunks(Wu, [j * 128])[:, 0]
            blocks.append(_pad_block(np.stack([g, u], axis=1)))
        for b in range(4):
            blocks.append(_pad_block(_rowblock(Wd, f0, npass, b * 256, 256)))
        f0 += npass
    return blocks


def pack_weights(inp, phases):
    blocks = []
    for p in phases:
        if p.startswith("ffn"):
            l, i = {"ffn0a": (0, 0), "ffn0b": (0, 1), "ffn1a": (1, 0), "ffn1b": (1, 1)}[p]
            blocks += pack_ffn(inp["ffn_w_gate"][l, i], inp["ffn_w_up"][l, i], inp["ffn_w_down"][l, i])
        elif p == "mamba":
            Win = inp["ssm_w_in"][0]
            Wout = inp["ssm_w_out"][0]
            blocks.append(_pad_block(Win[:, 5120:5152].reshape(8, 128, 32).transpose(1, 0, 2)))
            for g in range(4):
                xs0 = 2048 + g * 512
                blocks.append(_pad_block(_colchunks(Win, [xs0, xs0 + 128])))
                blocks.append(_pad_block(_colchunks(Win, [xs0 + 256, xs0 + 384])))
                blocks.append(_pad_block(_colchunks(Win, [2048 + 2048 + g * 128, 2048 + 2560 + g * 128])))
                for hf in range(2):
                    z0 = g * 512 + hf * 256
                    blocks.append(_pad_block(Win[:, z0:z0 + 256].reshape(8, 128, 256).transpose(1, 0, 2)))
                for hf in range(2):
                    blocks.append(_pad_block(_rowblock(Wout, g * 4, 4, hf * 512, 512)))
        elif p == "kv":
            Wk, Wv = inp["w_k"], inp["w_v"]
            for kh in range(4):
                pl = kh * 64 + np.concatenate([np.arange(64), np.arange(64)])
                sw = kh * 64 + np.concatenate([PERM64, PERM64])
                blocks.append(_pad_block(_colchunks(Wk, [pl, sw])))
            blocks.append(_pad_block(Wv.reshape(8, 128, 256).transpose(1, 0, 2)))
        elif p == "attn":
            Wq, Wo = inp["attn_w_q"][0], inp["attn_w_o"][0]
            for c in range(8):
                blocks.append(_pad_block(_colchunks(Wq, [c * 128 + np.arange(128), c * 128 + PERM128])))
            for b in range(4):
                blocks.append(_pad_block(_rowblock(Wo, 0, 8, b * 256, 256)))
    if not blocks:
        blocks.append(np.zeros((128, 2048), np.float32))
    return np.stack(blocks, axis=0)


def pack_consts(inp):
    cc = np.zeros((128, NCC), np.float32)
    vecs = [inp["norm_w"][l, i] for l in range(2) for i in range(3)] + [inp["kv_norm_w"], inp["final_norm_w"]]
    for v, w in enumerate(vecs):
        cc[:, C_NW + v * 8:C_NW + v * 8 + 8] = w.reshape(8, 128).T
    cw = inp["ssm_conv_w"][0]
    cc[:, C_CW:C_CW + 96] = cw.reshape(4, 24, 128).transpose(2, 1, 0).reshape(128, 96)
    cc[:, C_CB:C_CB + 24] = inp["ssm_conv_b"][0].reshape(24, 128).T
    bq = inp["attn_b_q"][0]
    for c in range(8):
        cc[:, C_BQ + c * 2] = bq[c * 128 + np.arange(128)]
        cc[:, C_BQ + c * 2 + 1] = bq[c * 128 + PERM128]
    bk = inp["b_k"]
    for kh in range(4):
        cc[:, C_BK + kh * 2] = bk[kh * 64 + np.concatenate([np.arange(64), np.arange(64)])]
        cc[:, C_BK + kh * 2 + 1] = bk[kh * 64 + np.concatenate([PERM64, PERM64])]
    cc[:, C_BO:C_BO + 8] = inp["attn_b_o"][0].reshape(8, 128).T
    cb = np.zeros((128, NCB), np.float32)
    cb[:, B_DTB:B_DTB + 32] = inp["ssm_dt_bias"][0][None, :]
    cb[:, B_ALOG:B_ALOG + 32] = inp["ssm_a_log"][0][None, :]
    cb[:, B_DSK:B_DSK + 32] = inp["ssm_d"][0][None, :]
    cb[:, B_BV:B_BV + 256] = inp["b_v"][None, :]
    cb[:, B_SINK:B_SINK + 16] = inp["attn_sinks"][0][None, :]
    cb[:, B_SNW:B_SNW + 2048] = inp["ssm_norm_w"][0][None, :]
    return cc, cb


def const_tables():
    i = np.arange(128)
    ident = np.eye(128, dtype=np.float32)
    tri = (i[:, None] <= i[None, :]).astype(np.float32)
    U = (i[:, None] > i[None, :]).astype(np.float32)
    ones = np.ones((128, 128), np.float32)
    cmat = np.concatenate([ident, tri, U, ones], axis=1)
    q = i[:, None]
    j = np.arange(256)[None, :]
    valid = (j <= q + 128) & (j > q)
    maskb = np.where(valid, 0.0, -30000.0).astype(np.float32)
    pos = np.arange(S, dtype=np.float64)
    inv = 1.0 / (10000.0 ** (np.arange(0, 64, 2, dtype=np.float64) / 64.0))
    ang = pos[:, None] * inv[None, :]
    cos, sin = np.cos(ang).astype(np.float32), np.sin(ang).astype(np.float32)
    fi = (i % 64) % 32
    sign = np.where((i % 64) < 32, -1.0, 1.0).astype(np.float32)
    rope = np.stack([cos[:, fi].T, sin[:, fi].T * sign[:, None]], axis=1).astype(np.float32)
    return cmat, maskb, np.ascontiguousarray(rope)


_CACHE = {}


def run(inputs, phases, cores):
    key = tuple(phases)
    kb = KB(list(phases))
    nc = kb.build()
    wblk = pack_weights(inputs, phases)
    cc, cb = pack_consts(inputs)
    cmat, maskb, rope = const_tables()
    x = np.asarray(inputs["x"], np.float32)
    in_maps = []
    for b in cores:
        in_maps.append({"xT": np.ascontiguousarray(x[b].T), "wblk": wblk, "ccol": cc, "cbc": cb, "cmat": cmat,
                        "maskb": maskb, "rope": rope})
    res = run_bass_kernel_spmd(nc, in_maps, core_ids=list(range(len(cores))))
    outs = [np.ascontiguousarray(np.asarray(r["outT"]).T) for r in res.results]
    return np.stack(outs, axis=0)


def kernel(**inputs):
    inputs = {k: np.asarray(v) for k, v in inputs.items()}
    return run(inputs, PHASES, list(range(8))).astype(np.float32)
```

```python
import numpy as np
from contextlib import ExitStack
import concourse.bass as bass
import concourse.mybir as mybir
from concourse.bass_utils import run_bass_kernel_spmd

F32 = mybir.dt.float32
BF16 = mybir.dt.bfloat16
AF = mybir.ActivationFunctionType
ALU = mybir.AluOpType

ENGS = ["pe", "act", "dve", "pool", "sp"]
BLOCK_ATTR = {"pe": "tensor", "act": "scalar", "dve": "vector", "pool": "gpsimd", "sp": "sync"}
COMPUTE = ("pe", "act", "dve")


class Res:
    __slots__ = ("name", "w", "rs", "dsem", "dcount")

    def __init__(self, name):
        self.name = name
        self.w = None
        self.rs = []
        self.dsem = None
        self.dcount = 0


class Op:
    __slots__ = ("eng", "fn", "deps", "idx", "is_dma", "sig", "dres", "sigval", "sem", "final")


class Prog:
    def __init__(self, nc):
        self.nc = nc
        self.ops = {e: [] for e in ENGS}
        self.dma_res = []

    def fresh(self, name):
        r = Res(name)
        r.rs = [self.ops[e][-1] for e in COMPUTE if self.ops[e]]
        return r

    def add(self, eng, fn, reads=(), writes=(), dma=False):
        op = Op()
        op.eng = eng
        op.fn = fn
        op.is_dma = dma
        op.idx = len(self.ops[eng])
        op.sig = False
        op.sigval = 0
        op.sem = None
        op.dres = None
        op.final = False
        deps = []
        for r in reads:
            if r.w is not None:
                deps.append(r.w)
        for w in writes:
            if w.w is not None:
                deps.append(w.w)
            for x in w.rs:
                deps.append(x)
        best = {}
        out = []
        for d in deps:
            if d is op:
                continue
            if d.is_dma:
                out.append(d)
                continue
            if d.eng == "pe" and eng == "pe" and not dma:
                continue
            if d.eng not in best or best[d.eng].idx < d.idx:
                best[d.eng] = d
        op.deps = out + list(best.values())
        for r in reads:
            if dma:
                r.rs.append(op)
            else:
                r.rs = [x for x in r.rs if x.is_dma or x.eng != eng] + [op]
        for w in writes:
            w.w = op
            w.rs = []
        if dma:
            assert len(writes) == 1
            op.dres = writes[0]
            if op.dres.dsem is None:
                op.dres.dsem = True
                self.dma_res.append(op.dres)
        self.ops[eng].append(op)
        return op

    def emit(self, stack):
        nc = self.nc
        for e in ENGS:
            for op in self.ops[e]:
                for d in op.deps:
                    d.sig = True
        esem = {e: stack.enter_context(nc.semaphore("s_" + e)) for e in ENGS}
        for i, r in enumerate(self.dma_res):
            r.dsem = stack.enter_context(nc.semaphore("d%d" % i))
            r.dcount = 0
        for e in ENGS:
            cnt = 0
            for op in self.ops[e]:
                if op.is_dma:
                    r = op.dres
                    r.dcount += 16
                    op.sem = r.dsem
                    op.sigval = r.dcount
                elif op.sig:
                    cnt += 1
                    op.sem = esem[e]
                    op.sigval = cnt
        block = stack.enter_context(nc.Block())
        for e in ENGS:
            if self.ops[e]:
                self._emit_engine(block, e)

    def _emit_engine(self, block, e):
        ops = self.ops[e]
        deco = getattr(block, BLOCK_ATTR[e])

        @deco
        def _(eng):
            waited = {}
            for op in ops:
                need = {}
                for d in op.deps:
                    key = id(d.sem)
                    if need.get(key, (None, 0))[1] < d.sigval:
                        need[key] = (d.sem, d.sigval)
                for key, (sem, val) in need.items():
                    if waited.get(key, 0) >= val:
                        continue
                    eng.wait_ge(sem, val)
                    waited[key] = val
                ins = op.fn(eng)
                if op.is_dma:
                    ins.then_inc(op.sem, 16)
                elif op.sig:
                    ins.then_inc(op.sem, 1)
            for op in ops:
                if op.is_dma and op.final:
                    eng.wait_ge(op.sem, op.sigval)


_SBN = [0]


def SB(nc, name, shape, dt):
    _SBN[0] += 1
    return nc.sbuf_tensor("%s_%d" % (name, _SBN[0]), shape, dt)


D = 1024
S = 2048
NDC = 8
NTG = 4
TGW = 512
DFF = 2816
NFC = 22
PASSES = [6, 6, 5, 5]
EPS = 1e-5
NH_SSM = 32
RING = 6
LOOK = 4

C_NW = 0
C_CW = 64
C_CB = C_CW + 96
C_BQ = C_CB + 24
C_BK = C_BQ + 16
C_BO = C_BK + 8
NCC = C_BO + 8
B_DTB = 0
B_ALOG = 32
B_DSK = 64
B_BV = 96
B_SINK = 352
B_SMALL = 368
B_SNW = 368
NCB = B_SNW + 2048
NCM = 512

PHASES = ["ffn0a", "mamba", "ffn0b", "kv", "ffn1a", "attn", "ffn1b", "final"]


def n_blocks(phases):
    n = 0
    for p in phases:
        if p.startswith("ffn"):
            n += NFC + 4 * len(PASSES)
        elif p == "mamba":
            n += 1 + 4 * 7
        elif p == "kv":
            n += 5
        elif p == "attn":
            n += 12
    return n


class KB:
    def __init__(self, phases, debug_out=None):
        self.phases = phases
        self.nblk = max(1, n_blocks(phases))
        nc = bass.Bass("TRN2", target_bir_lowering=False)
        self.nc = nc
        self.P = Prog(nc)
        self.st = ExitStack()
        st = self.st
        self.xT_d = nc.dram_tensor("xT", [D, S], F32, kind="ExternalInput").ap()
        self.wb_d = nc.dram_tensor("wblk", [self.nblk, 128, 2048], F32, kind="ExternalInput").ap()
        self.cc_d = nc.dram_tensor("ccol", [128, NCC], F32, kind="ExternalInput").ap()
        self.cb_d = nc.dram_tensor("cbc", [128, NCB], F32, kind="ExternalInput").ap()
        self.cm_d = nc.dram_tensor("cmat", [128, NCM], F32, kind="ExternalInput").ap()
        self.mk_d = nc.dram_tensor("maskb", [128, 256], F32, kind="ExternalInput").ap()
        self.rope_d = nc.dram_tensor("rope", [128, 2, S], F32, kind="ExternalInput").ap()
        self.out_d = nc.dram_tensor("outT", [D, S], F32, kind="ExternalOutput").ap()
        P = self.P
        sb = lambda name, shape, dt: st.enter_context(SB(nc, name, shape, dt))
        self.xT = sb("xT_sb", [128, NDC, S], F32)
        self.rx = [[Res("x%d_%d" % (dc, tg)) for tg in range(NTG)] for dc in range(NDC)]
        self.ring = sb("ring", [128, RING, 2048], BF16)
        self.rring = [Res("ring%d" % i) for i in range(RING)]
        self.ccol = sb("ccol_sb", [128, NCC], F32)
        self.cbc = sb("cbc_sb", [128, B_SMALL], F32)
        self.cmb = sb("cmat_bf", [128, NCM], BF16)
        self.cmf = sb("cmat_f", [128, NCM], F32)
        self.maskb = sb("maskb_sb", [128, 256], F32)
        self.rconst = Res("const")
        self.ps = [st.enter_context(nc.psum_tensor("ps%d" % i, [128, 512], F32)) for i in range(8)]
        self.rps = [Res("ps%d" % i) for i in range(8)]
        self.wi = 0
        self.wissued = 0
        rc = [Res("c%d" % i) for i in range(6)]
        P.add("sp", lambda e: e.dma_start(out=self.ccol[:], in_=self.cc_d), writes=[rc[0]], dma=True)
        P.add("sp", lambda e: e.dma_start(out=self.cbc[:], in_=self.cb_d[:, 0:B_SMALL]), writes=[rc[1]], dma=True)
        P.add("sp", lambda e: e.dma_start(out=self.cmf[:], in_=self.cm_d), writes=[rc[2]], dma=True)
        P.add("sp", lambda e: e.dma_start(out=self.maskb[:], in_=self.mk_d), writes=[rc[3]], dma=True)
        P.add("pool", lambda e: e.dma_start(out=self.cmb[:], in_=self.cm_d), writes=[rc[4]], dma=True)
        self.rcs = rc[:5]
        xv = self.xT_d.rearrange("(c p) t -> p c t", p=128)
        for tg in range(NTG):
            for dc in range(NDC):
                sl = slice(tg * TGW, (tg + 1) * TGW)
                P.add("sp", lambda e, dc=dc, sl=sl: e.dma_start(out=self.xT[:, dc, sl], in_=xv[:, dc, sl]),
                      writes=[self.rx[dc][tg]], dma=True)
        self.ident = self.cmb[:, 0:128]
        self.tri_b = self.cmb[:, 128:256]
        self.U_b = self.cmb[:, 256:384]
        self.ones_b = self.cmb[:, 384:512]
        self.tri_f = self.cmf[:, 128:256]
        self.ones_f = self.cmf[:, 384:512]
        self.kT = None

    def wget(self, keep=0):
        P = self.P
        i = self.wi
        self.wi += 1
        assert i < self.nblk, "weight stream overrun"
        lim = min(self.nblk, i + 1 + LOOK, i - keep + RING)
        while self.wissued < lim:
            k = self.wissued
            s = k % RING
            P.add("pool", lambda e, k=k, s=s: e.dma_start(out=self.ring[:, s, :], in_=self.wb_d[k]),
                  writes=[self.rring[s]], dma=True)
            self.wissued += 1
        s = i % RING
        return self.ring[:, s, :], self.rring[s]

    def creads(self):
        return list(self.rcs)

    def emit_norm(self, ph, widx, out_t, rout, final=False):
        P, nc = self.P, self.nc
        sq = [ph.enter_context(SB(nc, "nsq%d" % k, [128, TGW], BF16)) for k in range(2)]
        rsq = [P.fresh("nsq%d" % k) for k in range(2)]
        sd = [ph.enter_context(SB(nc, "nsd%d" % k, [128, TGW], F32)) for k in range(2)]
        rsd = [P.fresh("nsd%d" % k) for k in range(2)]
        rstd = [ph.enter_context(SB(nc, "nrstd%d" % k, [128, TGW], F32)) for k in range(2)]
        rrstd = [P.fresh("nrstd%d" % k) for k in range(2)]
        psn, rpsn = self.ps[6], self.rps[6]
        cr = self.creads()
        for tg in range(NTG):
            sl = slice(tg * TGW, (tg + 1) * TGW)
            k2 = tg % 2
            for dc in range(NDC):
                k = dc % 2
                P.add("act", lambda e, k=k, dc=dc, sl=sl: e.activation(out=sq[k][:], in_=self.xT[:, dc, sl], func=AF.Square),
                      reads=[self.rx[dc][tg]], writes=[rsq[k]])
                P.add("pe", lambda e, k=k, dc=dc: e.matmul(psn[:], lhsT=self.ones_b, rhs=sq[k][:], start=(dc == 0), stop=(dc == NDC - 1)),
                      reads=[rsq[k]] + cr, writes=[rpsn])
            P.add("act", lambda e, k2=k2: e.activation(out=sd[k2][:], in_=psn[:], func=AF.Ln, bias=self.epsc[:, 0:1], scale=1.0 / D),
                  reads=[rpsn] + cr, writes=[rsd[k2]])
            P.add("act", lambda e, k2=k2: e.activation(out=rstd[k2][:], in_=sd[k2][:], func=AF.Exp, scale=-0.5),
                  reads=[rsd[k2]], writes=[rrstd[k2]])
            for dc in range(NDC):
                col = C_NW + widx * 8 + dc
                P.add("dve", lambda e, dc=dc, sl=sl, col=col, k2=k2: e.scalar_tensor_tensor(
                    out=out_t[:, dc, sl], in0=self.xT[:, dc, sl], scalar=self.ccol[:, col:col + 1], in1=rstd[k2][:],
                    op0=ALU.mult, op1=ALU.mult),
                    reads=[self.rx[dc][tg], rrstd[k2]] + cr, writes=[rout[dc][tg]])

    def emit_ffn(self, widx):
        P, nc = self.P, self.nc
        with ExitStack() as ph:
            hT = ph.enter_context(SB(nc, "hT", [128, NDC, S], BF16))
            rh = [[P.fresh("h") for _ in range(NTG)] for _ in range(NDC)]
            with ExitStack() as phn:
                self.emit_norm(phn, widx, hT, rh)
            npmax = max(PASSES)
            aT = ph.enter_context(SB(nc, "aT", [128, npmax, S], BF16))
            ra = [[P.fresh("a") for _ in range(NTG)] for _ in range(npmax)]
            sg = [ph.enter_context(SB(nc, "sg%d" % k, [128, TGW], F32)) for k in range(2)]
            rsg = [P.fresh("sg") for _ in range(2)]
            cnt = 0
            for npass_i, npass in enumerate(PASSES):
                for jj in range(npass):
                    slot, rslot = self.wget()
                    sv = slot.rearrange("p (a b c) -> p a b c", a=2, b=NDC)
                    for tg in range(NTG):
                        sl = slice(tg * TGW, (tg + 1) * TGW)
                        k = cnt % 2
                        cnt += 1
                        pg, rpg = self.ps[k], self.rps[k]
                        pu, rpu = self.ps[2 + k], self.rps[2 + k]
                        for which, (pp, rpp) in enumerate(((pg, rpg), (pu, rpu))):
                            for dc in range(NDC):
                                P.add("pe", lambda e, pp=pp, which=which, dc=dc, sl=sl, sv=sv: e.matmul(
                                    pp[:], lhsT=sv[:, which, dc, :], rhs=hT[:, dc, sl], start=(dc == 0), stop=(dc == NDC - 1)),
                                    reads=[rslot, rh[dc][tg]], writes=[rpp])
                        P.add("act", lambda e, k=k, pg=pg: e.activation(out=sg[k][:], in_=pg[:], func=AF.Silu),
                              reads=[rpg], writes=[rsg[k]])
                        P.add("dve", lambda e, k=k, pu=pu, jj=jj, sl=sl: e.tensor_tensor(
                            out=aT[:, jj, sl], in0=sg[k][:], in1=pu[:], op=ALU.mult),
                            reads=[rsg[k], rpu], writes=[ra[jj][tg]])
                last = (npass_i == len(PASSES) - 1)
                if last:
                    blks = []
                    for b in range(4):
                        slot, rslot = self.wget(keep=b)
                        blks.append((slot[:, 0:npass * 256].rearrange("p (j c) -> p j c", j=npass), rslot))
                    for tg in range(NTG):
                        sl = slice(tg * TGW, (tg + 1) * TGW)
                        for dc in range(NDC):
                            sv, rslot = blks[dc // 2]
                            d2 = dc % 2
                            k = cnt % 2
                            cnt += 1
                            pd, rpd = self.ps[4 + k], self.rps[4 + k]
                            for jj in range(npass):
                                P.add("pe", lambda e, pd=pd, sv=sv, jj=jj, d2=d2, sl=sl, npass=npass: e.matmul(
                                    pd[:], lhsT=sv[:, jj, d2 * 128:(d2 + 1) * 128], rhs=aT[:, jj, sl],
                                    start=(jj == 0), stop=(jj == npass - 1)),
                                    reads=[rslot, ra[jj][tg]], writes=[rpd])
                            P.add("dve", lambda e, pd=pd, dc=dc, sl=sl: e.scalar_tensor_tensor(
                                out=self.xT[:, dc, sl], in0=pd[:], scalar=0.5, in1=self.xT[:, dc, sl],
                                op0=ALU.mult, op1=ALU.add),
                                reads=[rpd, self.rx[dc][tg]], writes=[self.rx[dc][tg]])
                for b in range(0 if last else 4):
                    slot, rslot = self.wget()
                    sv = slot[:, 0:npass * 256].rearrange("p (j c) -> p j c", j=npass)
                    for d2 in range(2):
                        dc = b * 2 + d2
                        for tg in range(NTG):
                            sl = slice(tg * TGW, (tg + 1) * TGW)
                            k = cnt % 2
                            cnt += 1
                            pd, rpd = self.ps[4 + k], self.rps[4 + k]
                            for jj in range(npass):
                                P.add("pe", lambda e, pd=pd, sv=sv, jj=jj, d2=d2, sl=sl, npass=npass: e.matmul(
                                    pd[:], lhsT=sv[:, jj, d2 * 128:(d2 + 1) * 128], rhs=aT[:, jj, sl],
                                    start=(jj == 0), stop=(jj == npass - 1)),
                                    reads=[rslot, ra[jj][tg]], writes=[rpd])
                            P.add("dve", lambda e, pd=pd, dc=dc, sl=sl: e.scalar_tensor_tensor(
                                out=self.xT[:, dc, sl], in0=pd[:], scalar=0.5, in1=self.xT[:, dc, sl],
                                op0=ALU.mult, op1=ALU.add),
                                reads=[rpd, self.rx[dc][tg]], writes=[self.rx[dc][tg]])

    def emit_final(self, do_norm=True):
        P, nc = self.P, self.nc
        ov = self.out_d.rearrange("(c p) t -> p c t", p=128)
        with ExitStack() as ph:
            if do_norm:
                oT = ph.enter_context(SB(nc, "oT", [128, NDC, S], F32))
                ro = [[P.fresh("o") for _ in range(NTG)] for _ in range(NDC)]
                self.emit_norm(ph, 7, oT, ro)
            else:
                oT, ro = self.xT, self.rx
            for tg in range(NTG):
                for dc in range(NDC):
                    sl = slice(tg * TGW, (tg + 1) * TGW)
                    o = P.add("sp", lambda e, dc=dc, sl=sl: e.dma_start(out=ov[:, dc, sl], in_=oT[:, dc, sl]),
                              reads=[ro[dc][tg]], writes=[Res("out")], dma=True)
                    o.final = True

    def emit_rope_proj(self, ph, hT, rh, nchunks, bias_col0, out_t, rout, ropeb, rrope, tmp, rtmp, cnt0=0):
        P = self.P
        cr = self.creads()
        cnt = cnt0
        for c in range(nchunks):
            slot, rslot = self.wget()
            sv = slot.rearrange("p (a b c) -> p a b c", a=2, b=NDC)
            for tg in range(NTG):
                sl = slice(tg * TGW, (tg + 1) * TGW)
                k = cnt % 2
                cnt += 1
                pq, rpq = self.ps[k], self.rps[k]
                pqs, rpqs = self.ps[2 + k], self.rps[2 + k]
                P.add("sp", lambda e, k=k, sl=sl: e.dma_start(out=ropeb[k][:], in_=self.rope_d[:, :, sl]),
                      writes=[rrope[k]], dma=True)
                for which, (pp, rpp) in enumerate(((pq, rpq), (pqs, rpqs))):
                    for dc in range(NDC):
                        P.add("pe", lambda e, pp=pp, which=which, dc=dc, sl=sl, sv=sv: e.matmul(
                            pp[:], lhsT=sv[:, which, dc, :], rhs=hT[:, dc, sl], start=(dc == 0), stop=(dc == NDC - 1)),
                            reads=[rslot, rh[dc][tg]], writes=[rpp])
                b0 = bias_col0 + c * 2
                P.add("dve", lambda e, k=k, pq=pq, b0=b0: e.scalar_tensor_tensor(
                    out=tmp[2 * k][:], in0=pq[:], scalar=self.ccol[:, b0:b0 + 1], in1=ropeb[k][:, 0, :],
                    op0=ALU.add, op1=ALU.mult), reads=[rpq, rrope[k]] + cr, writes=[rtmp[2 * k]])
                P.add("dve", lambda e, k=k, pqs=pqs, b0=b0: e.scalar_tensor_tensor(
                    out=tmp[2 * k + 1][:], in0=pqs[:], scalar=self.ccol[:, b0 + 1:b0 + 2], in1=ropeb[k][:, 1, :],
                    op0=ALU.add, op1=ALU.mult), reads=[rpqs, rrope[k]] + cr, writes=[rtmp[2 * k + 1]])
                P.add("dve", lambda e, k=k, c=c, sl=sl: e.tensor_tensor(
                    out=out_t[:, c, sl], in0=tmp[2 * k][:], in1=tmp[2 * k + 1][:], op=ALU.add),
                    reads=[rtmp[2 * k], rtmp[2 * k + 1]], writes=[rout[c][tg]])
        return cnt

    def _rope_bufs(self, ph):
        P, nc = self.P, self.nc
        ropeb = [ph.enter_context(SB(nc, "ropeb%d" % k, [128, 2, TGW], F32)) for k in range(2)]
        rrope = [P.fresh("ropeb") for _ in range(2)]
        tmp = [ph.enter_context(SB(nc, "rtmp%d" % k, [128, TGW], F32)) for k in range(4)]
        rtmp = [P.fresh("rtmp") for _ in range(4)]
        return ropeb, rrope, tmp, rtmp

    def emit_kv(self):
        P, nc = self.P, self.nc
        st = self.st
        self.kT = st.enter_context(SB(nc, "kT", [128, 4, S], BF16))
        self.rk = [[P.fresh("k") for _ in range(NTG)] for _ in range(4)]
        self.vtok = st.enter_context(SB(nc, "vtok", [128, 16, 256], BF16))
        self.rv = [P.fresh("v") for _ in range(16)]
        cr = self.creads()
        with ExitStack() as ph:
            hT = ph.enter_context(SB(nc, "hT", [128, NDC, S], BF16))
            rh = [[P.fresh("h") for _ in range(NTG)] for _ in range(NDC)]
            with ExitStack() as phn:
                self.emit_norm(phn, 6, hT, rh)
            ropeb, rrope, tmp, rtmp = self._rope_bufs(ph)
            self.emit_rope_proj(ph, hT, rh, 4, C_BK, self.kT, self.rk, ropeb, rrope, tmp, rtmp)
            slot, rslot = self.wget()
            sv = slot.rearrange("p (b c) -> p b c", b=NDC)
            for n in range(16):
                k = n % 2
                pv, rpv = self.ps[4 + k], self.rps[4 + k]
                for dc in range(NDC):
                    P.add("pe", lambda e, pv=pv, dc=dc, n=n: e.matmul(
                        pv[:, 0:256], lhsT=hT[:, dc, n * 128:(n + 1) * 128], rhs=sv[:, dc, :],
                        start=(dc == 0), stop=(dc == NDC - 1)), reads=[rslot, rh[dc][n // 4]], writes=[rpv])
                P.add("dve", lambda e, pv=pv, n=n: e.tensor_tensor(
                    out=self.vtok[:, n, :], in0=pv[:, 0:256], in1=self.cbc[:, B_BV:B_BV + 256], op=ALU.add),
                    reads=[rpv] + cr, writes=[self.rv[n]])

    def emit_attn(self):
        P, nc = self.P, self.nc
        cr = self.creads()
        with ExitStack() as ph:
            qT = ph.enter_context(SB(nc, "qT", [128, NDC, S], BF16))
            rq = [[P.fresh("q") for _ in range(NTG)] for _ in range(NDC)]
            with ExitStack() as ph2:
                hT = ph2.enter_context(SB(nc, "hT", [128, NDC, S], BF16))
                rh = [[P.fresh("h") for _ in range(NTG)] for _ in range(NDC)]
                with ExitStack() as phn:
                    self.emit_norm(phn, 4, hT, rh)
                ropeb, rrope, tmp, rtmp = self._rope_bufs(ph2)
                self.emit_rope_proj(ph2, hT, rh, 8, C_BQ, qT, rq, ropeb, rrope, tmp, rtmp)
            aTt = ph.enter_context(SB(nc, "attnT", [128, NDC, S], BF16))
            raT = [[P.fresh("aT") for _ in range(NTG)] for _ in range(NDC)]
            sm = [ph.enter_context(SB(nc, "sm%d" % k, [128, 4, 256], F32)) for k in range(2)]
            rsm = [P.fresh("sm") for _ in range(2)]
            pb = [ph.enter_context(SB(nc, "pb%d" % k, [128, 4, 256], BF16)) for k in range(2)]
            rpb = [P.fresh("pb") for _ in range(2)]
            ptb = [ph.enter_context(SB(nc, "ptb%d" % k, [128, 8, 128], BF16)) for k in range(2)]
            rptb = [P.fresh("ptb") for _ in range(2)]
            stat2 = [ph.enter_context(SB(nc, "stat%d" % i, [128, 6, 16], F32)) for i in range(2)]
            rstat2 = [[P.fresh("stat%d" % i) for i in range(6)] for _ in range(2)]
            atok = ph.enter_context(SB(nc, "atok", [128, 1024], BF16))
            ratok = P.fresh("atok")
            po = [self.ps[4], self.ps[5]]
            rpo = [self.rps[4], self.rps[5]]
            X = mybir.AxisListType.X

            sinkmax = ph.enter_context(SB(nc, "sinkmax", [128, 4], F32))
            rsinkmax = P.fresh("sinkmax")
            P.add("dve", lambda e: e.tensor_reduce(
                out=sinkmax[:], in_=self.cbc[:, B_SINK:B_SINK + 16].rearrange("p (j h) -> p j h", j=4), axis=X, op=ALU.max),
                reads=cr, writes=[rsinkmax])

            def geom(n):
                nk = 128 if n == 0 else 256
                return nk, nk // 128, (0 if n == 0 else (n - 1) * 128), (128 if n == 0 else 0)

            def stA1(g):
                n, j = g // 4, g % 4
                k = g % 2
                nk, nhalf, k0, mcol = geom(n)
                stat, rstat = stat2[n % 2], rstat2[n % 2]
                kh = j
                kreads = [self.rk[kh][n // 4]] + ([self.rk[kh][(n - 1) // 4]] if n > 0 else [])
                for hh in range(4):
                    h = 4 * j + hh
                    c, base = h // 2, 64 * (h % 2)
                    pS, rpS = self.ps[2 * k + hh % 2], self.rps[2 * k + hh % 2]
                    P.add("pe", lambda e, pS=pS, base=base, c=c, n=n, kh=kh, k0=k0, nk=nk, hh=hh: e.matmul(
                        pS[:, (hh // 2) * 256:(hh // 2) * 256 + nk], lhsT=qT[base:base + 64, c, n * 128:(n + 1) * 128],
                        rhs=self.kT[base:base + 64, kh, k0:k0 + nk], start=True, stop=True),
                        reads=[rq[c][n // 4]] + kreads, writes=[rpS])
                for b2 in range(2):
                    pS, rpS = self.ps[2 * k + b2], self.rps[2 * k + b2]
                    P.add("dve", lambda e, k=k, pS=pS, nk=nk, mcol=mcol, b2=b2: e.scalar_tensor_tensor(
                        out=sm[k][:, 2 * b2:2 * b2 + 2, 0:nk], in0=pS[:].rearrange("p (h s) -> p h s", h=2)[:, :, 0:nk],
                        scalar=0.125, in1=self.maskb[:, mcol:mcol + nk].unsqueeze(1).broadcast_to([128, 2, nk]),
                        op0=ALU.mult, op1=ALU.add), reads=[rpS] + cr, writes=[rsm[k]])
                P.add("dve", lambda e, k=k, nk=nk, j=j, stat=stat: e.tensor_reduce(
                    out=stat[:, 0, j:j + 1], in_=sm[k][:, :, 0:nk], axis=mybir.AxisListType.XY, op=ALU.max),
                    reads=[rsm[k]], writes=[rstat[0]])
                P.add("dve", lambda e, j=j, stat=stat: e.tensor_tensor(
                    out=stat[:, 0, j:j + 1], in0=stat[:, 0, j:j + 1], in1=sinkmax[:, j:j + 1], op=ALU.max),
                    reads=[rstat[0], rsinkmax], writes=[rstat[0]])
                P.add("dve", lambda e, j=j, stat=stat: e.tensor_scalar(
                    out=stat[:, 1, j:j + 1], in0=stat[:, 0, j:j + 1], scalar1=-1.0, scalar2=None, op0=ALU.mult),
                    reads=[rstat[0]], writes=[rstat[1]])

            def stA2(g):
                n, j = g // 4, g % 4
                k = g % 2
                nk, nhalf, k0, mcol = geom(n)
                stat, rstat = stat2[n % 2], rstat2[n % 2]
                P.add("act", lambda e, k=k, nk=nk, j=j, stat=stat: e.activation(
                    out=pb[k][:, :, 0:nk], in_=sm[k][:, :, 0:nk], func=AF.Exp, bias=stat[:, 1, j:j + 1], scale=1.0),
                    reads=[rsm[k], rstat[1]], writes=[rpb[k]])

            def stB1(g):
                n, j = g // 4, g % 4
                k = g % 2
                nk, nhalf, k0, mcol = geom(n)
                stat, rstat = stat2[n % 2], rstat2[n % 2]
                P.add("dve", lambda e, k=k, nk=nk, j=j, stat=stat: e.tensor_reduce(
                    out=stat[:, 2, 4 * j:4 * j + 4].rearrange("p (a b) -> p b a", a=2),
                    in_=pb[k][:, :, 0:nk].rearrange("p (b a) s -> p b a s", b=2), axis=X, op=ALU.add),
                    reads=[rpb[k]], writes=[rstat[2]])
                if j == 3:
                    P.add("dve", lambda e, stat=stat: e.tensor_tensor(
                        out=stat[:, 3, :].rearrange("p (j h) -> p j h", j=4),
                        in0=self.cbc[:, B_SINK:B_SINK + 16].rearrange("p (j h) -> p j h", j=4),
                        in1=stat[:, 0, 0:4].to_broadcast([128, 4, 4]), op=ALU.subtract),
                        reads=[rstat[0]] + cr, writes=[rstat[3]])
                    P.add("act", lambda e, stat=stat: e.activation(out=stat[:, 3, :], in_=stat[:, 3, :], func=AF.Exp), reads=[rstat[3]], writes=[rstat[3]])
                    P.add("dve", lambda e, stat=stat: e.tensor_tensor(out=stat[:, 4, :], in0=stat[:, 3, :], in1=stat[:, 2, :], op=ALU.add),
                          reads=[rstat[3], rstat[2]], writes=[rstat[4]])
                    P.add("dve", lambda e, stat=stat: e.reciprocal(out=stat[:, 5, :], in_=stat[:, 4, :]), reads=[rstat[4]], writes=[rstat[5]])

                for hh in range(4):
                    pt, rpt = self.ps[6 + hh // 2], self.rps[6 + hh // 2]
                    for hf in range(nhalf):
                        slot_i = (hh % 2) * 2 + hf
                        P.add("pe", lambda e, pt=pt, k=k, hh=hh, hf=hf, slot_i=slot_i: e.matmul(
                            pt[:, slot_i * 128:(slot_i + 1) * 128], lhsT=pb[k][:, (hh % 2) * 2 + hh // 2, hf * 128:(hf + 1) * 128], rhs=self.ident,
                            start=True, stop=True), reads=[rpb[k]] + cr, writes=[rpt])
                for b2 in range(2):
                    pt, rpt = self.ps[6 + b2], self.rps[6 + b2]
                    for a2 in range(2):
                        P.add("act", lambda e, k=k, pt=pt, b2=b2, nhalf=nhalf, a2=a2: e.copy(
                            out=ptb[k][:, 4 * b2 + 2 * a2:4 * b2 + 2 * a2 + nhalf, :],
                            in_=pt[:, a2 * 256:a2 * 256 + nhalf * 128].rearrange("p (f t) -> p f t", f=nhalf)),
                            reads=[rpt], writes=[rptb[k]])

            def stB2(g):
                n, j = g // 4, g % 4
                k = g % 2
                nk, nhalf, k0, mcol = geom(n)
                stat, rstat = stat2[n % 2], rstat2[n % 2]
                kh = j
                for hh in range(4):
                    h = 4 * j + hh
                    ob = h // 8
                    oc = (h % 8) * 64
                    for hf in range(nhalf):
                        nb = n if nhalf == 1 else (n - 1 + hf)
                        si = hh * 2 + hf
                        P.add("pe", lambda e, ob=ob, oc=oc, k=k, hf=hf, nb=nb, kh=kh, nhalf=nhalf, si=si: e.matmul(
                            po[ob][:, oc:oc + 64], lhsT=ptb[k][:, si, :],
                            rhs=self.vtok[:, nb, kh * 64:(kh + 1) * 64], start=(hf == 0), stop=(hf == nhalf - 1)),
                            reads=[rptb[k], self.rv[nb]], writes=[rpo[ob]])
                if j == 3:
                    for ob in range(2):
                        P.add("dve", lambda e, ob=ob, stat=stat: e.tensor_tensor(
                            out=atok[:, ob * 512:(ob + 1) * 512].rearrange("p (h d) -> p h d", h=8),
                            in0=po[ob][:].rearrange("p (h d) -> p h d", h=8),
                            in1=stat[:, 5, ob * 8:(ob + 1) * 8].to_broadcast([128, 8, 64]), op=ALU.mult),
                            reads=[rpo[ob], rstat[5]], writes=[ratok])
                    for half in range(2):
                        pt, rpt = self.ps[6 + half], self.rps[6 + half]
                        for c4 in range(4):
                            c = half * 4 + c4
                            P.add("pe", lambda e, pt=pt, c4=c4, c=c: e.matmul(
                                pt[:, c4 * 128:(c4 + 1) * 128], lhsT=atok[:, c * 128:(c + 1) * 128], rhs=self.ident,
                                start=True, stop=True), reads=[ratok] + cr, writes=[rpt])
                        P.add("act", lambda e, pt=pt, half=half, n=n: e.copy(
                            out=aTt[:, half * 4:(half + 1) * 4, n * 128:(n + 1) * 128],
                            in_=pt[:].rearrange("p (c t) -> p c t", c=4)),
                            reads=[rpt], writes=[raT[half * 4 + c4][n // 4] for c4 in range(4)])

            stages = [stA1, stA2, stB1, stB2]
            NG = 64
            for t in range(NG + len(stages) - 1):
                for si_ in reversed(range(len(stages))):
                    g = t - si_
                    if 0 <= g < NG:
                        stages[si_](g)
            cnt = 0
            blks = []
            for b in range(4):
                slot, rslot = self.wget(keep=b)
                blks.append((slot.rearrange("p (c k) -> p c k", c=NDC), rslot))
            for tg in range(NTG):
                sl = slice(tg * TGW, (tg + 1) * TGW)
                for dc in range(NDC):
                    sv, rslot = blks[dc // 2]
                    d2 = dc % 2
                    k = cnt % 2
                    cnt += 1
                    pd, rpd = self.ps[2 + k], self.rps[2 + k]
                    for c in range(NDC):
                        P.add("pe", lambda e, pd=pd, sv=sv, c=c, d2=d2, sl=sl: e.matmul(
                            pd[:], lhsT=sv[:, c, d2 * 128:(d2 + 1) * 128], rhs=aTt[:, c, sl],
                            start=(c == 0), stop=(c == NDC - 1)), reads=[rslot, raT[c][tg]], writes=[rpd])
                    P.add("dve", lambda e, pd=pd, dc=dc, sl=sl: e.scalar_tensor_tensor(
                        out=self.xT[:, dc, sl], in0=pd[:], scalar=self.ccol[:, C_BO + dc:C_BO + dc + 1], in1=self.xT[:, dc, sl],
                        op0=ALU.add, op1=ALU.add), reads=[rpd, self.rx[dc][tg]] + cr, writes=[self.rx[dc][tg]])

    def emit_mamba(self):
        P, nc = self.P, self.nc
        cr = self.creads()
        X = mybir.AxisListType.X
        with ExitStack() as ph:
            hT = ph.enter_context(SB(nc, "hT", [128, NDC, S], BF16))
            rh = [[P.fresh("h") for _ in range(NTG)] for _ in range(NDC)]
            with ExitStack() as phn:
                self.emit_norm(phn, 1, hT, rh)
            tab = ph.enter_context(SB(nc, "tab", [128, 5, 512], F32))
            rtab = [P.fresh("tab%d" % i) for i in range(5)]
            with ExitStack() as p0:
                tA = [p0.enter_context(SB(nc, "tA%d" % k, [128, 512], F32)) for k in range(2)]
                rtA = [P.fresh("tA") for _ in range(2)]
                eA = p0.enter_context(SB(nc, "eA", [128, 32], F32))
                reA = P.fresh("eA")
                slot, rslot = self.wget()
                sv = slot[:, 0:256].rearrange("p (b h) -> p b h", b=NDC)
                pdt, rpdt = self.ps[0], self.rps[0]
                for c in range(16):
                    for dc in range(NDC):
                        P.add("pe", lambda e, c=c, dc=dc, sv=sv: e.matmul(
                            pdt[:, c * 32:(c + 1) * 32], lhsT=hT[:, dc, c * 128:(c + 1) * 128], rhs=sv[:, dc, :],
                            start=(dc == 0), stop=(dc == NDC - 1)), reads=[rslot, rh[dc][c // 4]], writes=[rpdt])
                v3 = lambda ap: ap.rearrange("p (c h) -> p c h", c=16)
                P.add("dve", lambda e: e.tensor_tensor(
                    out=v3(tA[0][:]), in0=v3(pdt[:]), in1=self.cbc[:, B_DTB:B_DTB + 32].unsqueeze(1).broadcast_to([128, 16, 32]),
                    op=ALU.add), reads=[rpdt] + cr, writes=[rtA[0]])
                P.add("act", lambda e: e.activation(out=tA[0][:], in_=tA[0][:], func=AF.Exp), reads=[rtA[0]], writes=[rtA[0]])
                P.add("act", lambda e: e.activation(out=tab[:, 0, :], in_=tA[0][:], func=AF.Ln, bias=self.kc[:, 1:2], scale=1.0),
                      reads=[rtA[0]] + cr, writes=[rtab[0]])
                P.add("act", lambda e: e.activation(out=eA[:], in_=self.cbc[:, B_ALOG:B_ALOG + 32], func=AF.Exp),
                      reads=cr, writes=[reA])
                P.add("dve", lambda e: e.scalar_tensor_tensor(
                    out=v3(tab[:, 1, :]), in0=v3(tab[:, 0, :]), scalar=-1.0, in1=eA[:].unsqueeze(1).broadcast_to([128, 16, 32]),
                    op0=ALU.mult, op1=ALU.mult), reads=[rtab[0], reA], writes=[rtab[1]])
                pacs, rpacs = self.ps[1], self.rps[1]
                pal, rpal = self.ps[2], self.rps[2]
                ahl = [p0.enter_context(SB(nc, "ahl%d" % k, [128, 512], BF16)) for k in range(2)]
                rahl = [P.fresh("ahl") for _ in range(2)]
                P.add("dve", lambda e: e.tensor_copy(out=ahl[0][:], in_=tab[:, 1, :]), reads=[rtab[1]], writes=[rahl[0]])
                P.add("dve", lambda e: e.tensor_tensor(out=ahl[1][:], in0=tab[:, 1, :], in1=ahl[0][:], op=ALU.subtract),
                      reads=[rtab[1], rahl[0]], writes=[rahl[1]])
                for k in range(2):
                    P.add("pe", lambda e, k=k: e.matmul(pacs[:], lhsT=self.tri_b, rhs=ahl[k][:], start=(k == 0), stop=(k == 1)),
                          reads=[rahl[k]] + cr, writes=[rpacs])
                for k in range(2):
                    P.add("pe", lambda e, k=k: e.matmul(pal[:], lhsT=self.ones_b, rhs=ahl[k][:], start=(k == 0), stop=(k == 1)),
                          reads=[rahl[k]] + cr, writes=[rpal])
                P.add("act", lambda e: e.activation(out=tab[:, 2, :], in_=pacs[:], func=AF.Exp), reads=[rpacs], writes=[rtab[2]])
                P.add("act", lambda e: e.activation(out=tab[:, 4, :], in_=pal[:], func=AF.Exp), reads=[rpal], writes=[rtab[4]])
                P.add("act", lambda e: e.copy(out=tA[1][:], in_=pacs[:]), reads=[rpacs], writes=[rtA[1]])
                P.add("dve", lambda e: e.tensor_tensor(out=tA[0][:], in0=pal[:], in1=tA[1][:], op=ALU.subtract),
                      reads=[rpal, rtA[1], rtA[0]], writes=[rtA[0]])
                P.add("act", lambda e: e.activation(out=tA[0][:], in_=tA[0][:], func=AF.Exp), reads=[rtA[0]], writes=[rtA[0]])
                P.add("dve", lambda e: e.tensor_tensor(out=tab[:, 3, :], in0=tab[:, 0, :], in1=tA[0][:], op=ALU.mult),
                      reads=[rtab[0], rtA[0]], writes=[rtab[3]])
            self.dbg = 0
            if self.dbg == 1:
                return
            for g in range(4):
                self._mamba_group(g, hT, rh, tab, rtab, cr)
                if self.dbg >= 2:
                    return

    def _mamba_group(self, g, hT, rh, tab, rtab, cr):
        P, nc = self.P, self.nc
        with ExitStack() as pg:
            BT = pg.enter_context(SB(nc, "BT", [128, S], BF16))
            CT = pg.enter_context(SB(nc, "CT", [128, S], BF16))
            Btok = pg.enter_context(SB(nc, "Btok", [128, 16, 128], BF16))
            xstok = pg.enter_context(SB(nc, "xstok", [128, 16, 512], BF16))
            snw = pg.enter_context(SB(nc, "snw", [128, 512], F32))
            Sst = pg.enter_context(SB(nc, "Sst", [128, 512], F32))
            Sbf = pg.enter_context(SB(nc, "Sbf", [128, 512], BF16))
            rBT, rCT, rS, rSbf, rsnw = P.fresh("BT"), P.fresh("CT"), P.fresh("S"), P.fresh("Sbf"), P.fresh("snw")
            rBtok = [P.fresh("Btok") for _ in range(4)]
            rxstok = [P.fresh("xstok") for _ in range(4)]
            P.add("sp", lambda e, g=g: e.dma_start(out=snw[:], in_=self.cb_d[:, B_SNW + g * 512:B_SNW + (g + 1) * 512]),
                  writes=[rsnw], dma=True)
            tcnt = 0
            with ExitStack() as s1:
                ubuf = s1.enter_context(SB(nc, "ubuf", [128, S + 4], F32))
                acc = s1.enter_context(SB(nc, "acc", [128, S], F32))
                xsT = s1.enter_context(SB(nc, "xsTc", [128, S], BF16))
                rub = [P.fresh("ubuf") for _ in range(NTG + 1)]
                racc = [P.fresh("acc") for _ in range(NTG)]
                P.add("dve", lambda e: e.memset(ubuf[:, 0:3], 0.0), writes=[rub[0]])
                xsT2 = [xsT, s1.enter_context(SB(nc, "xsTd", [128, S], BF16))]
                units = []
                state = {"ccnt": 0, "tcnt": 0}
                rBTl, rCTl = [None], [None]

                def stA(u):
                    ci, tg = u // 4, u % 4
                    bi, k2 = ci // 2, ci % 2
                    if tg == 0:
                        if k2 == 0:
                            slot, rslot = self.wget()
                            state["sv"] = slot.rearrange("p (k b c) -> p k b c", k=2, b=NDC)
                            state["rslot"] = rslot
                        if ci < 4:
                            state["dst"] = xsT2[state["ccnt"] % 2]
                            state["ccnt"] += 1
                        else:
                            state["dst"] = BT if ci == 4 else CT
                        state["rdst"] = [P.fresh("dst") for _ in range(NTG)]
                        if ci == 4:
                            rBTl[0] = state["rdst"]
                        elif ci == 5:
                            rCTl[0] = state["rdst"]
                    sv, rslot, dst, rdst = state["sv"], state["rslot"], state["dst"], state["rdst"]
                    cc = g * 4 + ci if ci < 4 else (16 + g if ci == 4 else 20 + g)
                    units.append((ci, tg, dst, rdst))
                    sl = slice(tg * TGW, (tg + 1) * TGW)
                    pu, rpu = self.ps[tg % 2], self.rps[tg % 2]
                    for dc in range(NDC):
                        P.add("pe", lambda e, pu=pu, sv=sv, k2=k2, dc=dc, sl=sl: e.matmul(
                            pu[:], lhsT=sv[:, k2, dc, :], rhs=hT[:, dc, sl], start=(dc == 0), stop=(dc == NDC - 1)),
                            reads=[rslot, rh[dc][tg]], writes=[rpu])
                    P.add("act", lambda e, pu=pu, tg=tg: e.copy(out=ubuf[:, 3 + tg * TGW:3 + (tg + 1) * TGW], in_=pu[:]),
                          reads=[rpu], writes=[rub[tg + 1]])
                    c3 = C_CW + cc * 4 + 3
                    P.add("act", lambda e, pu=pu, sl=sl, c3=c3, cc=cc: e.activation(
                        out=acc[:, sl], in_=pu[:], func=AF.Identity, bias=self.ccol[:, C_CB + cc:C_CB + cc + 1],
                        scale=self.ccol[:, c3:c3 + 1]), reads=[rpu] + cr, writes=[racc[tg]])
                    for kk in (2, 1, 0):
                        ck = C_CW + cc * 4 + kk
                        P.add("dve", lambda e, kk=kk, ck=ck, tg=tg, sl=sl: e.scalar_tensor_tensor(
                            out=acc[:, sl], in0=ubuf[:, kk + tg * TGW:kk + (tg + 1) * TGW], scalar=self.ccol[:, ck:ck + 1],
                            in1=acc[:, sl], op0=ALU.mult, op1=ALU.add),
                            reads=[rub[tg], rub[tg + 1], racc[tg]] + cr, writes=[racc[tg]])

                def stA2(u):
                    ci, tg, dst, rdst = units[u]
                    sl = slice(tg * TGW, (tg + 1) * TGW)
                    P.add("act", lambda e, dst=dst, sl=sl: e.activation(out=dst[:, sl], in_=acc[:, sl], func=AF.Silu),
                          reads=[racc[tg]], writes=[rdst[tg]])

                def stB(u):
                    ci, tg, dst, rdst = units[u]
                    if ci > 4:
                        return
                    k = state["tcnt"] % 2
                    state["tcnt"] += 1
                    pt, rpt = self.ps[2 + k], self.rps[2 + k]
                    for cq in range(4):
                        c = tg * 4 + cq
                        P.add("pe", lambda e, pt=pt, cq=cq, c=c, dst=dst: e.matmul(
                            pt[:, cq * 128:(cq + 1) * 128], lhsT=dst[:, c * 128:(c + 1) * 128], rhs=self.ident,
                            start=True, stop=True), reads=[rdst[tg]] + cr, writes=[rpt])
                    if ci < 4:
                        P.add("act", lambda e, pt=pt, tg=tg, ci=ci: e.copy(
                            out=xstok[:, tg * 4:(tg + 1) * 4, ci * 128:(ci + 1) * 128],
                            in_=pt[:].rearrange("p (c t) -> p c t", c=4)), reads=[rpt], writes=[rxstok[tg]])
                    else:
                        P.add("act", lambda e, pt=pt, tg=tg: e.copy(
                            out=Btok[:, tg * 4:(tg + 1) * 4, :],
                            in_=pt[:].rearrange("p (c t) -> p c t", c=4)), reads=[rpt], writes=[rBtok[tg]])

                NU, LAG = 24, 4
                for t in range(NU + LAG):
                    if t < NU:
                        stA(t)
                    if 0 <= t - 1 < NU:
                        stA2(t - 1)
                    if t - LAG >= 0:
                        stB(t - LAG)
                rBT, rCT = rBTl[0], rCTl[0]
            if self.dbg == 2:
                return
            with ExitStack() as s2:
                A = lambda name, shape, dt: s2.enter_context(SB(nc, name, shape, dt))
                yT = A("yT", [128, 4, S], BF16)
                ryT = [P.fresh("yT") for _ in range(NTG)]
                cbm = A("cbm", [128, 128], F32)
                Rb = A("Rb", [128, 1024], BF16)
                Eb = [A("Eb%d" % k, [128, 512], BF16) for k in range(2)]
                MT = [A("MT%d" % k, [128, 512], BF16) for k in range(2)]
                xdt = A("xdt", [128, 512], BF16)
                xdts = A("xdts", [128, 512], BF16)
                identD = A("identD", [128, 8, 128], BF16)
                ridentD = P.fresh("identD")
                for h in range(8):
                    P.add("dve", lambda e, h=h: e.tensor_scalar(
                        out=identD[:, h, :], in0=self.ident, scalar1=self.cbc[:, B_DSK + g * 8 + h:B_DSK + g * 8 + h + 1],
                        scalar2=None, op0=ALU.mult), reads=cr, writes=[ridentD])
                yb = A("yb", [128, 512], F32)
                sz = A("sz", [128, 512], F32)
                yg2 = [A("yg%d" % i, [128, 512], F32) for i in range(2)]
                yn2 = [A("yn%d" % i, [128, 512], BF16) for i in range(2)]
                ss2 = [A("ss%d" % i, [128, 4], F32) for i in range(2)]
                rRb2 = [P.fresh("Rb") for _ in range(2)]
                ryg2 = [P.fresh("yg") for _ in range(2)]
                ryn2 = [P.fresh("yn") for _ in range(2)]
                rss2 = [P.fresh("ss") for _ in range(2)]
                rcbm, rRb, rxdt, rxdts, rxsD, rt1, ryb, rsz, ryg, rjunk, ryn, rss = [P.fresh("s2") for _ in range(12)]
                rEb = [P.fresh("Eb") for _ in range(2)]
                rMT = [P.fresh("MT") for _ in range(2)]
                z0, rz0 = self.wget()
                z1, rz1 = self.wget()
                zv = [z0.rearrange("p (b c) -> p b c", b=NDC), z1.rearrange("p (b c) -> p b c", b=NDC)]
                rz = [rz0, rz1]
                b3 = lambda ap, n=8: ap.rearrange("p (h d) -> p h d", h=n)
                nhalf_c = P.fresh("nh")
                nh = A("nh", [128, 1], F32)
                P.add("dve", lambda e: e.memset(nh[:], -0.5), writes=[nhalf_c])
                pcb, rpcb = self.ps[2], self.rps[2]
                pyd, rpyd = self.ps[5], self.rps[5]
                pyo, rpyo = self.ps[6], self.rps[6]
                pst_, rpst_ = self.ps[0], self.rps[0]
                pz, rpz = self.ps[1], self.rps[1]
                ptr, rptr = self.ps[7], self.rps[7]

                def geo(c):
                    return slice(c * 128, (c + 1) * 128), c // 4, slice(c * 32 + g * 8, c * 32 + g * 8 + 8)

                def S0(c):
                    tsl, tg, hsl = geo(c)
                    P.add("pe", lambda e, tsl=tsl: e.matmul(pcb[:, 0:128], lhsT=BT[:, tsl], rhs=CT[:, tsl], start=True, stop=True),
                          reads=[rBT[tg], rCT[tg]], writes=[rpcb])
                    P.add("dve", lambda e: e.tensor_tensor(out=cbm[:], in0=pcb[:, 0:128], in1=self.tri_f, op=ALU.mult),
                          reads=[rpcb] + cr, writes=[rcbm])
                    for j in range(2):
                        pD, rpD = self.ps[3 + j], self.rps[3 + j]
                        P.add("pe", lambda e, pD=pD, j=j: e.matmul(pD[:], lhsT=self.U_b, rhs=Rb[:, j * 512:(j + 1) * 512], start=True, stop=True),
                              reads=[rRb2[j]] + cr, writes=[rpD])
                    for j in range(2):
                        pD, rpD = self.ps[3 + j], self.rps[3 + j]
                        P.add("act", lambda e, pD=pD, j=j: e.activation(out=Eb[j][:], in_=pD[:], func=AF.Exp),
                              reads=[rpD], writes=[rEb[j]])

                def SR(c):
                    tsl, tg, hsl = geo(c)
                    h0 = c * 32 + g * 8
                    P.add("dve", lambda e, h0=h0: e.tensor_tensor(
                        out=b3(Rb[:, 0:512], 4), in0=tab[:, 1, h0:h0 + 4].to_broadcast([128, 4, 128]),
                        in1=self.tri_f.unsqueeze(1).broadcast_to([128, 4, 128]), op=ALU.mult),
                        reads=[rtab[1]] + cr, writes=[rRb2[0]])
                    for hh in range(4, 8):
                        P.add("act", lambda e, h0=h0, hh=hh: e.activation(
                            out=Rb[:, hh * 128:(hh + 1) * 128], in_=self.tri_f, func=AF.Identity,
                            scale=tab[:, 1, h0 + hh:h0 + hh + 1]), reads=[rtab[1]] + cr, writes=[rRb2[1]])

                def S1(c):
                    tsl, tg, hsl = geo(c)
                    for j in range(2):
                        P.add("dve", lambda e, j=j: e.tensor_tensor(
                            out=b3(MT[j][:], 4), in0=b3(Eb[j][:], 4), in1=cbm[:].unsqueeze(1).broadcast_to([128, 4, 128]), op=ALU.mult),
                            reads=[rEb[j], rcbm], writes=[rMT[j]])
                    P.add("dve", lambda e, c=c, hsl=hsl: e.tensor_tensor(
                        out=b3(xdt[:]), in0=b3(xstok[:, c, :]), in1=tab[:, 0, hsl].to_broadcast([128, 8, 64]), op=ALU.mult),
                        reads=[rxstok[tg], rtab[0]], writes=[rxdt])
                    P.add("dve", lambda e, c=c, hsl=hsl: e.tensor_tensor(
                        out=b3(xdts[:]), in0=b3(xstok[:, c, :]), in1=tab[:, 3, hsl].to_broadcast([128, 8, 64]), op=ALU.mult),
                        reads=[rxstok[tg], rtab[3]], writes=[rxdts])

                def S2(c):
                    tsl, tg, hsl = geo(c)
                    for h in range(8):
                        P.add("pe", lambda e, h=h, c=c: e.matmul(
                            pyd[:, h * 64:(h + 1) * 64], lhsT=identD[:, h, :],
                            rhs=xstok[:, c, h * 64:(h + 1) * 64], start=True, stop=False),
                            reads=[ridentD, rxstok[tg]], writes=[rpyd])
                        P.add("pe", lambda e, h=h: e.matmul(
                            pyd[:, h * 64:(h + 1) * 64], lhsT=MT[h // 4][:, (h % 4) * 128:(h % 4 + 1) * 128],
                            rhs=xdt[:, h * 64:(h + 1) * 64], start=False, stop=True),
                            reads=[rMT[h // 4], rxdt], writes=[rpyd])
                    if c > 0:
                        P.add("pe", lambda e, tsl=tsl: e.matmul(pyo[:], lhsT=CT[:, tsl], rhs=Sbf[:], start=True, stop=True),
                              reads=[rCT[tg], rSbf], writes=[rpyo])
                    P.add("pe", lambda e, c=c: e.matmul(pst_[:], lhsT=Btok[:, c, :], rhs=xdts[:], start=True, stop=True),
                          reads=[rBtok[tg], rxdts], writes=[rpst_])
                    if c == 0:
                        P.add("act", lambda e: e.copy(out=Sst[:], in_=pst_[:]), reads=[rpst_], writes=[rS])
                    else:
                        P.add("dve", lambda e, hsl=hsl: e.tensor_tensor(
                            out=b3(Sst[:]), in0=b3(Sst[:]), in1=tab[:, 4, hsl].to_broadcast([128, 8, 64]), op=ALU.mult),
                            reads=[rS, rtab[4]], writes=[rS])
                        P.add("dve", lambda e: e.tensor_tensor(out=Sst[:], in0=Sst[:], in1=pst_[:], op=ALU.add),
                              reads=[rS, rpst_], writes=[rS])
                    if c < 15:
                        P.add("act", lambda e: e.copy(out=Sbf[:], in_=Sst[:]), reads=[rS], writes=[rSbf])
                    for hf in range(2):
                        for dc in range(NDC):
                            P.add("pe", lambda e, hf=hf, dc=dc, tsl=tsl: e.matmul(
                                pz[:, hf * 256:(hf + 1) * 256], lhsT=hT[:, dc, tsl], rhs=zv[hf][:, dc, :],
                                start=(dc == 0), stop=(dc == NDC - 1)), reads=[rz[hf], rh[dc][tg]], writes=[rpz])
                    P.add("act", lambda e: e.activation(out=sz[:], in_=pz[:], func=AF.Silu), reads=[rpz], writes=[rsz])

                def S3(c):
                    tsl, tg, hsl = geo(c)
                    yg, ryg, ss, rss = yg2[c % 2], ryg2[c % 2], ss2[c % 2], rss2[c % 2]
                    if c > 0:
                        P.add("dve", lambda e, hsl=hsl: e.tensor_tensor(
                            out=b3(yb[:]), in0=b3(pyo[:]), in1=tab[:, 2, hsl].to_broadcast([128, 8, 64]), op=ALU.mult),
                            reads=[rpyo, rtab[2]], writes=[ryb])
                        P.add("dve", lambda e: e.tensor_tensor(out=yb[:], in0=yb[:], in1=pyd[:], op=ALU.add),
                              reads=[ryb, rpyd], writes=[ryb])
                        P.add("dve", lambda e, yg=yg: e.tensor_tensor(out=yg[:], in0=yb[:], in1=sz[:], op=ALU.mult),
                              reads=[ryb, rsz], writes=[ryg])
                    else:
                        P.add("dve", lambda e, yg=yg: e.tensor_tensor(out=yg[:], in0=sz[:], in1=pyd[:], op=ALU.mult),
                              reads=[rpyd, rsz], writes=[ryg])
                    P.add("act", lambda e, yg=yg, ss=ss: e.activation(out=yb[:], in_=yg[:], func=AF.Square, accum_out=ss[:, 0:1]),
                          reads=[ryg], writes=[ryb, rss])
                    P.add("pool", lambda e, ss=ss: e.tensor_scalar(out=ss[:, 1:2], in0=ss[:, 0:1], scalar1=1.0 / 512, scalar2=EPS,
                                                            op0=ALU.mult, op1=ALU.add), reads=[rss], writes=[rss])
                    P.add("pool", lambda e, ss=ss: e.tensor_tensor(out=ss[:, 2:3], in0=ss[:, 1:2], in1=nh[:], op=ALU.pow),
                          reads=[rss, nhalf_c], writes=[rss])

                def S3b(c):
                    yg, ryg, ss, rss = yg2[c % 2], ryg2[c % 2], ss2[c % 2], rss2[c % 2]
                    yn, ryn = yn2[c % 2], ryn2[c % 2]
                    P.add("dve", lambda e, yg=yg, ss=ss, yn=yn: e.scalar_tensor_tensor(
                        out=yn[:], in0=yg[:], scalar=ss[:, 2:3], in1=snw[:], op0=ALU.mult, op1=ALU.mult),
                        reads=[ryg, rss, rsnw], writes=[ryn])

                def S4(c):
                    tsl, tg, hsl = geo(c)
                    yn, ryn = yn2[c % 2], ryn2[c % 2]
                    for q in range(4):
                        P.add("pe", lambda e, q=q, yn=yn: e.matmul(
                            ptr[:, q * 128:(q + 1) * 128], lhsT=yn[:, q * 128:(q + 1) * 128], rhs=self.ident,
                            start=True, stop=True), reads=[ryn] + cr, writes=[rptr])
                    P.add("act", lambda e, tsl=tsl: e.copy(out=yT[:, :, tsl], in_=ptr[:].rearrange("p (c t) -> p c t", c=4)),
                          reads=[rptr], writes=[ryT[tg]])

                order = [(SR, 0), (S3, 3), (S3b, 4), (S4, 5), (S2, 2), (S1, 1), (S0, 0)]
                for t in range(16 + 5):
                    for fn, lag in order:
                        c = t - lag
                        if 0 <= c < 16:
                            fn(c)
                cnt = 0
                oblk = []
                for hf in range(2):
                    slot, rslot = self.wget(keep=hf)
                    oblk.append((slot.rearrange("p (k c) -> p k c", k=4), rslot))
                for tg in range(NTG):
                    sl = slice(tg * TGW, (tg + 1) * TGW)
                    for dc in range(NDC):
                        ov, rslot = oblk[dc // 4]
                        d4 = dc % 4
                        pd, rpd = self.ps[3 + cnt % 2], self.rps[3 + cnt % 2]
                        cnt += 1
                        for kq in range(4):
                            P.add("pe", lambda e, pd=pd, ov=ov, kq=kq, d4=d4, sl=sl: e.matmul(
                                pd[:], lhsT=ov[:, kq, d4 * 128:(d4 + 1) * 128], rhs=yT[:, kq, sl],
                                start=(kq == 0), stop=(kq == 3)), reads=[rslot, ryT[tg]], writes=[rpd])
                        P.add("dve", lambda e, pd=pd, dc=dc, sl=sl: e.tensor_tensor(
                            out=self.xT[:, dc, sl], in0=pd[:], in1=self.xT[:, dc, sl], op=ALU.add),
                            reads=[rpd, self.rx[dc][tg]], writes=[self.rx[dc][tg]])

    def build(self, final_norm=True):
        P, nc = self.P, self.nc
        self.kc = self.st.enter_context(SB(nc, "kc", [128, 2], F32))
        self.epsc = self.kc
        reps = Res("eps")
        P.add("dve", lambda e: e.memset(self.kc[:, 0:1], EPS), writes=[reps])
        P.add("dve", lambda e: e.memset(self.kc[:, 1:2], 1.0), writes=[reps])
        self.rcs.append(reps)
        for p in self.phases:
            if p == "ffn0a":
                self.emit_ffn(0)
            elif p == "ffn0b":
                self.emit_ffn(2)
            elif p == "ffn1a":
                self.emit_ffn(3)
            elif p == "ffn1b":
                self.emit_ffn(5)
            elif p == "mamba":
                self.emit_mamba()
            elif p == "kv":
                self.emit_kv()
            elif p == "attn":
                self.emit_attn()
        self.emit_final(do_norm=("final" in self.phases))
        assert getattr(self, "dbg", 0) or self.wi == n_blocks(self.phases), (self.wi, n_blocks(self.phases))
        P.emit(self.st)
        self.st.close()
        return nc


def _pad_block(a):
    out = np.zeros((128, 2048), np.float32)
    a = np.ascontiguousarray(a).reshape(128, -1)
    out[:, :a.shape[1]] = a
    return out


def _colchunks(W, starts):
    outs = []
    for s in starts:
        idx = np.arange(s, s + 128) if np.isscalar(s) else s
        outs.append(W[:, idx].reshape(8, 128, 128).transpose(1, 0, 2))
    return np.stack(outs, axis=1)


def _rowblock(W, row0, nrow_chunks, col0, ncols):
    return W[row0 * 128:(row0 + nrow_chunks) * 128, col0:col0 + ncols].reshape(nrow_chunks, 128, ncols).transpose(1, 0, 2)


PERM64 = np.concatenate([np.arange(32, 64), np.arange(0, 32)])
PERM128 = np.concatenate([PERM64, 64 + PERM64])


def pack_ffn(Wg, Wu, Wd):
    blocks = []
    f0 = 0
    for npass in PASSES:
        for jj in range(npass):
            j = f0 + jj
            g = _colchunks(Wg, [j * 128])[:, 0]
            u = _colchunks(Wu, [j * 128])[:, 0]
            blocks.append(_pad_block(np.stack([g, u], axis=1)))
        for b in range(4):
            blocks.append(_pad_block(_rowblock(Wd, f0, npass, b * 256, 256)))
        f0 += npass
    return blocks


def pack_weights(inp, phases):
    blocks = []
    for p in phases:
        if p.startswith("ffn"):
            l, i = {"ffn0a": (0, 0), "ffn0b": (0, 1), "ffn1a": (1, 0), "ffn1b": (1, 1)}[p]
            blocks += pack_ffn(inp["ffn_w_gate"][l, i], inp["ffn_w_up"][l, i], inp["ffn_w_down"][l, i])
        elif p == "mamba":
            Win = inp["ssm_w_in"][0]
            Wout = inp["ssm_w_out"][0]
            blocks.append(_pad_block(Win[:, 5120:5152].reshape(8, 128, 32).transpose(1, 0, 2)))
            for g in range(4):
                xs0 = 2048 + g * 512
                blocks.append(_pad_block(_colchunks(Win, [xs0, xs0 + 128])))
                blocks.append(_pad_block(_colchunks(Win, [xs0 + 256, xs0 + 384])))
                blocks.append(_pad_block(_colchunks(Win, [2048 + 2048 + g * 128, 2048 + 2560 + g * 128])))
                for hf in range(2):
                    z0 = g * 512 + hf * 256
                    blocks.append(_pad_block(Win[:, z0:z0 + 256].reshape(8, 128, 256).transpose(1, 0, 2)))
                for hf in range(2):
                    blocks.append(_pad_block(_rowblock(Wout, g * 4, 4, hf * 512, 512)))
        elif p == "kv":
            Wk, Wv = inp["w_k"], inp["w_v"]
            for kh in range(4):
                pl = kh * 64 + np.concatenate([np.arange(64), np.arange(64)])
                sw = kh * 64 + np.concatenate([PERM64, PERM64])
                blocks.append(_pad_block(_colchunks(Wk, [pl, sw])))
            blocks.append(_pad_block(Wv.reshape(8, 128, 256).transpose(1, 0, 2)))
        elif p == "attn":
            Wq, Wo = inp["attn_w_q"][0], inp["attn_w_o"][0]
            for c in range(8):
                blocks.append(_pad_block(_colchunks(Wq, [c * 128 + np.arange(128), c * 128 + PERM128])))
            for b in range(4):
                blocks.append(_pad_block(_rowblock(Wo, 0, 8, b * 256, 256)))
    if not blocks:
        blocks.append(np.zeros((128, 2048), np.float32))
    return np.stack(blocks, axis=0)


def pack_consts(inp):
    cc = np.zeros((128, NCC), np.float32)
    vecs = [inp["norm_w"][l, i] for l in range(2) for i in range(3)] + [inp["kv_norm_w"], inp["final_norm_w"]]
    for v, w in enumerate(vecs):
        cc[:, C_NW + v * 8:C_NW + v * 8 + 8] = w.reshape(8, 128).T
    cw = inp["ssm_conv_w"][0]
    cc[:, C_CW:C_CW + 96] = cw.reshape(4, 24, 128).transpose(2, 1, 0).reshape(128, 96)
    cc[:, C_CB:C_CB + 24] = inp["ssm_conv_b"][0].reshape(24, 128).T
    bq = inp["attn_b_q"][0]
    for c in range(8):
        cc[:, C_BQ + c * 2] = bq[c * 128 + np.arange(128)]
        cc[:, C_BQ + c * 2 + 1] = bq[c * 128 + PERM128]
    bk = inp["b_k"]
    for kh in range(4):
        cc[:, C_BK + kh * 2] = bk[kh * 64 + np.concatenate([np.arange(64), np.arange(64)])]
        cc[:, C_BK + kh * 2 + 1] = bk[kh * 64 + np.concatenate([PERM64, PERM64])]
    cc[:, C_BO:C_BO + 8] = inp["attn_b_o"][0].reshape(8, 128).T
    cb = np.zeros((128, NCB), np.float32)
    cb[:, B_DTB:B_DTB + 32] = inp["ssm_dt_bias"][0][None, :]
    cb[:, B_ALOG:B_ALOG + 32] = inp["ssm_a_log"][0][None, :]
    cb[:, B_DSK:B_DSK + 32] = inp["ssm_d"][0][None, :]
    cb[:, B_BV:B_BV + 256] = inp["b_v"][None, :]
    cb[:, B_SINK:B_SINK + 16] = inp["attn_sinks"][0][None, :]
    cb[:, B_SNW:B_SNW + 2048] = inp["ssm_norm_w"][0][None, :]
    return cc, cb


def const_tables():
    i = np.arange(128)
    ident = np.eye(128, dtype=np.float32)
    tri = (i[:, None] <= i[None, :]).astype(np.float32)
    U = (i[:, None] > i[None, :]).astype(np.float32)
    ones = np.ones((128, 128), np.float32)
    cmat = np.concatenate([ident, tri, U, ones], axis=1)
    q = i[:, None]
    j = np.arange(256)[None, :]
    valid = (j <= q + 128) & (j > q)
    maskb = np.where(valid, 0.0, -30000.0).astype(np.float32)
    pos = np.arange(S, dtype=np.float32)
    inv = (1.0 / (np.float32(10000.0) ** (np.arange(0, 64, 2, dtype=np.float32) / np.float32(64)))).astype(np.float32)
    ang = (pos[:, None] * inv[None, :]).astype(np.float32)
    cos, sin = np.cos(ang).astype(np.float32), np.sin(ang).astype(np.float32)
    fi = (i % 64) % 32
    sign = np.where((i % 64) < 32, -1.0, 1.0).astype(np.float32)
    rope = np.stack([cos[:, fi].T, sin[:, fi].T * sign[:, None]], axis=1).astype(np.float32)
    return cmat, maskb, np.ascontiguousarray(rope)


_CACHE = {}


def run(inputs, phases, cores):
    key = tuple(phases)
    kb = KB(list(phases))
    nc = kb.build()
    wblk = pack_weights(inputs, phases)
    cc, cb = pack_consts(inputs)
    cmat, maskb, rope = const_tables()
    x = np.asarray(inputs["x"], np.float32)
    in_maps = []
    for b in cores:
        in_maps.append({"xT": np.ascontiguousarray(x[b].T), "wblk": wblk, "ccol": cc, "cbc": cb, "cmat": cmat,
                        "maskb": maskb, "rope": rope})
    res = run_bass_kernel_spmd(nc, in_maps, core_ids=list(range(len(cores))))
    outs = [np.ascontiguousarray(np.asarray(r["outT"]).T) for r in res.results]
    return np.stack(outs, axis=0)


def kernel(**inputs):
    inputs = {k: np.asarray(v) for k, v in inputs.items()}
    return run(inputs, PHASES, list(range(8))).astype(np.float32)
```
